# Optimizing a Trainium2 kernel written in Bass

```python
import math
import jax
import jax.numpy as jnp
from jax import lax
import numpy as np

D_MODEL = 1024
BATCH = 2
SEQ = 8192
DEPTH = 2

CTX_LEN = 256
GRID_W = 64
D_MIX = D_MODEL
MLA_HEADS = 4
MLA_W = D_MIX // 2
MLA_DV = MLA_W // MLA_HEADS
MLA_NOPE = 128
MLA_ROPE = 64
Q_LORA = 384
KV_LORA = 256
DIFF_HEADS = 4
DIFF_W = D_MIX // 4
DIFF_DV = DIFF_W // DIFF_HEADS
DIFF_DQK = DIFF_DV // 2
CHUNK_W = D_MIX - MLA_W - DIFF_W
CHUNK_GROUPS = 4
CHUNK_GW = CHUNK_W // CHUNK_GROUPS
CHUNK = 128
D_FF = 2816
CONV_W = 3
Q_BLOCK = 128
ROPE_BASE = 10000.0
EPS = 1e-6
DN_ALPHA = (2 * DEPTH) ** 0.25
DN_BETA = (8 * DEPTH) ** -0.25
MLA_SCALE = (MLA_NOPE + MLA_ROPE) ** -0.5
DIFF_SCALE = DIFF_DQK ** -0.5
IN_SPLITS = [Q_LORA,
             Q_LORA + KV_LORA,
             Q_LORA + KV_LORA + MLA_ROPE,
             Q_LORA + KV_LORA + MLA_ROPE + DIFF_W,
             Q_LORA + KV_LORA + MLA_ROPE + 2 * DIFF_W,
             Q_LORA + KV_LORA + MLA_ROPE + 3 * DIFF_W]
IN_W = Q_LORA + KV_LORA + MLA_ROPE + 3 * DIFF_W + 2 * CHUNK_W

kernel_name = "hymba_mla_diff_chunkmlp_convffn_deepnorm_dit"

PARAM_NAMES = ["ada_w", "ada_b", "w_in", "mla_gq", "mla_wuq", "mla_gkv", "mla_wukv",
               "diff_lq1", "diff_lk1", "diff_lq2", "diff_lk2", "diff_subln_g",
               "sgu_ln_g", "sgu_ln_b", "sgu_ws", "sgu_bs", "w_o", "ln1_g", "ln1_b",
               "ffn_wup", "ffn_convw", "ffn_convb", "ffn_wdown", "ln2_g", "ln2_b"]


def layer_norm(x, g, b):
    xf = x.astype(jnp.float32)
    mu = jnp.mean(xf, -1, keepdims=True)
    var = jnp.mean(jnp.square(xf - mu), -1, keepdims=True)
    return ((xf - mu) * lax.rsqrt(var + EPS)).astype(x.dtype) * g + b


def rms_norm(x, g):
    xf = x.astype(jnp.float32)
    return (xf * lax.rsqrt(jnp.mean(xf * xf, -1, keepdims=True) + EPS)).astype(x.dtype) * g


def rope_1d(x, pos):
    n = x.shape[-1] // 2
    inv = ROPE_BASE ** (-jnp.arange(n, dtype=jnp.float32) / n)
    ang = pos.astype(jnp.float32)[:, None] * inv[None, :]
    cos = jnp.cos(ang)[None, :, None, :].astype(x.dtype)
    sin = jnp.sin(ang)[None, :, None, :].astype(x.dtype)
    x1, x2 = x[..., :n], x[..., n:]
    return jnp.concatenate([x1 * cos - x2 * sin, x2 * cos + x1 * sin], -1)


def axial_rope(x, rows, cols):
    half = x.shape[-1] // 2
    return jnp.concatenate([rope_1d(x[..., :half], rows), rope_1d(x[..., half:], cols)], -1)


def attention(q, k, v, scale):
    B, Sq, H, Dk = q.shape
    Dv = v.shape[-1]
    nblk = Sq // Q_BLOCK
    qb = jnp.moveaxis(q.reshape(B, nblk, Q_BLOCK, H, Dk), 1, 0)

    def one_block(qblk):
        s = jnp.einsum('bqhd,bkhd->bhqk', qblk, k).astype(jnp.float32) * scale
        p = jax.nn.softmax(s, axis=-1).astype(v.dtype)
        return jnp.einsum('bhqk,bkhd->bqhd', p, v)

    o = lax.map(one_block, qb)
    return jnp.moveaxis(o, 0, 1).reshape(B, Sq, H, Dv)


def project_stream(z, p, rows, cols):
    B, S, _ = z.shape
    if rows is None:
        rot = lambda t: t
    else:
        rot = lambda t: axial_rope(t, rows, cols)
    q_lat, kv_lat, k_r, dq, dk, dv, zch = jnp.split(z, IN_SPLITS, axis=-1)
    qm = (rms_norm(q_lat, p["mla_gq"]) @ p["mla_wuq"]).reshape(B, S, MLA_HEADS, MLA_NOPE + MLA_ROPE)
    qm = jnp.concatenate([qm[..., :MLA_NOPE], rot(qm[..., MLA_NOPE:])], -1)
    kvm = (rms_norm(kv_lat, p["mla_gkv"]) @ p["mla_wukv"]).reshape(B, S, MLA_HEADS, MLA_NOPE + MLA_DV)
    kr = rot(k_r[:, :, None, :])
    km = jnp.concatenate([kvm[..., :MLA_NOPE], jnp.broadcast_to(kr, (B, S, MLA_HEADS, MLA_ROPE))], -1)
    vm = kvm[..., MLA_NOPE:]

    def two_maps(t):
        t = t.reshape(B, S, DIFF_HEADS, 2, DIFF_DQK)
        return rot(jnp.swapaxes(t, 2, 3).reshape(B, S, 2 * DIFF_HEADS, DIFF_DQK))

    qd = two_maps(dq)
    kd = two_maps(dk)
    vd = jnp.tile(dv.reshape(B, S, DIFF_HEADS, DIFF_DV), (1, 1, 2, 1))
    return qm, km, vm, qd, kd, vd, zch


def chunk_mix(z, p):
    z = jax.nn.gelu(z)
    u, v = jnp.split(z, 2, axis=-1)
    v = layer_norm(v, p["sgu_ln_g"], p["sgu_ln_b"])
    B, S, _ = v.shape
    vg = v.reshape(B, S // CHUNK, CHUNK, CHUNK_GROUPS, CHUNK_GW)
    mixed = jnp.einsum('gpq,bnqgc->bnpgc', p["sgu_ws"], vg) + p["sgu_bs"].T[None, None, :, :, None]
    return u * mixed.reshape(B, S, CHUNK_W)


def mixer(h, hc, rows, cols, p, layer_idx, need_ctx):
    qm, km, vm, qd, kd, vd, zch = project_stream(h @ p["w_in"], p, rows, cols)
    qmc, kmc, vmc, qdc, kdc, vdc, zchc = project_stream(hc @ p["w_in"], p, None, None)
    lam_init = 0.8 - 0.6 * math.exp(-0.3 * layer_idx)
    lam = (jnp.exp(jnp.sum(p["diff_lq1"] * p["diff_lk1"]))
           - jnp.exp(jnp.sum(p["diff_lq2"] * p["diff_lk2"])) + lam_init)

    def merge(o_m, o_d, z_c):
        Bn, Sn = o_m.shape[:2]
        d = o_d[:, :, :DIFF_HEADS] - lam * o_d[:, :, DIFF_HEADS:]
        d = rms_norm(d, p["diff_subln_g"]) * (1.0 - lam_init)
        y = jnp.concatenate([o_m.reshape(Bn, Sn, MLA_W), d.reshape(Bn, Sn, DIFF_W), chunk_mix(z_c, p)], -1)
        return y @ p["w_o"]

    o_m = attention(qm, jnp.concatenate([kmc, km], 1), jnp.concatenate([vmc, vm], 1), MLA_SCALE)
    o_d = attention(qd, jnp.concatenate([kdc, kd], 1), jnp.concatenate([vdc, vd], 1), DIFF_SCALE)
    y = merge(o_m, o_d, zch)
    yc = None
    if need_ctx:
        yc = merge(attention(qmc, kmc, vmc, MLA_SCALE), attention(qdc, kdc, vdc, DIFF_SCALE), zchc)
    return y, yc


def dwconv3(h, w, b):
    hp = jnp.pad(h, ((0, 0), (1, 1), (0, 0)))
    return hp[:, :-2] * w[0] + hp[:, 1:-1] * w[1] + hp[:, 2:] * w[2] + b


def conv_ffn(h, p):
    u = dwconv3(h @ p["ffn_wup"], p["ffn_convw"], p["ffn_convb"])
    gate, val = jnp.split(u, 2, axis=-1)
    return (jax.nn.silu(gate) * val) @ p["ffn_wdown"]


def layer(x, xc, mod, modc, rows, cols, p, layer_idx, need_ctx):
    sh1, sc1, g1, sh2, sc2, g2 = jnp.split(mod, 6, axis=-1)
    csh1, csc1, cg1, csh2, csc2, cg2 = jnp.split(modc, 6, axis=-1)
    y, yc = mixer(x * (1.0 + sc1) + sh1, xc * (1.0 + csc1) + csh1, rows, cols, p, layer_idx, need_ctx)
    x = layer_norm(DN_ALPHA * x + g1 * y, p["ln1_g"], p["ln1_b"])
    x = layer_norm(DN_ALPHA * x + g2 * conv_ffn(x * (1.0 + sc2) + sh2, p), p["ln2_g"], p["ln2_b"])
    if need_ctx:
        xc = layer_norm(DN_ALPHA * xc + cg1 * yc, p["ln1_g"], p["ln1_b"])
        xc = layer_norm(DN_ALPHA * xc + cg2 * conv_ffn(xc * (1.0 + csc2) + csh2, p), p["ln2_g"], p["ln2_b"])
    return x, xc


def setup_inputs(seed: int = 0) -> dict:
    key = jax.random.key(seed)
    ks = iter(jax.random.split(key, 40))
    f32 = jnp.float32
    nrm = lambda shape, s: jax.random.normal(next(ks), shape, f32) * s
    gain = lambda shape: 1.0 + nrm(shape, 0.02)
    L = DEPTH
    return {
        "x": nrm((BATCH, SEQ, D_MODEL), 1.0),
        "c": nrm((BATCH, D_MODEL), 1.0),
        "ctx": nrm((BATCH, CTX_LEN, D_MODEL), 1.0),
        "c_ctx": nrm((D_MODEL,), 1.0),
        "ada_w": nrm((L, D_MODEL, 6 * D_MODEL), 0.5 * D_MODEL ** -0.5),
        "ada_b": nrm((L, 6 * D_MODEL), 0.02),
        "w_in": nrm((L, D_MODEL, IN_W), D_MODEL ** -0.5),
        "mla_gq": gain((L, Q_LORA)),
        "mla_wuq": nrm((L, Q_LORA, MLA_HEADS * (MLA_NOPE + MLA_ROPE)), Q_LORA ** -0.5),
        "mla_gkv": gain((L, KV_LORA)),
        "mla_wukv": nrm((L, KV_LORA, MLA_HEADS * (MLA_NOPE + MLA_DV)), KV_LORA ** -0.5),
        "diff_lq1": nrm((L, DIFF_DQK), 0.1),
        "diff_lk1": nrm((L, DIFF_DQK), 0.1),
        "diff_lq2": nrm((L, DIFF_DQK), 0.1),
        "diff_lk2": nrm((L, DIFF_DQK), 0.1),
        "diff_subln_g": gain((L, DIFF_DV)),
        "sgu_ln_g": gain((L, CHUNK_W)),
        "sgu_ln_b": nrm((L, CHUNK_W), 0.02),
        "sgu_ws": nrm((L, CHUNK_GROUPS, CHUNK, CHUNK), CHUNK ** -0.5),
        "sgu_bs": gain((L, CHUNK_GROUPS, CHUNK)),
        "w_o": nrm((L, D_MIX, D_MODEL), DN_BETA * D_MIX ** -0.5),
        "ln1_g": gain((L, D_MODEL)),
        "ln1_b": nrm((L, D_MODEL), 0.02),
        "ffn_wup": nrm((L, D_MODEL, 2 * D_FF), D_MODEL ** -0.5),
        "ffn_convw": nrm((L, CONV_W, 2 * D_FF), CONV_W ** -0.5),
        "ffn_convb": nrm((L, 2 * D_FF), 0.01),
        "ffn_wdown": nrm((L, D_FF, D_MODEL), DN_BETA * D_FF ** -0.5),
        "ln2_g": gain((L, D_MODEL)),
        "ln2_b": nrm((L, D_MODEL), 0.02),
    }


def reference(x, c, ctx, c_ctx, ada_w, ada_b, w_in, mla_gq, mla_wuq, mla_gkv, mla_wukv,
              diff_lq1, diff_lk1, diff_lq2, diff_lk2, diff_subln_g, sgu_ln_g, sgu_ln_b, sgu_ws, sgu_bs,
              w_o, ln1_g, ln1_b, ffn_wup, ffn_convw, ffn_convb, ffn_wdown, ln2_g, ln2_b):
    S = x.shape[1]
    ROWS = S // GRID_W
    rows = jnp.repeat(jnp.arange(ROWS, dtype=jnp.int32), GRID_W)
    cols = jnp.tile(jnp.arange(GRID_W, dtype=jnp.int32), ROWS)
    stacked = [ada_w, ada_b, w_in, mla_gq, mla_wuq, mla_gkv, mla_wukv,
               diff_lq1, diff_lk1, diff_lq2, diff_lk2, diff_subln_g,
               sgu_ln_g, sgu_ln_b, sgu_ws, sgu_bs, w_o, ln1_g, ln1_b,
               ffn_wup, ffn_convw, ffn_convb, ffn_wdown, ln2_g, ln2_b]
    s_c = jax.nn.silu(c)
    s_cc = jax.nn.silu(c_ctx)
    xc = ctx
    for l in range(DEPTH):
        p = {name: arr[l] for name, arr in zip(PARAM_NAMES, stacked)}
        mod = (s_c @ p["ada_w"] + p["ada_b"])[:, None, :]
        modc = (s_cc @ p["ada_w"] + p["ada_b"])[None, None, :]
        x, xc = layer(x, xc, mod, modc, rows, cols, p, l, l < DEPTH - 1)
    return x
```

```python
import contextlib
import math
import numpy as np
import concourse.bass as bass
import concourse.mybir as mybir
from concourse.bass_utils import run_bass_kernel_spmd

F32 = mybir.dt.float32
BF16 = mybir.dt.bfloat16
AF = mybir.ActivationFunctionType
ALU = mybir.AluOpType
AX = mybir.AxisListType

D = 1024
DEPTH = 2
SEQ = 8192
CTX = 256
OWN = 2048
T = CTX + OWN
TH = T + 2
NKEY = CTX + SEQ
NKT = NKEY // 128
INW = 1984
DFF = 2816
NFC = 22
EPS = 1e-6
ALPHA = (2 * DEPTH) ** 0.25
MLA_SCALE = 192 ** -0.5
DIFF_SCALE = 32 ** -0.5
KVROWS = 1600
CH = 256
R_KN, R_VM, R_DK, R_KR, R_VD = 0, 512, 1024, 1280, 1344
BLKS = [(0, 256), (256, 512), (768, 512), (1280, 512), (1792, 512)]


class _Op:
    __slots__ = ("eng", "fn", "deps", "signal", "sigval", "dma_sem", "ndma", "inc1")

    def __init__(self, eng, fn, deps, dma_sem=None, ndma=1):
        self.eng = eng
        self.fn = fn
        self.deps = deps
        self.signal = dma_sem is not None
        self.sigval = None
        self.dma_sem = dma_sem
        self.ndma = ndma
        self.inc1 = False


class _Nop:
    def then_inc(self, *a, **k):
        return self


class Sched:
    ENGS = ("pe", "act", "dve", "pool", "sp")
    SAME_ENG_SYNC = ("act", "dve", "pool")

    def __init__(self, nc, stack):
        self.nc = nc
        self.stack = stack
        self.ops = {e: [] for e in self.ENGS}
        self.lastw = {}
        self.readers = {}
        self.dma_sems = {}
        self.dma_last = {}
        self.bar = []
        self.esem = {e: stack.enter_context(nc.semaphore(f"S_{e}")) for e in self.ENGS}

    def _deps(self, reads, writes):
        deps = [(b, "raw") for b in self.bar]
        for k in reads:
            w = self.lastw.get(k)
            if w is not None:
                deps.append((w, "raw"))
        for k in writes:
            w = self.lastw.get(k)
            if w is not None:
                deps.append((w, "waw"))
            last = {}
            for r in self.readers.get(k, ()):
                last[r.dma_sem if r.dma_sem is not None else ("e", r.eng)] = r
            for r in last.values():
                deps.append((r, "war"))
        return deps

    def _commit(self, op, reads, writes):
        for k in reads:
            self.readers.setdefault(k, []).append(op)
        for k in writes:
            self.lastw[k] = op
            self.readers[k] = []

    def op(self, eng, fn, reads=(), writes=()):
        o = _Op(eng, fn, self._deps(reads, writes))
        self.ops[eng].append(o)
        self._commit(o, reads, writes)
        return o

    def dma(self, queue, semname, fns, reads=(), writes=(), inc1=False):
        if not isinstance(fns, (list, tuple)):
            fns = [fns]
        if semname not in self.dma_sems:
            self.dma_sems[semname] = [self.stack.enter_context(self.nc.semaphore(f"D_{semname}")), 0]
        deps = self._deps(reads, writes)
        prev = self.dma_last.get(semname)
        if prev is not None:
            deps.append((prev, "waw"))
        o = _Op(queue, fns, deps, dma_sem=semname, ndma=len(fns))
        o.inc1 = inc1
        self.ops[queue].append(o)
        self.dma_last[semname] = o
        self._commit(o, reads, writes)
        return o

    def barrier(self):
        b = []
        for e in self.ENGS:
            if self.ops[e]:
                b.append(self.ops[e][-1])
        b.extend(self.dma_last.values())
        self.bar = b

    def final_wait(self, eng_name):
        o = _Op(eng_name, lambda eng: _Nop(), [(d, "raw") for d in self.dma_last.values()])
        self.ops[eng_name].append(o)

    def _skip(self, o, d, kind):
        if d.dma_sem is not None:
            return False
        if d.eng == o.eng and o.dma_sem is None:
            if kind == "war" or d.eng not in self.SAME_ENG_SYNC:
                return True
        return False

    def emit(self):
        nc = self.nc
        for e in self.ENGS:
            for o in self.ops[e]:
                for (d, kind) in o.deps:
                    if d.dma_sem is None and not self._skip(o, d, kind):
                        d.signal = True
        for e in self.ENGS:
            c = 0
            for o in self.ops[e]:
                if o.dma_sem is not None:
                    s = self.dma_sems[o.dma_sem]
                    s[1] += (1 if o.inc1 else 16) * o.ndma
                    o.sigval = s[1]
                    assert s[1] < 60000, ("dma sem overflow", o.dma_sem)
                elif o.signal:
                    c += 1
                    o.sigval = c
            assert c < 60000, ("engine sem overflow", e, c)

        def run(eng_name, eng):
            waited = {}
            for o in self.ops[eng_name]:
                need = {}
                for (d, kind) in o.deps:
                    if self._skip(o, d, kind):
                        continue
                    if d.dma_sem is not None:
                        key = ("d", d.dma_sem)
                        sem = self.dma_sems[d.dma_sem][0]
                    else:
                        key = ("e", d.eng)
                        sem = self.esem[d.eng]
                    v = d.sigval
                    if waited.get(key, 0) >= v:
                        continue
                    if key not in need or need[key][1] < v:
                        need[key] = (sem, v)
                for key, (sem, v) in need.items():
                    eng.wait_ge(sem, v)
                    waited[key] = v
                if o.dma_sem is not None:
                    sem = self.dma_sems[o.dma_sem][0]
                    for f in o.fn:
                        f(eng).then_inc(sem, 1 if o.inc1 else 16)
                else:
                    ins = o.fn(eng)
                    if o.signal:
                        ins.then_inc(self.esem[eng_name], 1)

        with nc.Block() as block:
            @block.tensor
            def _(e):
                run("pe", e)

            @block.scalar
            def _(e):
                run("act", e)

            @block.vector
            def _(e):
                run("dve", e)

            @block.gpsimd
            def _(e):
                run("pool", e)

            @block.sync
            def _(e):
                run("sp", e)


import os
NOCC = bool(int(os.environ.get("K_NOCC", "0")))
PARAMS = ["ada_w", "adab", "w_in", "wuq", "wukv", "w_o", "wup", "wdown"]


def build_program(n_layers=DEPTH, debug=False):
    nc = bass.Bass("TRN2", target_bir_lowering=False)
    dram = {}

    def din(name, shape, dt=F32):
        dram[name] = nc.dram_tensor(name, list(shape), dt, kind="ExternalInput").ap()
        return dram[name]

    def dint(name, shape, dt=BF16, kind="Internal"):
        dram[name] = nc.dram_tensor(name, list(shape), dt, kind=kind).ap()
        return dram[name]

    xT_d = din("xT", [D, T])
    cc_d = din("cc", [128, 8, 2])
    tabs_d = din("tabs", [128, 4, T])
    rmat_d = din("rmat", [128, 3, 128])
    hmask_d = din("hmask", [128, 2, 4])
    gmask_d = din("gmask", [128, 4])
    ada_w_d = din("ada_w", [DEPTH, D, 6 * D])
    colsin_d = din("colsin", [128, DEPTH, NCOLIN])
    w_in_d = din("w_in", [DEPTH, D, INW])
    wuq_d = din("wuq", [DEPTH, 384, 768])
    wukv_d = din("wukv", [DEPTH, 256, 1024])
    w_o_d = din("w_o", [DEPTH, D, D])
    wup_d = din("wup", [DEPTH, D, 2 * DFF])
    wdown_d = din("wdown", [DEPTH, DFF, D])
    sgu_ws_d = din("sgu_wsT", [DEPTH, 128, 4, 128])
    sgu_bsT_d = din("sgu_bsT", [DEPTH, 128, 2, 128])
    sgu_gb_d = din("sgu_gb", [DEPTH, 128, 2, 256])
    lamv_d = din("lamv", [DEPTH, 128, 4, 32])
    okind = "ExternalOutput"
    out_d = dint("outT", [D, OWN], F32, kind=okind)

    dbg = {}
    ikind = "Internal"
    xsp_d = [dint(f"xsp{l}", [128, 8, T], F32) for l in range(n_layers)]
    kvsrc_d = [dint(f"kvsrc{l}", [KVROWS, OWN], BF16) for l in range(n_layers)]
    kvall_d = [dint(f"kvall{l}", [4 * KVROWS, OWN], BF16) for l in range(n_layers)]
    kvctx_d = [dint(f"kvctx{l}", [KVROWS, CTX], BF16, kind=ikind) for l in range(n_layers)]
    qn_d = [dint(f"qn{l}", [512, T], BF16, kind=ikind) for l in range(n_layers)]
    qr_d = [dint(f"qr{l}", [256, T], BF16, kind=ikind) for l in range(n_layers)]
    dq_d = [dint(f"dq{l}", [8 * 128, T], BF16, kind=ikind) for l in range(n_layers)]
    hsrc_d = [dint(f"hsrc{l}", [128, 16], BF16) for l in range(n_layers)]
    hall_d = [dint(f"hall{l}", [4 * 128, 16], BF16) for l in range(n_layers)]
    if debug:
        dbg_cc = [dint(f"dbg_cc{l}", [128, 8, TH], BF16, kind="ExternalOutput") for l in range(1)]
        dbg_x1 = [dint(f"dbg_x1{l}", [128, 8, T], F32, kind="ExternalOutput") for l in range(1)]
        dbg_x2 = [dint(f"dbg_x2{l}", [128, 8, T], F32, kind="ExternalOutput") for l in range(1)]
        dbg_cols = dint("dbg_cols", [128, NCOLS], F32, kind="ExternalOutput")

    with contextlib.ExitStack() as st:
        S = Sched(nc, st)

        def sb(name, shape, dt):
            return st.enter_context(nc.sbuf_tensor(name, list(shape), dt))

        ARW = 38920
        AR = sb("AR", [128, ARW], F32)
        hc = sb("hc", [128, 8, TH], BF16)
        cm = sb("cm", [128, 2, T], BF16)
        cols = sb("cols", [128, NCOLS], F32)
        colsin = sb("colsin_sb", [128, DEPTH, NCOLIN], F32)
        onesb = sb("onesb", [128, 128], BF16)
        onesf = sb("onesf", [128, 128], F32)
        rmat = sb("rmat_sb", [128, 3, 128], F32)
        hmask = sb("hmask_sb", [128, 2, 4], F32)
        gmask = sb("gmask_sb", [128, 4], F32)
        scs_t = sb("scs_t", [128, 8, 2], F32)
        ps = [st.enter_context(nc.psum_tensor(f"ps{i}", [128, 512], F32)) for i in range(8)]

        def carve(off_bytes, shape, dt):
            n = int(np.prod(shape[1:]))
            nb = n * (4 if dt == F32 else 2)
            assert off_bytes % 4 == 0 and nb % 4 == 0
            assert off_bytes + nb <= ARW * 4, (off_bytes, nb)
            v = AR[:, off_bytes // 4:(off_bytes + nb) // 4]
            if dt != F32:
                v = v.bitcast(dt)
            if len(shape) == 3:
                v = v.rearrange("p (a b) -> p a b", a=shape[1])
            elif len(shape) == 4:
                v = v.rearrange("p (a b c) -> p a b c", a=shape[1], b=shape[2])
            elif len(shape) == 5:
                v = v.rearrange("p (a b c d) -> p a b c d", a=shape[1], b=shape[2], c=shape[3])
            return v

        xT = carve(0, [128, 8, T], F32)

        def mm(out, lhsT, rhs, start, stop, reads, writes):
            return S.op("pe", lambda e: e.matmul(out, lhsT=lhsT, rhs=rhs, start=start, stop=stop), reads=reads, writes=writes)

        def act(out, in_, func, reads, writes, scale=1.0, bias=None):
            if bias is None:
                return S.op("act", lambda e: e.activation(out=out, in_=in_, func=func, scale=scale), reads=reads, writes=writes)
            return S.op("act", lambda e: e.activation(out=out, in_=in_, func=func, bias=bias, scale=scale), reads=reads, writes=writes)

        def tt(eng, out, in0, in1, op, reads, writes):
            return S.op(eng, lambda e: e.tensor_tensor(out=out, in0=in0, in1=in1, op=op), reads=reads, writes=writes)

        def ts(eng, out, in0, s1, s2, op0, op1, reads, writes):
            if s2 is None:
                return S.op(eng, lambda e: e.tensor_scalar(out=out, in0=in0, scalar1=s1, scalar2=None, op0=op0), reads=reads, writes=writes)
            return S.op(eng, lambda e: e.tensor_scalar(out=out, in0=in0, scalar1=s1, scalar2=s2, op0=op0, op1=op1), reads=reads, writes=writes)

        def stt(out, in0, scalar, in1, op0, op1, reads, writes):
            return S.op("dve", lambda e: e.scalar_tensor_tensor(out=out, in0=in0, scalar=scalar, in1=in1, op0=op0, op1=op1), reads=reads, writes=writes)

        def cp(eng, out, in_, reads, writes):
            return S.op(eng, lambda e: e.tensor_copy(out=out, in_=in_), reads=reads, writes=writes)

        def recip(out, in_, reads, writes):
            return S.op("dve", lambda e: e.reciprocal(out=out, in_=in_), reads=reads, writes=writes)

        def dma(q, sem, out, in_, reads, writes):
            return S.dma(q, sem, lambda e: e.dma_start(out=out, in_=in_), reads=reads, writes=writes)

        psrr = [0]

        def psnext():
            i = psrr[0] % 8
            psrr[0] += 1
            return i

        def C(name, l=None):
            o, w = COLS[name if l is None else (name, l)]
            return cols[:, o:o + w]

        def CI(name, l):
            o, w = COLIN[name]
            return colsin[:, l, o:o + w]

        dma("sp", "c0", colsin[:], colsin_d, [], ["colsin"])
        dma("sp", "c1", rmat[:], rmat_d, [], ["rmat"])
        dma("sp", "c2", hmask[:], hmask_d, [], ["hmask"])
        dma("sp", "c2", gmask[:], gmask_d, [], ["gmask"])
        S.op("pool", lambda e: e.memset(onesb[:], 1.0), writes=["onesb"])
        S.op("pool", lambda e: e.memset(onesf[:], 1.0), writes=["onesf"])
        S.op("pool", lambda e: e.memset(cols[:], 0.0), writes=["cols"])
        for b, (t0, n) in enumerate(BLKS):
            dma("sp", "xin", xT[:, :, t0:t0 + n], xT_d.rearrange("(k p) t -> p k t", p=128)[:, :, t0:t0 + n], [], [("x", b)])
        o_eps, _ = COLS["eps"]
        S.op("dve", lambda e: e.memset(cols[:, o_eps:o_eps + 1], EPS / (ALPHA * ALPHA)), reads=["cols"], writes=["cols"])
        S.op("dve", lambda e: e.memset(cols[:, o_eps + 1:o_eps + 2], EPS), reads=["cols"], writes=["cols"])
        S.op("dve", lambda e: e.memset(cols[:, o_eps + 2:o_eps + 3], 4.0 * EPS), reads=["cols"], writes=["cols"])
        EPS_LN, EPS_RMS, EPS_G = cols[:, o_eps:o_eps + 1], cols[:, o_eps + 1:o_eps + 2], cols[:, o_eps + 2:o_eps + 3]

        ccs = carve(73728, [128, 8, 2], F32)
        dma("sp", "c3", ccs, cc_d, [], ["ccs"])
        act(scs_t[:], ccs, AF.Silu, ["ccs"], ["scs"])
        o_mod, _ = COLS["mod"]
        ada_it = [0]

        def MOD(l, j, s):
            o = o_mod + (l * 6 + j) * 16 + s * 8
            return cols[:, o:o + 8]

        def ada_bufs(base):
            return [carve(base + i * 32768, [128, 8, 1024], F32) for i in range(2)], carve(base + 2 * 32768, [128, 1024], F32)

        def ada_load(l, j, buf, key, slot):
            dma("sp", f"aw{slot}", buf, ada_w_d[l].rearrange("(k p) c -> p k c", p=128)[:, :, j * 1024:(j + 1) * 1024], [], [key])

        def ada_compute(l, j, buf, key, modrow, mkey, bank=None):
            ident = rmat[0:2, 2, 0:2]
            for half in range(2):
                pr = psnext() if bank is None else bank
                for kk in range(8):
                    mm(ps[pr][0:2, 0:512], scs_t[:, kk, :], buf[:, kk, half * 512:(half + 1) * 512], kk == 0, kk == 7, [key, "scs"], [("ps", pr)])
                cp("dve", modrow[0:2, half * 512:(half + 1) * 512], ps[pr][0:2, 0:512], [("ps", pr)], [mkey])
            pi = psnext() if bank is None else bank
            for ko in range(8):
                mm(ps[pi][:, ko * 2:ko * 2 + 2], modrow[0:2, ko * 128:(ko + 1) * 128], ident, True, True, [mkey, "rmat"], [("ps", pi)])
            o = o_mod + (l * 6 + j) * 16
            for s in range(2):
                tt("dve", cols[:, o + s * 8:o + s * 8 + 8], ps[pi][:, 0:16].rearrange("p (k s) -> p k s", s=2)[:, :, s],
                   CI("adab", l)[:, j * 8:(j + 1) * 8], ALU.add, [("ps", pi), "colsin"], ["cols"])

        def emit_ada(l):
            awb, modrow = ada_bufs(74240)
            for j in range(6):
                it = ada_it[0]
                ada_load(l, j, awb[it % 2], ("awb", it % 2), it % 2)
                ada_compute(l, j, awb[it % 2], ("awb", it % 2), modrow, "modrow")
                ada_it[0] += 1
            ada_finish(l, 73728 + 256)

        def ada_finish(l, lbase):
            for s in range(2):
                sl = slice(s * 8, s * 8 + 8)
                ts("dve", C("sc1p", l)[:, sl], MOD(l, 1, s), 1.0, None, ALU.add, None, ["cols"], ["cols"])
                ts("dve", C("sc2p", l)[:, sl], MOD(l, 4, s), 1.0, None, ALU.add, None, ["cols"], ["cols"])
                ts("dve", C("g1s", l)[:, sl], MOD(l, 2, s), 1.0 / ALPHA, None, ALU.mult, None, ["cols"], ["cols"])
                ts("dve", C("g2s", l)[:, sl], MOD(l, 5, s), 1.0 / ALPHA, None, ALU.mult, None, ["cols"], ["cols"])
                if l == 0:
                    cp("dve", C("A1", l)[:, sl], C("sc1p", l)[:, sl], ["cols"], ["cols"])
                    cp("dve", C("B1", l)[:, sl], MOD(l, 0, s), ["cols"], ["cols"])
                else:
                    tt("dve", C("A1", l)[:, sl], CI("ln2_g", l - 1), C("sc1p", l)[:, sl], ALU.mult, ["cols", "colsin"], ["cols"])
                    tt("dve", C("B1", l)[:, sl], CI("ln2_b", l - 1), C("sc1p", l)[:, sl], ALU.mult, ["cols", "colsin"], ["cols"])
                    tt("dve", C("B1", l)[:, sl], C("B1", l)[:, sl], MOD(l, 0, s), ALU.add, ["cols"], ["cols"])
                tt("dve", C("A2", l)[:, sl], CI("ln1_g", l), C("sc2p", l)[:, sl], ALU.mult, ["cols", "colsin"], ["cols"])
                tt("dve", C("B2", l)[:, sl], CI("ln1_b", l), C("sc2p", l)[:, sl], ALU.mult, ["cols", "colsin"], ["cols"])
                tt("dve", C("B2", l)[:, sl], C("B2", l)[:, sl], MOD(l, 3, s), ALU.add, ["cols"], ["cols"])
            lv = carve(lbase, [128, 4, 32], F32)
            lt = carve(lbase + 512, [128, 2, 32], F32)
            dma("sp", "c4", lv, lamv_d[l], [], ["lv"])
            tt("dve", lt[:, 0, :], lv[:, 0, :], lv[:, 1, :], ALU.mult, ["lv"], ["lt"])
            tt("dve", lt[:, 1, :], lv[:, 2, :], lv[:, 3, :], ALU.mult, ["lv"], ["lt"])
            S.op("dve", lambda e, l=l, lt=lt: e.reduce_sum(out=C("lsum", l), in_=lt, axis=AX.X), reads=["lt"], writes=["cols"])
            act(C("lsum", l), C("lsum", l), AF.Exp, ["cols"], ["cols"])
            lam_init = 0.8 - 0.6 * math.exp(-0.3 * l)
            tt("dve", C("neglam", l), C("lsum", l)[:, 1:2], C("lsum", l)[:, 0:1], ALU.subtract, ["cols"], ["cols"])
            ts("dve", C("neglam", l), C("neglam", l), -lam_init, None, ALU.add, None, ["cols"], ["cols"])
            ts("dve", C("gsub", l), CI("subg", l), 1.0 - lam_init, None, ALU.mult, None, ["colsin"], ["cols"])

        emit_ada(0)
        for l_ in range(2, n_layers):
            emit_ada(l_)
        if debug:
            dma("sp", "dbgc", dbg_cols, cols[:], ["cols"], ["dbgcols"])

        LNT = 73728 + 16384

        def emit_ln(b, l, gname, bname, Aname, Bname, s, write_x, hdst, lnbase):
            t0, n = BLKS[b]
            sqb = [carve(lnbase + i * 2048, [128, 512], F32) for i in range(2)]
            mean = carve(lnbase + 4096, [128, 512], F32)
            m2 = carve(lnbase + 6144, [128, 512], F32)
            rstd = carve(lnbase + 8192, [128, 512], F32)
            xh = [carve(lnbase + 10240 + i * 2048, [128, 512], F32) for i in range(2)]
            xk = ("x", b)
            p1, p2 = psnext(), psnext()
            for k in range(8):
                u = xT[:, k, t0:t0 + n]
                sq = sqb[k % 2]
                tt("pool", sq[:, :n], u, u, ALU.mult, [xk], [("lnsq", k % 2)])
                mm(ps[p1][:, :n], onesf[:], u, k == 0, k == 7, [xk, "onesf"], [("ps", p1)])
                mm(ps[p2][:, :n], onesf[:], sq[:, :n], k == 0, k == 7, [("lnsq", k % 2), "onesf"], [("ps", p2)])
            act(mean[:, :n], ps[p1][:, :n], AF.Identity, [("ps", p1)], ["lnmean"], scale=1.0 / D)
            tt("pool", m2[:, :n], mean[:, :n], mean[:, :n], ALU.mult, ["lnmean"], ["lnm2"])
            stt(rstd[:, :n], ps[p2][:, :n], 1.0 / D, m2[:, :n], ALU.mult, ALU.subtract, [("ps", p2), "lnm2"], ["lnrstd"])
            act(rstd[:, :n], rstd[:, :n], AF.Sqrt, ["lnrstd"], ["lnrstd"], bias=EPS_LN)
            recip(rstd[:, :n], rstd[:, :n], ["lnrstd"], ["lnrstd"])
            for k in range(8):
                u = xT[:, k, t0:t0 + n]
                x_ = xh[k % 2]
                tt("pool", x_[:, :n], u, mean[:, :n], ALU.subtract, [xk, "lnmean"], [("lnxh", k % 2)])
                tt("dve", x_[:, :n], x_[:, :n], rstd[:, :n], ALU.mult, [("lnxh", k % 2), "lnrstd"], [("lnxh", k % 2)])
                if write_x:
                    act(u, x_[:, :n], AF.Identity, [("lnxh", k % 2), "colsin"], [xk],
                        scale=CI(gname, l)[:, k:k + 1], bias=CI(bname, l)[:, k:k + 1])
                if hdst is not None:
                    act(hdst[:, k, t0:t0 + n], x_[:, :n], AF.Identity, [("lnxh", k % 2), "cols"], [("hc", b)],
                        scale=C(Aname[0], Aname[1])[:, s * 8 + k:s * 8 + k + 1], bias=C(Bname[0], Bname[1])[:, s * 8 + k:s * 8 + k + 1])

        for l in range(n_layers):
            last = (l == DEPTH - 1)
            blks_res = [1, 2, 3, 4] if last else [0, 1, 2, 3, 4]
            if l == 0:
                for b, (t0, n) in enumerate(BLKS):
                    s = 1 if b == 0 else 0
                    for k in range(8):
                        act(hc[:, k, t0:t0 + n], xT[:, k, t0:t0 + n], AF.Identity, [("x", b), "cols"], [("hc", b)],
                            scale=C("A1", l)[:, s * 8 + k:s * 8 + k + 1], bias=C("B1", l)[:, s * 8 + k:s * 8 + k + 1])
            for b in blks_res:
                t0, n = BLKS[b]
                dma("sp", "xsp", xsp_d[l][:, :, t0:t0 + n], xT[:, :, t0:t0 + n], [("x", b)], [("xsp", b)])
            if debug and False:
                dma("sp", "dbgh", dbg_h[l], hc[:], [("hc", b) for b in range(5)], ["dbgh"])
            S.barrier()

            win = carve(0, [128, 8, INW], BF16)
            wuq = carve(31744, [128, 3, 768], BF16)
            wukv = carve(36352, [128, 2, 1024], BF16)
            tabs = carve(40448, [128, 4, 512], F32)
            qlat = carve(48640, [128, 3, 512], F32)
            kvlat = carve(54784, [128, 2, 512], F32)
            sqq = carve(58880, [128, 2, 512], F32)
            rstdq = carve(62976, [128, 2, 512], F32)
            qn = carve(67072, [128, 3, 512], BF16)
            kvn = carve(70144, [128, 2, 512], BF16)
            rxs = carve(72192, [128, 2, 512], F32)
            rt1 = carve(76288, [128, 2, 512], F32)
            rt2 = carve(80384, [128, 2, 512], F32)
            stQn = carve(84480, [128, 4, 512], BF16)
            stQr = carve(88576, [128, 4, 512], BF16)
            stdQ = carve(92672, [128, 2, 512], BF16)
            stKn = carve(94720, [128, 4, 512], BF16)
            stKr = carve(98816, [128, 512], BF16)
            stdK = carve(99840, [128, 2, 512], BF16)
            stVm = carve(101888, [128, 4, 512], BF16)
            stVd = carve(105984, [128, 4, 256], BF16)
            ug = carve(108032, [128, 2, 512], F32)
            gtm = carve(112128, [128, 2, 512], F32)
            vt = carve(116224, [128, 2, 256], F32)
            vt2 = carve(118272, [128, 2, 256], F32)
            vbf = carve(120320, [128, 2, 256], BF16)
            bnst = carve(121344, [128, 2, 8], F32)
            bnag = carve(121408, [128, 2, 2], F32)
            tmpc = carve(121424, [128, 2, 128], F32)
            wsT = carve(122448, [128, 4, 128], BF16)
            bsT = carve(123472, [128, 2, 128], F32)
            sgb = carve(124496, [128, 2, 256], F32)
            stM = carve(126976, [128, 8, 512], BF16)
            rxs5 = carve(135168, [128, 5, 512], F32)
            sqq5 = carve(145408, [128, 5, 512], F32)
            vbf4 = carve(58880, [128, 4, 256], BF16)

            for kk in range(8):
                dma("pool", f"wl{kk}", win[:, kk, :], w_in_d[l, kk * 128:(kk + 1) * 128, :], [], [("win", kk)])
            dma("pool", "wld", wuq, wuq_d[l].rearrange("(k p) c -> p k c", p=128), [], ["wuq"])
            dma("pool", "wld", wukv, wukv_d[l].rearrange("(k p) c -> p k c", p=128), [], ["wukv"])
            dma("pool", "wld", wsT, sgu_ws_d[l], [], ["wsT"])
            dma("sp", "c5", bsT, sgu_bsT_d[l], [], ["bsT"])
            dma("sp", "c6", sgb, sgu_gb_d[l], [], ["sgb"])

            RM = rmat[0:64, 0, 0:64]
            RD = rmat[:, 1, :]

            def rope(pz, P, n, scale, cos, sin, R, dst, dkeys, wkeys, i):
                xs_, t1_, t2_ = rxs[0:P, i % 2, :n], rt1[0:P, i % 2, :n], rt2[0:P, i % 2, :n]
                act(xs_, pz, AF.Identity, dkeys, [("rxs", i % 2)], scale=scale)
                pr = psnext()
                mm(ps[pr][0:P, :n], R, xs_, True, True, [("rxs", i % 2), "rmat"], [("ps", pr)])
                tt("pool", t1_, xs_, cos, ALU.mult, [("rxs", i % 2), "tabs"], [("rt1", i % 2)])
                tt("dve", t2_, ps[pr][0:P, :n], sin, ALU.mult, [("ps", pr), "tabs"], [("rt2", i % 2)])
                tt("pool", dst, t1_, t2_, ALU.add, [("rt1", i % 2), ("rt2", i % 2)], wkeys)

            ri = [0]
            for b, (t0, n) in enumerate(BLKS):
                hk = ("hc", b)
                ntile = n // 128
                dma("sp", "tabs", tabs[:, :, :n], tabs_d[:, :, t0:t0 + n], [], ["tabs"])
                cosM, sinM, cosD, sinD = tabs[0:64, 0, :n], tabs[0:64, 1, :n], tabs[:, 2, :n], tabs[:, 3, :n]

                def proj(col0, M, K_src="h"):
                    pi = psnext()
                    for k in range(8):
                        mm(ps[pi][0:M, :n], win[:, k, col0:col0 + M], hc[:, k, t0:t0 + n], k == 0, k == 7, [("win", k), hk], [("ps", pi)])
                    return pi

                def rms(src, nch, width, gname, dstn, tag, ti):
                    pi = psnext()
                    for c in range(nch):
                        tt("pool", sqq[:, c % 2, :n], src[:, c, :n], src[:, c, :n], ALU.mult, [tag], [("sqq", c % 2)])
                        mm(ps[pi][:, :n], onesf[:], sqq[:, c % 2, :n], c == 0, c == nch - 1, [("sqq", c % 2), "onesf"], [("ps", pi)])
                    r_ = rstdq[:, ti, :n]
                    act(r_, ps[pi][:, :n], AF.Sqrt, [("ps", pi)], [("rstdq", ti)], scale=1.0 / width, bias=EPS_RMS)
                    recip(r_, r_, [("rstdq", ti)], [("rstdq", ti)])
                    for c in range(nch):
                        stt(dstn[:, c, :n], src[:, c, :n], CI(gname, l)[:, c:c + 1], r_, ALU.mult, ALU.mult,
                            [tag, ("rstdq", ti), "colsin"], [tag + "n"])

                for c in range(3):
                    pi = proj(c * 128, 128)
                    act(qlat[:, c, :n], ps[pi][:, :n], AF.Identity, [("ps", pi)], [("qlat", c)])
                    tt("pool", sqq5[:, c, :n], qlat[:, c, :n], qlat[:, c, :n], ALU.mult, [("qlat", c)], [("sqq5", c)])
                for c in range(2):
                    pi = proj(384 + c * 128, 128)
                    act(kvlat[:, c, :n], ps[pi][:, :n], AF.Identity, [("ps", pi)], [("kvlat", c)])
                    tt("pool", sqq5[:, 3 + c, :n], kvlat[:, c, :n], kvlat[:, c, :n], ALU.mult, [("kvlat", c)], [("sqq5", 3 + c)])
                rspec = [(640, 64, 1.0), (704, 128, DIFF_SCALE), (832, 128, DIFF_SCALE), (960, 128, 1.0), (1088, 128, 1.0)]
                for idx, (col0, P_, sc_) in enumerate(rspec):
                    pi = proj(col0, P_)
                    act(rxs5[0:P_, idx, :n], ps[pi][0:P_, :n], AF.Identity, [("ps", pi)], [("rxs5", idx)], scale=sc_)
                for t in range(ntile):
                    pi = psnext()
                    for k in range(8):
                        mm(ps[pi][:, 0:256], hc[:, k, t0 + t * 128:t0 + (t + 1) * 128], win[:, k, 1216:1472], k == 0, k == 7, [("win", k), hk], [("ps", pi)])
                    cp("dve", stVd[:, t, :], ps[pi][:, 0:256], [("ps", pi)], ["stVd"])
                for c in range(2):
                    pi = proj(1472 + c * 128, 128)
                    xs_ = gtm[:, 0, :n]
                    t_ = gtm[:, 1, :n]
                    act(xs_, ps[pi][:, :n], AF.Identity, [("ps", pi)], ["gxs"])
                    tt("pool", t_, xs_, xs_, ALU.mult, ["gxs"], ["gt"])
                    ts("pool", t_, t_, 0.044715, 1.0, ALU.mult, ALU.add, ["gt"], ["gt"])
                    tt("pool", t_, t_, xs_, ALU.mult, ["gt", "gxs"], ["gt"])
                    act(t_, t_, AF.Tanh, ["gt"], ["gt"], scale=0.7978845608028654)
                    stt(ug[:, c, :n], t_, 1.0, xs_, ALU.add, ALU.mult, ["gt", "gxs"], [("ug", c)])
                for t in range(ntile):
                    vi = t % 2
                    pi = psnext()
                    for k in range(8):
                        mm(ps[pi][:, 0:256], hc[:, k, t0 + t * 128:t0 + (t + 1) * 128], win[:, k, 1728:1984], k == 0, k == 7, [("win", k), hk], [("ps", pi)])
                    xs_, t_ = vt[:, vi, :], vt2[:, vi, :]
                    act(xs_, ps[pi][:, 0:256], AF.Identity, [("ps", pi)], [("vt", vi)])
                    tt("pool", t_, xs_, xs_, ALU.mult, [("vt", vi)], [("vt2", vi)])
                    ts("pool", t_, t_, 0.044715, 1.0, ALU.mult, ALU.add, [("vt2", vi)], [("vt2", vi)])
                    tt("pool", t_, t_, xs_, ALU.mult, [("vt2", vi), ("vt", vi)], [("vt2", vi)])
                    act(t_, t_, AF.Tanh, [("vt2", vi)], [("vt2", vi)], scale=0.7978845608028654)
                    stt(xs_, t_, 1.0, xs_, ALU.add, ALU.mult, [("vt2", vi), ("vt", vi)], [("vt", vi)])
                    S.op("dve", lambda e, o=bnst[:, vi, 0:6], i_=xs_: e.bn_stats(out=o, in_=i_), reads=[("vt", vi)], writes=[("bnst", vi)])
                    S.op("dve", lambda e, o=bnag[:, vi, :], i_=bnst[:, vi, 0:6]: e.bn_aggr(out=o, in_=i_), reads=[("bnst", vi)], writes=[("bnag", vi)])
                    act(bnag[:, vi, 1:2], bnag[:, vi, 1:2], AF.Sqrt, [("bnag", vi)], [("bnag", vi)], bias=EPS_G)
                    recip(bnag[:, vi, 1:2], bnag[:, vi, 1:2], [("bnag", vi)], [("bnag", vi)])
                    ts("dve", xs_, xs_, bnag[:, vi, 0:1], bnag[:, vi, 1:2], ALU.subtract, ALU.mult, [("vt", vi), ("bnag", vi)], [("vt", vi)])
                    tt("pool", xs_, xs_, sgb[:, 0, :], ALU.mult, [("vt", vi), "sgb"], [("vt", vi)])
                    tt("pool", vbf4[:, t, :], xs_, sgb[:, 1, :], ALU.add, [("vt", vi), "sgb"], [("vbf4", t)])

                def rms_b(src, nch, sq0, width, gname, dstn, tag, ti):
                    pi = psnext()
                    for c in range(nch):
                        mm(ps[pi][:, :n], onesf[:], sqq5[:, sq0 + c, :n], c == 0, c == nch - 1, [("sqq5", sq0 + c), "onesf"], [("ps", pi)])
                    r_ = rstdq[:, ti, :n]
                    act(r_, ps[pi][:, :n], AF.Sqrt, [("ps", pi)], [("rstdq", ti)], scale=1.0 / width, bias=EPS_RMS)
                    recip(r_, r_, [("rstdq", ti)], [("rstdq", ti)])
                    for c in range(nch):
                        stt(dstn[:, c, :n], src[:, c, :n], CI(gname, l)[:, c:c + 1], r_, ALU.mult, ALU.mult,
                            [(tag, c), ("rstdq", ti), "colsin"], [tag + "n"])

                rms_b(qlat, 3, 0, 384, "gq", qn, "qlat", 0)
                rms_b(kvlat, 2, 3, 256, "gkv", kvn, "kvlat", 1)
                rdst = [(64, cosM, sinM, RM, stKr[0:64, :n], "stKr"), (128, cosD, sinD, RD, stdQ[:, 0, :n], "stdQ"),
                        (128, cosD, sinD, RD, stdQ[:, 1, :n], "stdQ"), (128, cosD, sinD, RD, stdK[:, 0, :n], "stdK"),
                        (128, cosD, sinD, RD, stdK[:, 1, :n], "stdK")]
                for idx, (P_, cos_, sin_, R_, dst_, wk_) in enumerate(rdst):
                    i_ = ri[0]
                    ri[0] += 1
                    xs_, t1_, t2_ = rxs5[0:P_, idx, :n], rt1[0:P_, i_ % 2, :n], rt2[0:P_, i_ % 2, :n]
                    pr = psnext()
                    mm(ps[pr][0:P_, :n], R_, xs_, True, True, [("rxs5", idx), "rmat"], [("ps", pr)])
                    tt("pool", t1_, xs_, cos_, ALU.mult, [("rxs5", idx), "tabs"], [("rt1", i_ % 2)])
                    tt("dve", t2_, ps[pr][0:P_, :n], sin_, ALU.mult, [("ps", pr), "tabs"], [("rt2", i_ % 2)])
                    tt("pool", dst_, t1_, t2_, ALU.add, [("rt1", i_ % 2), ("rt2", i_ % 2)], [wk_])
                    if 1 <= idx <= 2:
                        c = idx - 1
                        for j in range(4):
                            ts("dve", stM[:, c * 4 + j, :n], stdQ[:, c, :n], gmask[:, j:j + 1], None, ALU.mult, None, ["stdQ", "gmask"], ["stM"])
                for h in range(4):
                    pi = psnext()
                    for c in range(3):
                        mm(ps[pi][:, :n], wuq[:, c, h * 128:(h + 1) * 128], qn[:, c, :n], c == 0, c == 2, ["wuq", "qlatn"], [("ps", pi)])
                    act(stQn[:, h, :n], ps[pi][:, :n], AF.Identity, [("ps", pi)], ["stQn"], scale=MLA_SCALE)
                for h in range(4):
                    pi = psnext()
                    for c in range(3):
                        mm(ps[pi][0:64, :n], wuq[:, c, 512 + h * 64:512 + (h + 1) * 64], qn[:, c, :n], c == 0, c == 2, ["wuq", "qlatn"], [("ps", pi)])
                    rope(ps[pi][0:64, :n], 64, n, MLA_SCALE, cosM, sinM, RM, stQr[0:64, h, :n], [("ps", pi)], ["stQr"], ri[0])
                    ri[0] += 1
                for h in range(4):
                    pi = psnext()
                    for c in range(2):
                        mm(ps[pi][:, :n], wukv[:, c, h * 128:(h + 1) * 128], kvn[:, c, :n], c == 0, c == 1, ["wukv", "kvlatn"], [("ps", pi)])
                    cp("dve", stKn[:, h, :n], ps[pi][:, :n], [("ps", pi)], ["stKn"])
                for t in range(ntile):
                    pi = psnext()
                    for c in range(2):
                        mm(ps[pi][:, :], kvn[:, c, t * 128:(t + 1) * 128], wukv[:, c, 512:1024], c == 0, c == 1, ["wukv", "kvlatn"], [("ps", pi)])
                    act(stVm[:, t, :], ps[pi][:, :], AF.Identity, [("ps", pi)], ["stVm"])
                for t in range(ntile):
                    for c in range(2):
                        pm = psnext()
                        for gg in range(2):
                            g = 2 * c + gg
                            mm(ps[pm][gg * 64:(gg + 1) * 64, 0:128], vbf4[:, t, g * 64:(g + 1) * 64], wsT[:, g, :], True, True,
                               [("vbf4", t), "wsT"], [("ps", pm)])
                        tt("dve", tmpc[:, c, :], ps[pm][:, 0:128], bsT[:, c, :], ALU.add, [("ps", pm), "bsT"], [("tmpc", c)])
                        stt(cm[:, c, t0 + t * 128:t0 + (t + 1) * 128], ug[:, c, t * 128:(t + 1) * 128], 0.5, tmpc[:, c, :], ALU.mult, ALU.mult,
                            [("ug", c), ("tmpc", c)], [("cm", b)])
                dma("sp", "stq1", qn_d[l].rearrange("(h p) t -> p h t", p=128)[:, :, t0:t0 + n], stQn[:, :, :n], ["stQn"], ["qn_d"])
                dma("sp", "stq2", qr_d[l].rearrange("(h p) t -> p h t", p=64)[:, :, t0:t0 + n], stQr[0:64, :, :n], ["stQr"], ["qr_d"])
                dma("sp", "stq3", dq_d[l].rearrange("(g p) t -> p g t", p=128)[:, :, t0:t0 + n], stM[:, :, :n], ["stM"], ["dq_d"])
                if b == 0:
                    kd, c0, ncol, nt_all = kvctx_d[l], 0, CTX, 2
                else:
                    kd, c0, ncol, nt_all = kvsrc_d[l], t0 - CTX, OWN, 16
                dma("sp", "stk1", kd[R_KN:R_KN + 512, :].rearrange("(h p) t -> p h t", p=128)[:, :, c0:c0 + n], stKn[:, :, :n], ["stKn"], ["kv_d"])
                dma("sp", "stk2", kd[R_KR:R_KR + 64, c0:c0 + n], stKr[0:64, :n], ["stKr"], ["kv_d"])
                for c in range(2):
                    dma("sp", "stk3", kd[R_DK + c * 128:R_DK + (c + 1) * 128, c0:c0 + n], stdK[:, c, :n], ["stdK"], ["kv_d"])
                tl0 = c0 // 128
                vmv = kd[R_VM:R_VM + 512, :].rearrange("(h p) (t d) -> p t h d", p=128, d=128)
                for h in range(4):
                    dma("sp", "stv1", vmv[:, tl0:tl0 + ntile, h, :], stVm[:, 0:ntile, h * 128:(h + 1) * 128], ["stVm"], ["kv_d"])
                vdv = kd[R_VD:R_VD + 256, :].rearrange("a (two rest) -> (a two) rest", two=2).rearrange("(h p) (t d) -> p t h d", p=128, d=64)
                for h in range(4):
                    dma("sp", "stv2", vdv[:, tl0:tl0 + ntile, h, :], stVd[:, 0:ntile, h * 64:(h + 1) * 64], ["stVd"], ["kv_d"])
            S.barrier()
            for c_ in range((KVROWS + CH - 1) // CH):
                sz = min(CH, KVROWS - c_ * CH)
                S.dma("pool", f"cc{c_ % 2}", lambda e, l=l, c_=c_, sz=sz: e.collective_compute(
                    "AllGather", ALU.bypass, replica_groups=[[0, 1, 2, 3], [4, 5, 6, 7]],
                    ins=[kvsrc_d[l][c_ * CH:c_ * CH + sz, :]], outs=[kvall_d[l][c_ * 4 * CH:c_ * 4 * CH + 4 * sz, :]]),
                    reads=["kv_d"], writes=["kvall"], inc1=True)
            if debug:
                pass
            if debug and False:
                dma("sp", "dbgcm", dbg_cm[l], cm[:], [("cm", b) for b in range(5)], ["dbgcm"])
            S.barrier()

            KA = carve(0, [128, NKEY], BF16)
            KR = carve(16896, [128, NKEY], BF16)
            VA = carve(33792, [128, NKT, 128], BF16)
            QA = carve(50688, [128, T], BF16)
            QB = carve(55296, [128, T], BF16)
            PT = carve(59904, [128, 4, 512], BF16)
            RT = carve(64000, [128, 2, 512], F32)
            DD = carve(68096, [128, 6, 512], F32)
            ACC = carve(80384, [128, 2, 2, 512], F32)
            ACC = carve(80384, [128, 2, 2, 512], F32)
            def kv_view(a, n_):
                c_ = a // CH
                i0 = a % CH
                sz = min(CH, KVROWS - c_ * CH)
                assert i0 + n_ <= sz
                return kvall_d[l][c_ * 4 * CH:c_ * 4 * CH + 4 * sz, :].rearrange("(r i) col -> i r col", r=4)[i0:i0 + n_]
            kctx = kvctx_d[l]
            SEGS = [(0, 2)] + [(2 + r * 16, 16) for r in range(4)]

            def seg_of(kt):
                return 0 if kt < 2 else 1 + (kt - 2) // 16

            def load_k(dst, rows0, nrows, name, sem):
                dma("sp", sem, dst[0:nrows, 0:CTX], kctx[rows0:rows0 + nrows, :], ["kv_d"], [(name, 0)])
                for r in range(4):
                    dma("sp", sem, dst[0:nrows, CTX + r * OWN:CTX + (r + 1) * OWN], kv_view(rows0, nrows)[:, r, :], ["kvall"], [(name, 1 + r)])

            qblocks = ([(0, 256, 2, 0)] if not last else []) + [(256 + i * 512, 512, NKT, i + 1) for i in range(4)]

            load_k(KR, R_KR, 64, "KR", "ldkr")
            steps = []
            for h in range(4):
                for (q0, nq, nk, qb) in qblocks:
                    for kt in range(nk):
                        steps.append((h, q0, nq, nk, qb, kt))
            qbi_of = {}
            for st_ in steps:
                key = (st_[0], st_[4])
                if key not in qbi_of:
                    qbi_of[key] = len(qbi_of)

            def mla_S(i):
                h, q0, nq, nk, qb, kt = steps[i]
                if kt == 0 and (qb == qblocks[0][3]):
                    load_k(KA, R_KN + h * 128, 128, "KA", "ldka")
                    dma("sp", "ldqa", QA, qn_d[l][h * 128:(h + 1) * 128, :], ["qn_d"], ["QA"])
                    dma("sp", "ldqb", QB[0:64, :], qr_d[l][h * 64:(h + 1) * 64, :], ["qr_d"], ["QB"])
                sg = seg_of(kt)
                sb_ = i % 3
                mm(ps[sb_][:, :nq], KA[:, kt * 128:(kt + 1) * 128], QA[:, q0:q0 + nq], True, False, [("KA", sg), "QA"], [("ps", sb_)])
                mm(ps[sb_][:, :nq], KR[0:64, kt * 128:(kt + 1) * 128], QB[0:64, q0:q0 + nq], False, True, [("KR", sg), "QB"], [("ps", sb_)])

            def mla_PV(i):
                h, q0, nq, nk, qb, kt = steps[i]
                sg = seg_of(kt)
                sb_ = i % 3
                qbi = qbi_of[(h, qb)]
                ob, sm = 3 + qbi % 2, 5 + qbi % 2
                if kt == 0 and (qb == qblocks[0][3]):
                    vm_c = kctx[R_VM + h * 128:R_VM + (h + 1) * 128, :].rearrange("p (t d) -> p t d", d=128)
                    dma("sp", "ldva", VA[:, 0:2, :], vm_c, ["kv_d"], [("VA", 0)])
                    for r in range(4):
                        dma("sp", "ldva", VA[:, 2 + r * 16:2 + (r + 1) * 16, :],
                            kv_view(R_VM + h * 128, 128)[:, r, :].rearrange("p (t d) -> p t d", d=128), ["kvall"], [("VA", 1 + r)])
                p_ = PT[:, i % 4, :nq]
                act(p_, ps[sb_][:, :nq], AF.Exp, [("ps", sb_)], [("PT", i % 4)])
                mm(ps[ob][:, :nq], VA[:, kt, :], p_, kt == 0, kt == nk - 1, [("VA", sg), ("PT", i % 4)], [("ps", ob)])
                mm(ps[sm][:, :nq], onesb[:], p_, kt == 0, kt == nk - 1, ["onesb", ("PT", i % 4)], [("ps", sm)])
                if kt == nk - 1:
                    r_ = RT[:, qbi % 2, :nq]
                    recip(r_, ps[sm][:, :nq], [("ps", sm)], [("RT", qbi % 2)])
                    tt("dve", hc[:, h, q0:q0 + nq], ps[ob][:, :nq], r_, ALU.mult, [("ps", ob), ("RT", qbi % 2)], [("hc", qb)])

            if steps:
                mla_S(0)
                if len(steps) > 1:
                    mla_S(1)
                for i in range(len(steps)):
                    if i + 2 < len(steps):
                        mla_S(i + 2)
                    mla_PV(i)

            dsteps = []
            for h in range(4):
                for (q0, nq, nk, qb) in qblocks:
                    for m in range(2):
                        for kt in range(nk):
                            dsteps.append((h, q0, nq, nk, qb, m, kt))
            dqbi = {}
            for st_ in dsteps:
                key = (st_[0], st_[4])
                if key not in dqbi:
                    dqbi[key] = len(dqbi)
            KM = [KA, KR]
            QM = [QA, QB]
            KMN = ["KA", "KR"]
            QMN = ["QA", "QB"]

            def d_S(i):
                h, q0, nq, nk, qb, m, kt = dsteps[i]
                if kt == 0 and m == 0 and qb == qblocks[0][3]:
                    if h % 2 == 0:
                        load_k(KA, R_DK + (h // 2) * 128, 128, "KA", "ldka")
                    for mm_ in range(2):
                        g = 2 * h + mm_
                        dma("sp", "ldqa" if mm_ == 0 else "ldqb", QM[mm_][:, :], dq_d[l][g * 128:(g + 1) * 128, :], ["dq_d"], [QMN[mm_]])
                sg = seg_of(kt)
                sb_ = i % 3
                mm(ps[sb_][:, :nq], KA[:, kt * 128:(kt + 1) * 128], QM[m][:, q0:q0 + nq], True, True, [("KA", sg), QMN[m]], [("ps", sb_)])

            def d_PV(i):
                h, q0, nq, nk, qb, m, kt = dsteps[i]
                sg = seg_of(kt)
                sb_ = i % 3
                qbi = dqbi[(h, qb)]
                ob = 3 + (qbi % 2) * 2 + m
                if kt == 0 and m == 0 and qb == qblocks[0][3]:
                    if h == 0:
                        S.op("pool", lambda e: e.memset(VA[:, :, 64:128], 1.0), reads=[], writes=[("VA", s_) for s_ in range(5)])
                    vdc = kctx[R_VD:R_VD + 256, :].rearrange("a (two rest) -> (a two) rest", two=2)[h * 128:(h + 1) * 128, :].rearrange("p (t d) -> p t d", d=64)
                    dma("sp", "ldva", VA[:, 0:2, 0:64], vdc, ["kv_d"], [("VA", 0)])
                    for r in range(4):
                        vdr = kv_view(R_VD + h * 64, 64)[:, r, :].rearrange("a (two rest) -> (a two) rest", two=2).rearrange("p (t d) -> p t d", d=64)
                        dma("sp", "ldva", VA[:, 2 + r * 16:2 + (r + 1) * 16, 0:64], vdr, ["kvall"], [("VA", 1 + r)])
                    if l == 0 and n_layers > 1:
                        awb2, modrow2 = ada_bufs(81920)
                        if h >= 1:
                            for g_ in (2 * (h - 1), 2 * (h - 1) + 1):
                                ada_compute(1, g_, awb2[g_ % 2], ("awb2", g_ % 2), modrow2, "modrow2", bank=7)
                        if h <= 2:
                            for g_ in (2 * h, 2 * h + 1):
                                ada_load(1, g_, awb2[g_ % 2], ("awb2", g_ % 2), g_ % 2)
                        if h == 3:
                            ada_finish(1, 81920 + 2 * 32768 + 4096)
                p_ = PT[:, i % 4, :nq]
                act(p_, ps[sb_][:, :nq], AF.Exp, [("ps", sb_)], [("PT", i % 4)])
                mm(ps[ob][:, :nq], VA[:, kt, :], p_, kt == 0, kt == nk - 1, [("VA", sg), ("PT", i % 4)], [("ps", ob)])
                if kt == nk - 1 and m == 1:
                    o0, o1 = 3 + (qbi % 2) * 2, 4 + (qbi % 2) * 2
                    sq_b = 7
                    r0, r1 = RT[0:64, 0, :nq], RT[0:64, 1, :nq]
                    d1, d2, dd, sq_, rs_ = DD[0:64, 0, :nq], DD[0:64, 1, :nq], DD[0:64, 2, :nq], DD[0:64, 3, :nq], DD[0:64, 4, :nq]
                    recip(r0, ps[o0][64:128, :nq], [("ps", o0)], [("RT", 0)])
                    recip(r1, ps[o1][64:128, :nq], [("ps", o1)], [("RT", 1)])
                    tt("dve", d1, ps[o0][0:64, :nq], r0, ALU.mult, [("ps", o0), ("RT", 0)], ["dd1"])
                    tt("dve", d2, ps[o1][0:64, :nq], r1, ALU.mult, [("ps", o1), ("RT", 1)], ["dd2"])
                    stt(dd, d2, C("neglam", l)[0:64, :], d1, ALU.mult, ALU.add, ["dd1", "dd2", "cols"], ["ddd"])
                    tt("pool", sq_, dd, dd, ALU.mult, ["ddd"], ["ddsq"])
                    mm(ps[sq_b][0:64, :nq], onesf[0:64, 0:64], sq_, True, True, ["ddsq", "onesf"], [("ps", sq_b)])
                    act(rs_, ps[sq_b][0:64, :nq], AF.Ln, [("ps", sq_b)], ["ddrs"], scale=1.0 / 64, bias=EPS_RMS[0:64, :])
                    act(rs_, rs_, AF.Exp, ["ddrs"], ["ddrs"], scale=-0.5)
                    hb = (h % 2) * 64
                    stt(hc[hb:hb + 64, 4 + h // 2, q0:q0 + nq], dd, C("gsub", l)[0:64, :], rs_, ALU.mult, ALU.mult, ["ddd", "ddrs", "cols"], [("hc", qb)])

            if dsteps:
                d_S(0)
                if len(dsteps) > 1:
                    d_S(1)
                for i in range(len(dsteps)):
                    if i + 2 < len(dsteps):
                        d_S(i + 2)
                    d_PV(i)
            if debug and l == 0:
                dma("sp", "dbgcc", dbg_cc[l], hc[:], [("hc", b) for b in range(5)], ["dbgcc"])
            S.barrier()

            wo = carve(73728, [128, 8, 1024], BF16)
            for kk in range(8):
                dma("pool", f"wl{kk}", wo[:, kk, :], w_o_d[l, kk * 128:(kk + 1) * 128, :], [], [("wo", kk)])
            for b in blks_res:
                t0, n = BLKS[b]
                dma("sp", "xin", xT[:, :, t0:t0 + n], xsp_d[l][:, :, t0:t0 + n], [("xsp", b)], [("x", b)])
            prev_b = None
            for b in blks_res:
                t0, n = BLKS[b]
                s = 1 if b == 0 else 0
                for oc in range(8):
                    pi = psnext()
                    for k in range(8):
                        rhs = hc[:, k, t0:t0 + n] if k < 6 else cm[:, k - 6, t0:t0 + n]
                        mm(ps[pi][:, :n], wo[:, k, oc * 128:(oc + 1) * 128], rhs, k == 0, k == 7, [("wo", k), ("hc", b), ("cm", b)], [("ps", pi)])
                    stt(xT[:, oc, t0:t0 + n], ps[pi][:, :n], C("g1s", l)[:, s * 8 + oc:s * 8 + oc + 1], xT[:, oc, t0:t0 + n], ALU.mult, ALU.add,
                        [("ps", pi), ("x", b), "cols"], [("x", b)])
                if prev_b is not None:
                    emit_ln(prev_b, l, "ln1_g", "ln1_b", ("A2", l), ("B2", l), 1 if prev_b == 0 else 0, True, hc, LNT)
                prev_b = b
            emit_ln(prev_b, l, "ln1_g", "ln1_b", ("A2", l), ("B2", l), 1 if prev_b == 0 else 0, True, hc, LNT)
            if debug and l == 0:
                dma("sp", "dbgx1", dbg_x1[l], xT, [("x", b) for b in range(5)], ["dbgx1"])

            hl = carve(LNT + 16384, [128, 2, 8], BF16)
            hg = carve(LNT + 16384 + 64, [128, 4, 16], BF16)
            hacc = carve(LNT + 16384 + 256, [128, 2, 8], F32)
            cp("dve", hl[:, 0, :], hc[:, :, CTX], [("hc", 1)], ["hl"])
            cp("dve", hl[:, 1, :], hc[:, :, T - 1], [("hc", 4)], ["hl"])
            dma("sp", "hs", hsrc_d[l], hl.rearrange("p a b -> p (a b)"), ["hl"], ["hsrc"])
            if True:
                S.dma("pool", "cc", lambda e, l=l: e.collective_compute("AllGather", ALU.bypass, replica_groups=[[0, 1, 2, 3], [4, 5, 6, 7]],
                                                                      ins=[hsrc_d[l]], outs=[hall_d[l]]), reads=["hsrc"], writes=["hall"], inc1=True)
            dma("sp", "hg", hg, hall_d[l].rearrange("(r p) c -> p r c", p=128), ["hall"], ["hg"])
            for side in range(2):
                w_ = 1 - side
                for r in range(4):
                    src = hg[:, r, w_ * 8:(w_ + 1) * 8]
                    mcol = hmask[:, side, r:r + 1]
                    if r == 0:
                        ts("dve", hacc[:, side, :], src, mcol, None, ALU.mult, None, ["hg", "hmask"], ["hacc"])
                    else:
                        stt(hacc[:, side, :], src, mcol, hacc[:, side, :], ALU.mult, ALU.add, ["hg", "hmask", "hacc"], ["hacc"])
                cp("dve", hc[:, :, T + side], hacc[:, side, :], ["hacc"], ["hchalo"])
            S.barrier()

            FB = 73728
            wgv = [carve(FB + i * 4096, [128, 8, 2, 128], BF16) for i in range(3)]
            wd = [carve(FB + 12288 + i * 2048, [128, 1024], BF16) for i in range(3)]
            ugb = [carve(FB + 18432 + i * 16400, [128, 2, OWN + 2], F32) for i in range(2)]
            cacc = carve(FB + 18432 + 32800, [128, 2, OWN], F32)
            aT = [carve(FB + 18432 + 32800 + 16384 + i * 4096, [128, OWN], BF16) for i in range(2)]
            CB = FB + 18432 + 32800 + 16384 + 8192
            ugc = carve(CB, [128, 2, CTX + 2], F32)
            caccc = carve(CB + 2064, [128, 2, CTX], F32)
            aTc = carve(CB + 2064 + 2048, [128, CTX], BF16)
            assert CB + 2064 + 2048 + 512 <= ARW * 4
            if not last:
                S.op("pool", lambda e, o=ugc: e.memset(o, 0.0), writes=["ugc"])
            wupv = wup_d[l].rearrange("(k p) (gv f c) -> p k gv f c", p=128, gv=2, c=128)

            facc = cm[:].rearrange("p a b -> p (a b)").bitcast(F32)[:, 0:2048].rearrange("p (a b) -> p a b", a=4)
            facc_i = [0]

            def cw(f, gv):
                fc = gv * NFC + f
                return (CI("cw0", l)[:, fc:fc + 1], CI("cw1", l)[:, fc:fc + 1], CI("cw2", l)[:, fc:fc + 1], CI("cb", l)[:, fc:fc + 1])

            def ffn_load(f):
                wb_ = f % 3
                for gv in range(2):
                    dma("pool", f"wgv{wb_}", wgv[wb_][:, :, gv, :], wupv[:, :, gv, f, :], [], [("wgv", wb_)])
                dma("pool", f"wd{wb_}", wd[wb_], wdown_d[l, f * 128:(f + 1) * 128, :], [], [("wd", wb_)])

            def ffn_up(f):
                wb_, ub = f % 3, f % 2
                for gv in range(2):
                    for tb in range(4):
                        t0 = CTX + tb * 512
                        pp = psnext()
                        for k in range(8):
                            mm(ps[pp][:, :], wgv[wb_][:, k, gv, :], hc[:, k, t0:t0 + 512], k == 0, k == 7, [("wgv", wb_), ("hc", tb + 1)], [("ps", pp)])
                        act(ugb[ub][:, gv, 1 + tb * 512:1 + (tb + 1) * 512], ps[pp][:, :], AF.Identity, [("ps", pp)], [("ugb", ub, gv)])
                    pp = psnext()
                    for k in range(8):
                        mm(ps[pp][:, 0:2], wgv[wb_][:, k, gv, :], hc[:, k, T:T + 2], k == 0, k == 7, [("wgv", wb_), "hchalo"], [("ps", pp)])
                    cp("dve", ugb[ub][:, gv, 0:1], ps[pp][:, 0:1], [("ps", pp)], [("ugb", ub, gv)])
                    cp("dve", ugb[ub][:, gv, OWN + 1:OWN + 2], ps[pp][:, 1:2], [("ps", pp)], [("ugb", ub, gv)])

            def ffn_conv(f):
                ub = f % 2
                for gv in range(2):
                    w0, w1, w2, bb = cw(f, gv)
                    act(cacc[:, gv, :], ugb[ub][:, gv, 1:OWN + 1], AF.Identity, [("ugb", ub, gv), "colsin"], [("cacc", gv)], scale=w1, bias=bb)
                    stt(cacc[:, gv, :], ugb[ub][:, gv, 0:OWN], w0, cacc[:, gv, :], ALU.mult, ALU.add, [("ugb", ub, gv), ("cacc", gv), "colsin"], [("cacc", gv)])
                    stt(cacc[:, gv, :], ugb[ub][:, gv, 2:OWN + 2], w2, cacc[:, gv, :], ALU.mult, ALU.add, [("ugb", ub, gv), ("cacc", gv), "colsin"], [("cacc", gv)])
                act(cacc[:, 0, :], cacc[:, 0, :], AF.Silu, [("cacc", 0)], [("cacc", 0)])
                tt("pool", aT[ub], cacc[:, 0, :], cacc[:, 1, :], ALU.mult, [("cacc", 0), ("cacc", 1)], [("aT", ub)])

            def ffn_down(f):
                wb_, ub = f % 3, f % 2
                for oc in range(8):
                    for tb in range(4):
                        t0 = CTX + tb * 512
                        pp = psnext()
                        mm(ps[pp][:, :], wd[wb_][:, oc * 128:(oc + 1) * 128], aT[ub][:, tb * 512:(tb + 1) * 512], True, True, [("wd", wb_), ("aT", ub)], [("ps", pp)])
                        xk = ("xo", tb, oc)
                        if oc % 2 == 0:
                            stt(xT[:, oc, t0:t0 + 512], ps[pp][:, :], C("g2s", l)[:, oc:oc + 1], xT[:, oc, t0:t0 + 512], ALU.mult, ALU.add,
                                [("ps", pp), xk, "cols"], [xk])
                        else:
                            fi = facc_i[0] % 4
                            facc_i[0] += 1
                            act(facc[:, fi, :], ps[pp][:, :], AF.Identity, [("ps", pp), "cols"], [("facc", fi)], scale=C("g2s", l)[:, oc:oc + 1])
                            tt("pool", xT[:, oc, t0:t0 + 512], xT[:, oc, t0:t0 + 512], facc[:, fi, :], ALU.add, [("facc", fi), xk], [xk])

            def ffn_ctx_up(f):
                wb_ = f % 3
                for gv in range(2):
                    pp = psnext()
                    for k in range(8):
                        mm(ps[pp][:, 0:CTX], wgv[wb_][:, k, gv, :], hc[:, k, 0:CTX], k == 0, k == 7, [("wgv", wb_), ("hc", 0)], [("ps", pp)])
                    act(ugc[:, gv, 1:CTX + 1], ps[pp][:, 0:CTX], AF.Identity, [("ps", pp)], ["ugc"])
                for gv in range(2):
                    w0, w1, w2, bb = cw(f, gv)
                    act(caccc[:, gv, :], ugc[:, gv, 1:CTX + 1], AF.Identity, ["ugc", "colsin"], [("caccc", gv)], scale=w1, bias=bb)
                    stt(caccc[:, gv, :], ugc[:, gv, 0:CTX], w0, caccc[:, gv, :], ALU.mult, ALU.add, ["ugc", ("caccc", gv), "colsin"], [("caccc", gv)])
                    stt(caccc[:, gv, :], ugc[:, gv, 2:CTX + 2], w2, caccc[:, gv, :], ALU.mult, ALU.add, ["ugc", ("caccc", gv), "colsin"], [("caccc", gv)])
                act(caccc[:, 0, :], caccc[:, 0, :], AF.Silu, [("caccc", 0)], [("caccc", 0)])
                tt("pool", aTc, caccc[:, 0, :], caccc[:, 1, :], ALU.mult, [("caccc", 0), ("caccc", 1)], ["aTc"])

            def ffn_ctx_down(f):
                wb_ = f % 3
                for oc in range(8):
                    pp = psnext()
                    mm(ps[pp][:, 0:CTX], wd[wb_][:, oc * 128:(oc + 1) * 128], aTc, True, True, [("wd", wb_), "aTc"], [("ps", pp)])
                    stt(xT[:, oc, 0:CTX], ps[pp][:, 0:CTX], C("g2s", l)[:, 8 + oc:8 + oc + 1], xT[:, oc, 0:CTX], ALU.mult, ALU.add,
                        [("ps", pp), ("xo", "c", oc), "cols"], [("xo", "c", oc)])

            ffn_load(0)
            ffn_load(1)
            ffn_up(0)
            ffn_conv(0)
            for f in range(NFC):
                if f + 2 < NFC:
                    ffn_load(f + 2)
                if f + 1 < NFC:
                    ffn_up(f + 1)
                if not last:
                    ffn_ctx_up(f)
                ffn_down(f)
                if not last:
                    ffn_ctx_down(f)
                if f + 1 < NFC:
                    ffn_conv(f + 1)
            S.barrier()
            for b in blks_res:
                s = 1 if b == 0 else 0
                if last:
                    emit_ln(b, l, "ln2_g", "ln2_b", None, None, s, True, None, LNT)
                else:
                    emit_ln(b, l, "ln2_g", "ln2_b", ("A1", l + 1), ("B1", l + 1), s, True, hc, LNT)
            if debug and l == 0:
                dma("sp", "dbgx2", dbg_x2[l], xT, [("x", b) for b in range(5)], ["dbgx2"])
            if l == n_layers - 1:
                for b in [1, 2, 3, 4]:
                    t0, n = BLKS[b]
                    dma("sp", "outs", out_d.rearrange("(k p) t -> p k t", p=128)[:, :, t0 - CTX:t0 - CTX + n], xT[:, :, t0:t0 + n], [("x", b)], ["out"])
            S.barrier()

        S.final_wait("sp")
        S.emit()
    return nc


def _mk_cols():
    cols = {}
    o = 0

    def add(name, w):
        nonlocal o
        cols[name] = (o, w)
        o += w

    add("eps", 4)
    add("mod", DEPTH * 6 * 16)
    for l in range(DEPTH):
        for nm in ("sc1p", "sc2p", "g1s", "g2s", "A1", "B1", "A2", "B2"):
            add((nm, l), 16)
        add(("lsum", l), 2)
        add(("neglam", l), 1)
        add(("gsub", l), 1)
    return cols, o


COLS, NCOLS = _mk_cols()


def _mk_colin():
    c = {}
    o = 0
    for nm, w in (("adab", 48), ("gq", 3), ("gkv", 2), ("subg", 1), ("ln1_g", 8), ("ln1_b", 8), ("ln2_g", 8), ("ln2_b", 8),
                  ("cw0", 44), ("cw1", 44), ("cw2", 44), ("cb", 44)):
        c[nm] = (o, w)
        o += w
    return c, o


COLIN, NCOLIN = _mk_colin()


def _rope_tables(core):
    r = core % 4
    tpos = np.arange(r * OWN, (r + 1) * OWN)
    rows = (tpos // 64).astype(np.float32)
    colsp = (tpos % 64).astype(np.float32)
    tabs = np.zeros((128, 4, T), np.float32)
    tabs[:, 0, :CTX] = 1.0
    tabs[:, 2, :CTX] = 1.0

    def fill(ci, si, width, nparts):
        half = width // 2
        n = half // 2
        inv = (10000.0 ** (-np.arange(n, dtype=np.float32) / n)).astype(np.float32)
        for p in range(nparts):
            j = p % width
            hf = j // half
            i = (j % half) % n
            pos = rows if hf == 0 else colsp
            ang = (pos * inv[i]).astype(np.float32)
            tabs[p, ci, CTX:] = np.cos(ang)
            tabs[p, si, CTX:] = np.sin(ang)

    fill(0, 1, 64, 64)
    fill(2, 3, 32, 128)
    return tabs


def _rot_mats():
    rm = np.zeros((128, 3, 128), np.float32)
    rm[:, 2, :] = np.eye(128, dtype=np.float32)
    for base in (0, 32):
        for i in range(16):
            rm[base + i + 16, 0, base + i] = -1.0
            rm[base + i, 0, base + i + 16] = 1.0
    for base in range(0, 128, 16):
        for i in range(8):
            rm[base + i + 8, 1, base + i] = -1.0
            rm[base + i, 1, base + i + 8] = 1.0
    return rm


def _colmajor(v, nchunk):
    return np.ascontiguousarray(np.asarray(v, np.float32).reshape(nchunk, 128).T)


def prep_inputs(inp):
    f = lambda a: np.ascontiguousarray(np.asarray(a, np.float32))
    L = DEPTH
    colsin = np.zeros((128, L, NCOLIN), np.float32)

    def put(name, l, arr):
        o, w = COLIN[name]
        colsin[:, l, o:o + w] = arr

    for l in range(L):
        put("adab", l, _colmajor(inp["ada_b"][l], 48))
        put("gq", l, _colmajor(inp["mla_gq"][l], 3))
        put("gkv", l, _colmajor(inp["mla_gkv"][l], 2))
        put("subg", l, np.tile(np.asarray(inp["diff_subln_g"][l], np.float32), 2)[:, None])
        put("ln1_g", l, _colmajor(inp["ln1_g"][l], 8))
        put("ln1_b", l, _colmajor(inp["ln1_b"][l], 8))
        put("ln2_g", l, _colmajor(inp["ln2_g"][l], 8))
        put("ln2_b", l, _colmajor(inp["ln2_b"][l], 8))
        for j in range(3):
            put(f"cw{j}", l, _colmajor(inp["ffn_convw"][l][j], 44))
        put("cb", l, _colmajor(inp["ffn_convb"][l], 44))
    wuq = f(inp["mla_wuq"]).reshape(L, 384, 4, 192)
    wuq_p = np.ascontiguousarray(np.concatenate([wuq[..., :128].reshape(L, 384, 512), wuq[..., 128:].reshape(L, 384, 256)], -1))
    wukv = f(inp["mla_wukv"]).reshape(L, 256, 4, 256)
    wukv_p = np.ascontiguousarray(np.concatenate([wukv[..., :128].reshape(L, 256, 512), wukv[..., 128:].reshape(L, 256, 512)], -1))
    ws = f(inp["sgu_ws"])
    wsT = np.ascontiguousarray(ws.transpose(0, 3, 1, 2))
    bs = f(inp["sgu_bs"])
    bsT = np.zeros((L, 128, 2, 128), np.float32)
    for g in range(4):
        bsT[:, (g % 2) * 64:(g % 2) * 64 + 64, g // 2, :] = bs[:, g, None, :]
    sgb = np.zeros((L, 128, 2, 256), np.float32)
    sgb[:, :, 0, :] = f(inp["sgu_ln_g"])[:, None, :]
    sgb[:, :, 1, :] = f(inp["sgu_ln_b"])[:, None, :]
    lamv = np.zeros((L, 128, 4, 32), np.float32)
    for i, nm in enumerate(("diff_lq1", "diff_lk1", "diff_lq2", "diff_lk2")):
        lamv[:, :, i, :] = f(inp[nm])[:, None, :]
    shared = {
        "rmat": _rot_mats(), "ada_w": f(inp["ada_w"]), "colsin": colsin, "w_in": f(inp["w_in"]), "wuq": wuq_p, "wukv": wukv_p,
        "w_o": f(inp["w_o"]), "wup": f(inp["ffn_wup"]), "wdown": f(inp["ffn_wdown"]), "sgu_wsT": wsT, "sgu_bsT": bsT,
        "sgu_gb": sgb, "lamv": lamv,
    }
    x = f(inp["x"])
    ctx = f(inp["ctx"])
    c = f(inp["c"])
    c_ctx = f(inp["c_ctx"])
    in_maps = []
    for core in range(8):
        b, r = core // 4, core % 4
        xloc = np.concatenate([ctx[b], x[b, r * OWN:(r + 1) * OWN]], 0)
        m = dict(shared)
        m["xT"] = np.ascontiguousarray(xloc.T)
        cc = np.zeros((128, 8, 2), np.float32)
        cc[:, :, 0] = _colmajor(c[b], 8)
        cc[:, :, 1] = _colmajor(c_ctx, 8)
        m["cc"] = cc
        m["tabs"] = _rope_tables(core)
        hm = np.zeros((128, 2, 4), np.float32)
        if r > 0:
            hm[:, 0, r - 1] = 1.0
        if r < 3:
            hm[:, 1, r + 1] = 1.0
        m["hmask"] = hm
        gm = np.zeros((128, 4), np.float32)
        for j in range(4):
            gm[j * 32:(j + 1) * 32, j] = 1.0
        m["gmask"] = gm
        in_maps.append(m)
    return in_maps


_NC_CACHE = {}


def kernel(**inputs):
    if "nc" not in _NC_CACHE:
        _NC_CACHE["nc"] = build_program()
    nc = _NC_CACHE["nc"]
    in_maps = prep_inputs(inputs)
    res = run_bass_kernel_spmd(nc, in_maps, core_ids=list(range(8)))
    out = np.zeros((2, SEQ, D), np.float32)
    for core in range(8):
        b, r = core // 4, core % 4
        out[b, r * OWN:(r + 1) * OWN, :] = res.results[core]["outT"].T
    return out
```

```python
import contextlib
import math
import numpy as np
import concourse.bass as bass
import concourse.mybir as mybir
from concourse.bass_utils import run_bass_kernel_spmd

F32 = mybir.dt.float32
BF16 = mybir.dt.bfloat16
AF = mybir.ActivationFunctionType
ALU = mybir.AluOpType
AX = mybir.AxisListType

D = 1024
DEPTH = 2
SEQ = 8192
CTX = 256
OWN = 2048
T = CTX + OWN
TH = T + 2
NKEY = CTX + SEQ
NKT = NKEY // 128
INW = 1984
DFF = 2816
NFC = 22
EPS = 1e-6
ALPHA = (2 * DEPTH) ** 0.25
MLA_SCALE = 192 ** -0.5
DIFF_SCALE = 32 ** -0.5
KVROWS = 1600
CH = 256
R_KN, R_VM, R_DK, R_KR, R_VD = 0, 512, 1024, 1280, 1344
BLKS = [(0, 256), (256, 512), (768, 512), (1280, 512), (1792, 512)]


class _Op:
    __slots__ = ("eng", "fn", "deps", "signal", "sigval", "dma_sem", "ndma", "inc1")

    def __init__(self, eng, fn, deps, dma_sem=None, ndma=1):
        self.eng = eng
        self.fn = fn
        self.deps = deps
        self.signal = dma_sem is not None
        self.sigval = None
        self.dma_sem = dma_sem
        self.ndma = ndma
        self.inc1 = False


class _Nop:
    def then_inc(self, *a, **k):
        return self


class Sched:
    ENGS = ("pe", "act", "dve", "pool", "sp")
    SAME_ENG_SYNC = ("act", "dve", "pool")

    def __init__(self, nc, stack):
        self.nc = nc
        self.stack = stack
        self.ops = {e: [] for e in self.ENGS}
        self.lastw = {}
        self.readers = {}
        self.dma_sems = {}
        self.dma_last = {}
        self.bar = []
        self.esem = {e: stack.enter_context(nc.semaphore(f"S_{e}")) for e in self.ENGS}

    def _deps(self, reads, writes):
        deps = [(b, "raw") for b in self.bar]
        for k in reads:
            w = self.lastw.get(k)
            if w is not None:
                deps.append((w, "raw"))
        for k in writes:
            w = self.lastw.get(k)
            if w is not None:
                deps.append((w, "waw"))
            last = {}
            for r in self.readers.get(k, ()):
                last[r.dma_sem if r.dma_sem is not None else ("e", r.eng)] = r
            for r in last.values():
                deps.append((r, "war"))
        return deps

    def _commit(self, op, reads, writes):
        for k in reads:
            self.readers.setdefault(k, []).append(op)
        for k in writes:
            self.lastw[k] = op
            self.readers[k] = []

    def op(self, eng, fn, reads=(), writes=()):
        o = _Op(eng, fn, self._deps(reads, writes))
        self.ops[eng].append(o)
        self._commit(o, reads, writes)
        return o

    def dma(self, queue, semname, fns, reads=(), writes=(), inc1=False):
        if not isinstance(fns, (list, tuple)):
            fns = [fns]
        if semname not in self.dma_sems:
            self.dma_sems[semname] = [self.stack.enter_context(self.nc.semaphore(f"D_{semname}")), 0]
        deps = self._deps(reads, writes)
        prev = self.dma_last.get(semname)
        if prev is not None:
            deps.append((prev, "waw"))
        o = _Op(queue, fns, deps, dma_sem=semname, ndma=len(fns))
        o.inc1 = inc1
        self.ops[queue].append(o)
        self.dma_last[semname] = o
        self._commit(o, reads, writes)
        return o

    def barrier(self):
        b = []
        for e in self.ENGS:
            if self.ops[e]:
                b.append(self.ops[e][-1])
        b.extend(self.dma_last.values())
        self.bar = b

    def final_wait(self, eng_name):
        o = _Op(eng_name, lambda eng: _Nop(), [(d, "raw") for d in self.dma_last.values()])
        self.ops[eng_name].append(o)

    def _skip(self, o, d, kind):
        if d.dma_sem is not None:
            return False
        if d.eng == o.eng and o.dma_sem is None:
            if kind == "war" or d.eng not in self.SAME_ENG_SYNC:
                return True
        return False

    def emit(self):
        nc = self.nc
        for e in self.ENGS:
            for o in self.ops[e]:
                for (d, kind) in o.deps:
                    if d.dma_sem is None and not self._skip(o, d, kind):
                        d.signal = True
        for e in self.ENGS:
            c = 0
            for o in self.ops[e]:
                if o.dma_sem is not None:
                    s = self.dma_sems[o.dma_sem]
                    s[1] += (1 if o.inc1 else 16) * o.ndma
                    o.sigval = s[1]
                    assert s[1] < 60000, ("dma sem overflow", o.dma_sem)
                elif o.signal:
                    c += 1
                    o.sigval = c
            assert c < 60000, ("engine sem overflow", e, c)

        def run(eng_name, eng):
            waited = {}
            for o in self.ops[eng_name]:
                need = {}
                for (d, kind) in o.deps:
                    if self._skip(o, d, kind):
                        continue
                    if d.dma_sem is not None:
                        key = ("d", d.dma_sem)
                        sem = self.dma_sems[d.dma_sem][0]
                    else:
                        key = ("e", d.eng)
                        sem = self.esem[d.eng]
                    v = d.sigval
                    if waited.get(key, 0) >= v:
                        continue
                    if key not in need or need[key][1] < v:
                        need[key] = (sem, v)
                for key, (sem, v) in need.items():
                    eng.wait_ge(sem, v)
                    waited[key] = v
                if o.dma_sem is not None:
                    sem = self.dma_sems[o.dma_sem][0]
                    for f in o.fn:
                        f(eng).then_inc(sem, 1 if o.inc1 else 16)
                else:
                    ins = o.fn(eng)
                    if o.signal:
                        ins.then_inc(self.esem[eng_name], 1)

        with nc.Block() as block:
            @block.tensor
            def _(e):
                run("pe", e)

            @block.scalar
            def _(e):
                run("act", e)

            @block.vector
            def _(e):
                run("dve", e)

            @block.gpsimd
            def _(e):
                run("pool", e)

            @block.sync
            def _(e):
                run("sp", e)


import os
NOCC = bool(int(os.environ.get("K_NOCC", "0")))
PARAMS = ["ada_w", "adab", "w_in", "wuq", "wukv", "w_o", "wup", "wdown"]


def build_program(n_layers=DEPTH, debug=False):
    nc = bass.Bass("TRN2", target_bir_lowering=False)
    dram = {}

    def din(name, shape, dt=F32):
        dram[name] = nc.dram_tensor(name, list(shape), dt, kind="ExternalInput").ap()
        return dram[name]

    def dint(name, shape, dt=BF16, kind="Internal"):
        dram[name] = nc.dram_tensor(name, list(shape), dt, kind=kind).ap()
        return dram[name]

    xT_d = din("xT", [D, T])
    cc_d = din("cc", [128, 8, 2])
    tabs_d = din("tabs", [128, 4, T])
    rmat_d = din("rmat", [128, 3, 128])
    hmask_d = din("hmask", [128, 2, 4])
    gmask_d = din("gmask", [128, 4])
    ada_w_d = din("ada_w", [DEPTH, D, 6 * D])
    colsin_d = din("colsin", [128, DEPTH, NCOLIN])
    w_in_d = din("w_in", [DEPTH, D, INW])
    wuq_d = din("wuq", [DEPTH, 384, 768])
    wukv_d = din("wukv", [DEPTH, 256, 1024])
    w_o_d = din("w_o", [DEPTH, D, D])
    wup_d = din("wup", [DEPTH, D, 2 * DFF])
    wdown_d = din("wdown", [DEPTH, DFF, D])
    sgu_ws_d = din("sgu_wsT", [DEPTH, 128, 4, 128])
    sgu_bsT_d = din("sgu_bsT", [DEPTH, 128, 2, 128])
    sgu_gb_d = din("sgu_gb", [DEPTH, 128, 2, 256])
    lamv_d = din("lamv", [DEPTH, 128, 4, 32])
    okind = "ExternalOutput"
    out_d = dint("outT", [D, OWN], F32, kind=okind)

    dbg = {}
    ikind = "Internal"
    xsp_d = [dint(f"xsp{l}", [128, 8, T], F32) for l in range(n_layers)]
    kvsrc_d = [dint(f"kvsrc{l}", [KVROWS, OWN], BF16) for l in range(n_layers)]
    kvall_d = [dint(f"kvall{l}", [4 * KVROWS, OWN], BF16) for l in range(n_layers)]
    kvctx_d = [dint(f"kvctx{l}", [KVROWS, CTX], BF16, kind=ikind) for l in range(n_layers)]
    qn_d = [dint(f"qn{l}", [512, T], BF16, kind=ikind) for l in range(n_layers)]
    qr_d = [dint(f"qr{l}", [256, T], BF16, kind=ikind) for l in range(n_layers)]
    dq_d = [dint(f"dq{l}", [8 * 128, T], BF16, kind=ikind) for l in range(n_layers)]
    hsrc_d = [dint(f"hsrc{l}", [128, 16], BF16) for l in range(n_layers)]
    hall_d = [dint(f"hall{l}", [4 * 128, 16], BF16) for l in range(n_layers)]
    if debug:
        dbg_cc = [dint(f"dbg_cc{l}", [128, 8, TH], BF16, kind="ExternalOutput") for l in range(1)]
        dbg_x1 = [dint(f"dbg_x1{l}", [128, 8, T], F32, kind="ExternalOutput") for l in range(1)]
        dbg_x2 = [dint(f"dbg_x2{l}", [128, 8, T], F32, kind="ExternalOutput") for l in range(1)]
        dbg_cols = dint("dbg_cols", [128, NCOLS], F32, kind="ExternalOutput")

    with contextlib.ExitStack() as st:
        S = Sched(nc, st)

        def sb(name, shape, dt):
            return st.enter_context(nc.sbuf_tensor(name, list(shape), dt))

        ARW = 38920
        AR = sb("AR", [128, ARW], F32)
        hc = sb("hc", [128, 8, TH], BF16)
        cm = sb("cm", [128, 2, T], BF16)
        cols = sb("cols", [128, NCOLS], F32)
        colsin = sb("colsin_sb", [128, DEPTH, NCOLIN], F32)
        onesb = sb("onesb", [128, 128], BF16)
        onesf = sb("onesf", [128, 128], F32)
        rmat = sb("rmat_sb", [128, 3, 128], F32)
        hmask = sb("hmask_sb", [128, 2, 4], F32)
        gmask = sb("gmask_sb", [128, 4], F32)
        scs_t = sb("scs_t", [128, 8, 2], F32)
        ps = [st.enter_context(nc.psum_tensor(f"ps{i}", [128, 512], F32)) for i in range(8)]

        def carve(off_bytes, shape, dt):
            n = int(np.prod(shape[1:]))
            nb = n * (4 if dt == F32 else 2)
            assert off_bytes % 4 == 0 and nb % 4 == 0
            assert off_bytes + nb <= ARW * 4, (off_bytes, nb)
            v = AR[:, off_bytes // 4:(off_bytes + nb) // 4]
            if dt != F32:
                v = v.bitcast(dt)
            if len(shape) == 3:
                v = v.rearrange("p (a b) -> p a b", a=shape[1])
            elif len(shape) == 4:
                v = v.rearrange("p (a b c) -> p a b c", a=shape[1], b=shape[2])
            elif len(shape) == 5:
                v = v.rearrange("p (a b c d) -> p a b c d", a=shape[1], b=shape[2], c=shape[3])
            return v

        xT = carve(0, [128, 8, T], F32)

        def mm(out, lhsT, rhs, start, stop, reads, writes):
            return S.op("pe", lambda e: e.matmul(out, lhsT=lhsT, rhs=rhs, start=start, stop=stop), reads=reads, writes=writes)

        def act(out, in_, func, reads, writes, scale=1.0, bias=None):
            if bias is None:
                return S.op("act", lambda e: e.activation(out=out, in_=in_, func=func, scale=scale), reads=reads, writes=writes)
            return S.op("act", lambda e: e.activation(out=out, in_=in_, func=func, bias=bias, scale=scale), reads=reads, writes=writes)

        def tt(eng, out, in0, in1, op, reads, writes):
            return S.op(eng, lambda e: e.tensor_tensor(out=out, in0=in0, in1=in1, op=op), reads=reads, writes=writes)

        def ts(eng, out, in0, s1, s2, op0, op1, reads, writes):
            if s2 is None:
                return S.op(eng, lambda e: e.tensor_scalar(out=out, in0=in0, scalar1=s1, scalar2=None, op0=op0), reads=reads, writes=writes)
            return S.op(eng, lambda e: e.tensor_scalar(out=out, in0=in0, scalar1=s1, scalar2=s2, op0=op0, op1=op1), reads=reads, writes=writes)

        def stt(out, in0, scalar, in1, op0, op1, reads, writes):
            return S.op("dve", lambda e: e.scalar_tensor_tensor(out=out, in0=in0, scalar=scalar, in1=in1, op0=op0, op1=op1), reads=reads, writes=writes)

        def cp(eng, out, in_, reads, writes):
            return S.op(eng, lambda e: e.tensor_copy(out=out, in_=in_), reads=reads, writes=writes)

        def recip(out, in_, reads, writes):
            return S.op("dve", lambda e: e.reciprocal(out=out, in_=in_), reads=reads, writes=writes)

        def dma(q, sem, out, in_, reads, writes):
            return S.dma(q, sem, lambda e: e.dma_start(out=out, in_=in_), reads=reads, writes=writes)

        psrr = [0]

        def psnext():
            i = psrr[0] % 8
            psrr[0] += 1
            return i

        def C(name, l=None):
            o, w = COLS[name if l is None else (name, l)]
            return cols[:, o:o + w]

        def CI(name, l):
            o, w = COLIN[name]
            return colsin[:, l, o:o + w]

        dma("sp", "c0", colsin[:], colsin_d, [], ["colsin"])
        dma("sp", "c1", rmat[:], rmat_d, [], ["rmat"])
        dma("sp", "c2", hmask[:], hmask_d, [], ["hmask"])
        dma("sp", "c2", gmask[:], gmask_d, [], ["gmask"])
        S.op("pool", lambda e: e.memset(onesb[:], 1.0), writes=["onesb"])
        S.op("pool", lambda e: e.memset(onesf[:], 1.0), writes=["onesf"])
        S.op("pool", lambda e: e.memset(cols[:], 0.0), writes=["cols"])
        for b, (t0, n) in enumerate(BLKS):
            dma("sp", "xin", xT[:, :, t0:t0 + n], xT_d.rearrange("(k p) t -> p k t", p=128)[:, :, t0:t0 + n], [], [("x", b)])
        o_eps, _ = COLS["eps"]
        S.op("dve", lambda e: e.memset(cols[:, o_eps:o_eps + 1], EPS / (ALPHA * ALPHA)), reads=["cols"], writes=["cols"])
        S.op("dve", lambda e: e.memset(cols[:, o_eps + 1:o_eps + 2], EPS), reads=["cols"], writes=["cols"])
        S.op("dve", lambda e: e.memset(cols[:, o_eps + 2:o_eps + 3], 4.0 * EPS), reads=["cols"], writes=["cols"])
        EPS_LN, EPS_RMS, EPS_G = cols[:, o_eps:o_eps + 1], cols[:, o_eps + 1:o_eps + 2], cols[:, o_eps + 2:o_eps + 3]

        ccs = carve(73728, [128, 8, 2], F32)
        dma("sp", "c3", ccs, cc_d, [], ["ccs"])
        act(scs_t[:], ccs, AF.Silu, ["ccs"], ["scs"])
        o_mod, _ = COLS["mod"]
        ada_it = [0]

        def MOD(l, j, s):
            o = o_mod + (l * 6 + j) * 16 + s * 8
            return cols[:, o:o + 8]

        def ada_bufs(base):
            return [carve(base + i * 32768, [128, 8, 1024], F32) for i in range(2)], carve(base + 2 * 32768, [128, 1024], F32)

        def ada_load(l, j, buf, key, slot):
            dma("sp", f"aw{slot}", buf, ada_w_d[l].rearrange("(k p) c -> p k c", p=128)[:, :, j * 1024:(j + 1) * 1024], [], [key])

        def ada_compute(l, j, buf, key, modrow, mkey, bank=None):
            ident = rmat[0:2, 2, 0:2]
            for half in range(2):
                pr = psnext() if bank is None else bank
                for kk in range(8):
                    mm(ps[pr][0:2, 0:512], scs_t[:, kk, :], buf[:, kk, half * 512:(half + 1) * 512], kk == 0, kk == 7, [key, "scs"], [("ps", pr)])
                cp("dve", modrow[0:2, half * 512:(half + 1) * 512], ps[pr][0:2, 0:512], [("ps", pr)], [mkey])
            pi = psnext() if bank is None else bank
            for ko in range(8):
                mm(ps[pi][:, ko * 2:ko * 2 + 2], modrow[0:2, ko * 128:(ko + 1) * 128], ident, True, True, [mkey, "rmat"], [("ps", pi)])
            o = o_mod + (l * 6 + j) * 16
            for s in range(2):
                tt("dve", cols[:, o + s * 8:o + s * 8 + 8], ps[pi][:, 0:16].rearrange("p (k s) -> p k s", s=2)[:, :, s],
                   CI("adab", l)[:, j * 8:(j + 1) * 8], ALU.add, [("ps", pi), "colsin"], ["cols"])

        def emit_ada(l):
            awb, modrow = ada_bufs(74240)
            for j in range(6):
                it = ada_it[0]
                ada_load(l, j, awb[it % 2], ("awb", it % 2), it % 2)
                ada_compute(l, j, awb[it % 2], ("awb", it % 2), modrow, "modrow")
                ada_it[0] += 1
            ada_finish(l, 73728 + 256)

        def ada_finish(l, lbase):
            for s in range(2):
                sl = slice(s * 8, s * 8 + 8)
                ts("dve", C("sc1p", l)[:, sl], MOD(l, 1, s), 1.0, None, ALU.add, None, ["cols"], ["cols"])
                ts("dve", C("sc2p", l)[:, sl], MOD(l, 4, s), 1.0, None, ALU.add, None, ["cols"], ["cols"])
                ts("dve", C("g1s", l)[:, sl], MOD(l, 2, s), 1.0 / ALPHA, None, ALU.mult, None, ["cols"], ["cols"])
                ts("dve", C("g2s", l)[:, sl], MOD(l, 5, s), 1.0 / ALPHA, None, ALU.mult, None, ["cols"], ["cols"])
                if l == 0:
                    cp("dve", C("A1", l)[:, sl], C("sc1p", l)[:, sl], ["cols"], ["cols"])
                    cp("dve", C("B1", l)[:, sl], MOD(l, 0, s), ["cols"], ["cols"])
                else:
                    tt("dve", C("A1", l)[:, sl], CI("ln2_g", l - 1), C("sc1p", l)[:, sl], ALU.mult, ["cols", "colsin"], ["cols"])
                    tt("dve", C("B1", l)[:, sl], CI("ln2_b", l - 1), C("sc1p", l)[:, sl], ALU.mult, ["cols", "colsin"], ["cols"])
                    tt("dve", C("B1", l)[:, sl], C("B1", l)[:, sl], MOD(l, 0, s), ALU.add, ["cols"], ["cols"])
                tt("dve", C("A2", l)[:, sl], CI("ln1_g", l), C("sc2p", l)[:, sl], ALU.mult, ["cols", "colsin"], ["cols"])
                tt("dve", C("B2", l)[:, sl], CI("ln1_b", l), C("sc2p", l)[:, sl], ALU.mult, ["cols", "colsin"], ["cols"])
                tt("dve", C("B2", l)[:, sl], C("B2", l)[:, sl], MOD(l, 3, s), ALU.add, ["cols"], ["cols"])
            lv = carve(lbase, [128, 4, 32], F32)
            lt = carve(lbase + 512, [128, 2, 32], F32)
            dma("sp", "c4", lv, lamv_d[l], [], ["lv"])
            tt("dve", lt[:, 0, :], lv[:, 0, :], lv[:, 1, :], ALU.mult, ["lv"], ["lt"])
            tt("dve", lt[:, 1, :], lv[:, 2, :], lv[:, 3, :], ALU.mult, ["lv"], ["lt"])
            S.op("dve", lambda e, l=l, lt=lt: e.reduce_sum(out=C("lsum", l), in_=lt, axis=AX.X), reads=["lt"], writes=["cols"])
            act(C("lsum", l), C("lsum", l), AF.Exp, ["cols"], ["cols"])
            lam_init = 0.8 - 0.6 * math.exp(-0.3 * l)
            tt("dve", C("neglam", l), C("lsum", l)[:, 1:2], C("lsum", l)[:, 0:1], ALU.subtract, ["cols"], ["cols"])
            ts("dve", C("neglam", l), C("neglam", l), -lam_init, None, ALU.add, None, ["cols"], ["cols"])
            ts("dve", C("gsub", l), CI("subg", l), 1.0 - lam_init, None, ALU.mult, None, ["colsin"], ["cols"])

        emit_ada(0)
        for l_ in range(2, n_layers):
            emit_ada(l_)
        if debug:
            dma("sp", "dbgc", dbg_cols, cols[:], ["cols"], ["dbgcols"])

        LNT = 73728 + 16384

        def emit_ln(b, l, gname, bname, Aname, Bname, s, write_x, hdst, lnbase):
            t0, n = BLKS[b]
            sqb = [carve(lnbase + i * 2048, [128, 512], F32) for i in range(2)]
            mean = carve(lnbase + 4096, [128, 512], F32)
            m2 = carve(lnbase + 6144, [128, 512], F32)
            rstd = carve(lnbase + 8192, [128, 512], F32)
            xh = [carve(lnbase + 10240 + i * 2048, [128, 512], F32) for i in range(2)]
            p1, p2 = psnext(), psnext()
            for k in range(8):
                u = xT[:, k, t0:t0 + n]
                sq = sqb[k % 2]
                xk = ("xk", b, k)
                tt("pool", sq[:, :n], u, u, ALU.mult, [("x", b), xk], [("lnsq", k % 2)])
                mm(ps[p1][:, :n], onesf[:], u, k == 0, k == 7, [("x", b), xk, "onesf"], [("ps", p1)])
                mm(ps[p2][:, :n], onesf[:], sq[:, :n], k == 0, k == 7, [("lnsq", k % 2), "onesf"], [("ps", p2)])
            act(mean[:, :n], ps[p1][:, :n], AF.Identity, [("ps", p1)], ["lnmean"], scale=1.0 / D)
            tt("pool", m2[:, :n], mean[:, :n], mean[:, :n], ALU.mult, ["lnmean"], ["lnm2"])
            stt(rstd[:, :n], ps[p2][:, :n], 1.0 / D, m2[:, :n], ALU.mult, ALU.subtract, [("ps", p2), "lnm2"], ["lnrstd"])
            act(rstd[:, :n], rstd[:, :n], AF.Sqrt, ["lnrstd"], ["lnrstd"], bias=EPS_LN)
            recip(rstd[:, :n], rstd[:, :n], ["lnrstd"], ["lnrstd"])
            for k in range(8):
                u = xT[:, k, t0:t0 + n]
                x_ = xh[k % 2]
                xk = ("xk", b, k)
                tt("pool", x_[:, :n], u, mean[:, :n], ALU.subtract, [("x", b), xk, "lnmean"], [("lnxh", k % 2)])
                tt("dve", x_[:, :n], x_[:, :n], rstd[:, :n], ALU.mult, [("lnxh", k % 2), "lnrstd"], [("lnxh", k % 2)])
                if write_x:
                    act(u, x_[:, :n], AF.Identity, [("lnxh", k % 2), "colsin"], [xk],
                        scale=CI(gname, l)[:, k:k + 1], bias=CI(bname, l)[:, k:k + 1])
                if hdst is not None:
                    act(hdst[:, k, t0:t0 + n], x_[:, :n], AF.Identity, [("lnxh", k % 2), "cols"], [("hc", b)],
                        scale=C(Aname[0], Aname[1])[:, s * 8 + k:s * 8 + k + 1], bias=C(Bname[0], Bname[1])[:, s * 8 + k:s * 8 + k + 1])

        for l in range(n_layers):
            last = (l == DEPTH - 1)
            blks_res = [1, 2, 3, 4] if last else [0, 1, 2, 3, 4]
            if l == 0:
                for b, (t0, n) in enumerate(BLKS):
                    s = 1 if b == 0 else 0
                    for k in range(8):
                        act(hc[:, k, t0:t0 + n], xT[:, k, t0:t0 + n], AF.Identity, [("x", b), "cols"], [("hc", b)],
                            scale=C("A1", l)[:, s * 8 + k:s * 8 + k + 1], bias=C("B1", l)[:, s * 8 + k:s * 8 + k + 1])
            for b in blks_res:
                t0, n = BLKS[b]
                dma("sp", "xsp", xsp_d[l][:, :, t0:t0 + n], xT[:, :, t0:t0 + n], [("x", b)], [("xsp", b)])
            if debug and False:
                dma("sp", "dbgh", dbg_h[l], hc[:], [("hc", b) for b in range(5)], ["dbgh"])
            S.barrier()

            win = carve(0, [128, 8, INW], BF16)
            wuq = carve(31744, [128, 3, 768], BF16)
            wukv = carve(36352, [128, 2, 1024], BF16)
            tabs = carve(40448, [128, 4, 512], F32)
            qlat = carve(48640, [128, 3, 512], F32)
            kvlat = carve(54784, [128, 2, 512], F32)
            sqq = carve(58880, [128, 2, 512], F32)
            rstdq = carve(62976, [128, 2, 512], F32)
            qn = carve(67072, [128, 3, 512], BF16)
            kvn = carve(70144, [128, 2, 512], BF16)
            rxs = carve(72192, [128, 2, 512], F32)
            rt1 = carve(76288, [128, 2, 512], F32)
            rt2 = carve(80384, [128, 2, 512], F32)
            stQn = carve(84480, [128, 4, 512], BF16)
            stQr = carve(88576, [128, 4, 512], BF16)
            stdQ = carve(92672, [128, 2, 512], BF16)
            stKn = carve(94720, [128, 4, 512], BF16)
            stKr = carve(98816, [128, 512], BF16)
            stdK = carve(99840, [128, 2, 512], BF16)
            stVm = carve(101888, [128, 4, 512], BF16)
            stVd = carve(105984, [128, 4, 256], BF16)
            ug = carve(108032, [128, 2, 512], F32)
            gtm = carve(112128, [128, 2, 512], F32)
            vt = carve(116224, [128, 2, 256], F32)
            vt2 = carve(118272, [128, 2, 256], F32)
            vbf = carve(120320, [128, 2, 256], BF16)
            bnst = carve(121344, [128, 2, 8], F32)
            bnag = carve(121408, [128, 2, 2], F32)
            tmpc = carve(121424, [128, 2, 128], F32)
            wsT = carve(122448, [128, 4, 128], BF16)
            bsT = carve(123472, [128, 2, 128], F32)
            sgb = carve(124496, [128, 2, 256], F32)
            stM = carve(126976, [128, 8, 512], BF16)
            rxs5 = carve(135168, [128, 5, 512], F32)
            sqq5 = carve(145408, [128, 5, 512], F32)
            vbf4 = carve(58880, [128, 4, 256], BF16)

            for kk in range(8):
                dma("pool", f"wl{kk}", win[:, kk, :], w_in_d[l, kk * 128:(kk + 1) * 128, :], [], [("win", kk)])
            dma("pool", "wld", wuq, wuq_d[l].rearrange("(k p) c -> p k c", p=128), [], ["wuq"])
            dma("pool", "wld", wukv, wukv_d[l].rearrange("(k p) c -> p k c", p=128), [], ["wukv"])
            dma("pool", "wld", wsT, sgu_ws_d[l], [], ["wsT"])
            dma("sp", "c5", bsT, sgu_bsT_d[l], [], ["bsT"])
            dma("sp", "c6", sgb, sgu_gb_d[l], [], ["sgb"])

            RM = rmat[0:64, 0, 0:64]
            RD = rmat[:, 1, :]

            def rope(pz, P, n, scale, cos, sin, R, dst, dkeys, wkeys, i):
                xs_, t1_, t2_ = rxs[0:P, i % 2, :n], rt1[0:P, i % 2, :n], rt2[0:P, i % 2, :n]
                act(xs_, pz, AF.Identity, dkeys, [("rxs", i % 2)], scale=scale)
                pr = psnext()
                mm(ps[pr][0:P, :n], R, xs_, True, True, [("rxs", i % 2), "rmat"], [("ps", pr)])
                tt("pool", t1_, xs_, cos, ALU.mult, [("rxs", i % 2), "tabs"], [("rt1", i % 2)])
                tt("dve", t2_, ps[pr][0:P, :n], sin, ALU.mult, [("ps", pr), "tabs"], [("rt2", i % 2)])
                tt("pool", dst, t1_, t2_, ALU.add, [("rt1", i % 2), ("rt2", i % 2)], wkeys)

            ri = [0]
            for b, (t0, n) in enumerate(BLKS):
                hk = ("hc", b)
                ntile = n // 128
                dma("sp", "tabs", tabs[:, :, :n], tabs_d[:, :, t0:t0 + n], [], ["tabs"])
                cosM, sinM, cosD, sinD = tabs[0:64, 0, :n], tabs[0:64, 1, :n], tabs[:, 2, :n], tabs[:, 3, :n]

                def proj(col0, M, K_src="h"):
                    pi = psnext()
                    for k in range(8):
                        mm(ps[pi][0:M, :n], win[:, k, col0:col0 + M], hc[:, k, t0:t0 + n], k == 0, k == 7, [("win", k), hk], [("ps", pi)])
                    return pi

                def rms(src, nch, width, gname, dstn, tag, ti):
                    pi = psnext()
                    for c in range(nch):
                        tt("pool", sqq[:, c % 2, :n], src[:, c, :n], src[:, c, :n], ALU.mult, [tag], [("sqq", c % 2)])
                        mm(ps[pi][:, :n], onesf[:], sqq[:, c % 2, :n], c == 0, c == nch - 1, [("sqq", c % 2), "onesf"], [("ps", pi)])
                    r_ = rstdq[:, ti, :n]
                    act(r_, ps[pi][:, :n], AF.Sqrt, [("ps", pi)], [("rstdq", ti)], scale=1.0 / width, bias=EPS_RMS)
                    recip(r_, r_, [("rstdq", ti)], [("rstdq", ti)])
                    for c in range(nch):
                        stt(dstn[:, c, :n], src[:, c, :n], CI(gname, l)[:, c:c + 1], r_, ALU.mult, ALU.mult,
                            [tag, ("rstdq", ti), "colsin"], [tag + "n"])

                for c in range(3):
                    pi = proj(c * 128, 128)
                    act(qlat[:, c, :n], ps[pi][:, :n], AF.Identity, [("ps", pi)], [("qlat", c)])
                    tt("pool", sqq5[:, c, :n], qlat[:, c, :n], qlat[:, c, :n], ALU.mult, [("qlat", c)], [("sqq5", c)])
                for c in range(2):
                    pi = proj(384 + c * 128, 128)
                    act(kvlat[:, c, :n], ps[pi][:, :n], AF.Identity, [("ps", pi)], [("kvlat", c)])
                    tt("pool", sqq5[:, 3 + c, :n], kvlat[:, c, :n], kvlat[:, c, :n], ALU.mult, [("kvlat", c)], [("sqq5", 3 + c)])
                rspec = [(640, 64, 1.0), (704, 128, DIFF_SCALE), (832, 128, DIFF_SCALE), (960, 128, 1.0), (1088, 128, 1.0)]
                for idx, (col0, P_, sc_) in enumerate(rspec):
                    pi = proj(col0, P_)
                    act(rxs5[0:P_, idx, :n], ps[pi][0:P_, :n], AF.Identity, [("ps", pi)], [("rxs5", idx)], scale=sc_)
                for t in range(ntile):
                    pi = psnext()
                    for k in range(8):
                        mm(ps[pi][:, 0:256], hc[:, k, t0 + t * 128:t0 + (t + 1) * 128], win[:, k, 1216:1472], k == 0, k == 7, [("win", k), hk], [("ps", pi)])
                    cp("dve", stVd[:, t, :], ps[pi][:, 0:256], [("ps", pi)], ["stVd"])
                for c in range(2):
                    pi = proj(1472 + c * 128, 128)
                    xs_ = gtm[:, 0, :n]
                    t_ = gtm[:, 1, :n]
                    act(xs_, ps[pi][:, :n], AF.Identity, [("ps", pi)], ["gxs"])
                    tt("pool", t_, xs_, xs_, ALU.mult, ["gxs"], ["gt"])
                    ts("pool", t_, t_, 0.044715, 1.0, ALU.mult, ALU.add, ["gt"], ["gt"])
                    tt("pool", t_, t_, xs_, ALU.mult, ["gt", "gxs"], ["gt"])
                    act(t_, t_, AF.Tanh, ["gt"], ["gt"], scale=0.7978845608028654)
                    stt(ug[:, c, :n], t_, 1.0, xs_, ALU.add, ALU.mult, ["gt", "gxs"], [("ug", c)])
                for t in range(ntile):
                    vi = t % 2
                    pi = psnext()
                    for k in range(8):
                        mm(ps[pi][:, 0:256], hc[:, k, t0 + t * 128:t0 + (t + 1) * 128], win[:, k, 1728:1984], k == 0, k == 7, [("win", k), hk], [("ps", pi)])
                    xs_, t_ = vt[:, vi, :], vt2[:, vi, :]
                    act(xs_, ps[pi][:, 0:256], AF.Identity, [("ps", pi)], [("vt", vi)])
                    tt("pool", t_, xs_, xs_, ALU.mult, [("vt", vi)], [("vt2", vi)])
                    ts("pool", t_, t_, 0.044715, 1.0, ALU.mult, ALU.add, [("vt2", vi)], [("vt2", vi)])
                    tt("pool", t_, t_, xs_, ALU.mult, [("vt2", vi), ("vt", vi)], [("vt2", vi)])
                    act(t_, t_, AF.Tanh, [("vt2", vi)], [("vt2", vi)], scale=0.7978845608028654)
                    stt(xs_, t_, 1.0, xs_, ALU.add, ALU.mult, [("vt2", vi), ("vt", vi)], [("vt", vi)])
                    S.op("dve", lambda e, o=bnst[:, vi, 0:6], i_=xs_: e.bn_stats(out=o, in_=i_), reads=[("vt", vi)], writes=[("bnst", vi)])
                    S.op("dve", lambda e, o=bnag[:, vi, :], i_=bnst[:, vi, 0:6]: e.bn_aggr(out=o, in_=i_), reads=[("bnst", vi)], writes=[("bnag", vi)])
                    act(bnag[:, vi, 1:2], bnag[:, vi, 1:2], AF.Sqrt, [("bnag", vi)], [("bnag", vi)], bias=EPS_G)
                    recip(bnag[:, vi, 1:2], bnag[:, vi, 1:2], [("bnag", vi)], [("bnag", vi)])
                    ts("dve", xs_, xs_, bnag[:, vi, 0:1], bnag[:, vi, 1:2], ALU.subtract, ALU.mult, [("vt", vi), ("bnag", vi)], [("vt", vi)])
                    tt("pool", xs_, xs_, sgb[:, 0, :], ALU.mult, [("vt", vi), "sgb"], [("vt", vi)])
                    tt("pool", vbf4[:, t, :], xs_, sgb[:, 1, :], ALU.add, [("vt", vi), "sgb"], [("vbf4", t)])

                def rms_b(src, nch, sq0, width, gname, dstn, tag, ti):
                    pi = psnext()
                    for c in range(nch):
                        mm(ps[pi][:, :n], onesf[:], sqq5[:, sq0 + c, :n], c == 0, c == nch - 1, [("sqq5", sq0 + c), "onesf"], [("ps", pi)])
                    r_ = rstdq[:, ti, :n]
                    act(r_, ps[pi][:, :n], AF.Sqrt, [("ps", pi)], [("rstdq", ti)], scale=1.0 / width, bias=EPS_RMS)
                    recip(r_, r_, [("rstdq", ti)], [("rstdq", ti)])
                    for c in range(nch):
                        stt(dstn[:, c, :n], src[:, c, :n], CI(gname, l)[:, c:c + 1], r_, ALU.mult, ALU.mult,
                            [(tag, c), ("rstdq", ti), "colsin"], [tag + "n"])

                rms_b(qlat, 3, 0, 384, "gq", qn, "qlat", 0)
                rms_b(kvlat, 2, 3, 256, "gkv", kvn, "kvlat", 1)
                rdst = [(64, cosM, sinM, RM, stKr[0:64, :n], "stKr"), (128, cosD, sinD, RD, stdQ[:, 0, :n], "stdQ"),
                        (128, cosD, sinD, RD, stdQ[:, 1, :n], "stdQ"), (128, cosD, sinD, RD, stdK[:, 0, :n], "stdK"),
                        (128, cosD, sinD, RD, stdK[:, 1, :n], "stdK")]
                for idx, (P_, cos_, sin_, R_, dst_, wk_) in enumerate(rdst):
                    i_ = ri[0]
                    ri[0] += 1
                    xs_, t1_, t2_ = rxs5[0:P_, idx, :n], rt1[0:P_, i_ % 2, :n], rt2[0:P_, i_ % 2, :n]
                    pr = psnext()
                    mm(ps[pr][0:P_, :n], R_, xs_, True, True, [("rxs5", idx), "rmat"], [("ps", pr)])
                    tt("pool", t1_, xs_, cos_, ALU.mult, [("rxs5", idx), "tabs"], [("rt1", i_ % 2)])
                    tt("dve", t2_, ps[pr][0:P_, :n], sin_, ALU.mult, [("ps", pr), "tabs"], [("rt2", i_ % 2)])
                    tt("pool", dst_, t1_, t2_, ALU.add, [("rt1", i_ % 2), ("rt2", i_ % 2)], [wk_])
                    if 1 <= idx <= 2:
                        c = idx - 1
                        for j in range(4):
                            ts("dve", stM[:, c * 4 + j, :n], stdQ[:, c, :n], gmask[:, j:j + 1], None, ALU.mult, None, ["stdQ", "gmask"], ["stM"])
                for h in range(4):
                    pi = psnext()
                    for c in range(3):
                        mm(ps[pi][:, :n], wuq[:, c, h * 128:(h + 1) * 128], qn[:, c, :n], c == 0, c == 2, ["wuq", "qlatn"], [("ps", pi)])
                    act(stQn[:, h, :n], ps[pi][:, :n], AF.Identity, [("ps", pi)], ["stQn"], scale=MLA_SCALE)
                for h in range(4):
                    pi = psnext()
                    for c in range(3):
                        mm(ps[pi][0:64, :n], wuq[:, c, 512 + h * 64:512 + (h + 1) * 64], qn[:, c, :n], c == 0, c == 2, ["wuq", "qlatn"], [("ps", pi)])
                    rope(ps[pi][0:64, :n], 64, n, MLA_SCALE, cosM, sinM, RM, stQr[0:64, h, :n], [("ps", pi)], ["stQr"], ri[0])
                    ri[0] += 1
                for h in range(4):
                    pi = psnext()
                    for c in range(2):
                        mm(ps[pi][:, :n], wukv[:, c, h * 128:(h + 1) * 128], kvn[:, c, :n], c == 0, c == 1, ["wukv", "kvlatn"], [("ps", pi)])
                    cp("dve", stKn[:, h, :n], ps[pi][:, :n], [("ps", pi)], ["stKn"])
                for t in range(ntile):
                    pi = psnext()
                    for c in range(2):
                        mm(ps[pi][:, :], kvn[:, c, t * 128:(t + 1) * 128], wukv[:, c, 512:1024], c == 0, c == 1, ["wukv", "kvlatn"], [("ps", pi)])
                    act(stVm[:, t, :], ps[pi][:, :], AF.Identity, [("ps", pi)], ["stVm"])
                for t in range(ntile):
                    for c in range(2):
                        pm = psnext()
                        for gg in range(2):
                            g = 2 * c + gg
                            mm(ps[pm][gg * 64:(gg + 1) * 64, 0:128], vbf4[:, t, g * 64:(g + 1) * 64], wsT[:, g, :], True, True,
                               [("vbf4", t), "wsT"], [("ps", pm)])
                        tt("dve", tmpc[:, c, :], ps[pm][:, 0:128], bsT[:, c, :], ALU.add, [("ps", pm), "bsT"], [("tmpc", c)])
                        stt(cm[:, c, t0 + t * 128:t0 + (t + 1) * 128], ug[:, c, t * 128:(t + 1) * 128], 0.5, tmpc[:, c, :], ALU.mult, ALU.mult,
                            [("ug", c), ("tmpc", c)], [("cm", b)])
                dma("sp", "stq1", qn_d[l].rearrange("(h p) t -> p h t", p=128)[:, :, t0:t0 + n], stQn[:, :, :n], ["stQn"], ["qn_d"])
                dma("sp", "stq2", qr_d[l].rearrange("(h p) t -> p h t", p=64)[:, :, t0:t0 + n], stQr[0:64, :, :n], ["stQr"], ["qr_d"])
                dma("sp", "stq3", dq_d[l].rearrange("(g p) t -> p g t", p=128)[:, :, t0:t0 + n], stM[:, :, :n], ["stM"], ["dq_d"])
                if b == 0:
                    kd, c0, ncol, nt_all = kvctx_d[l], 0, CTX, 2
                else:
                    kd, c0, ncol, nt_all = kvsrc_d[l], t0 - CTX, OWN, 16
                dma("sp", "stk1", kd[R_KN:R_KN + 512, :].rearrange("(h p) t -> p h t", p=128)[:, :, c0:c0 + n], stKn[:, :, :n], ["stKn"], ["kv_d"])
                dma("sp", "stk2", kd[R_KR:R_KR + 64, c0:c0 + n], stKr[0:64, :n], ["stKr"], ["kv_d"])
                for c in range(2):
                    dma("sp", "stk3", kd[R_DK + c * 128:R_DK + (c + 1) * 128, c0:c0 + n], stdK[:, c, :n], ["stdK"], ["kv_d"])
                tl0 = c0 // 128
                vmv = kd[R_VM:R_VM + 512, :].rearrange("(h p) (t d) -> p t h d", p=128, d=128)
                for h in range(4):
                    dma("sp", "stv1", vmv[:, tl0:tl0 + ntile, h, :], stVm[:, 0:ntile, h * 128:(h + 1) * 128], ["stVm"], ["kv_d"])
                vdv = kd[R_VD:R_VD + 256, :].rearrange("a (two rest) -> (a two) rest", two=2).rearrange("(h p) (t d) -> p t h d", p=128, d=64)
                for h in range(4):
                    dma("sp", "stv2", vdv[:, tl0:tl0 + ntile, h, :], stVd[:, 0:ntile, h * 64:(h + 1) * 64], ["stVd"], ["kv_d"])
            S.barrier()
            for c_ in range((KVROWS + CH - 1) // CH):
                sz = min(CH, KVROWS - c_ * CH)
                S.dma("pool", f"cc{c_ % 2}", lambda e, l=l, c_=c_, sz=sz: e.collective_compute(
                    "AllGather", ALU.bypass, replica_groups=[[0, 1, 2, 3], [4, 5, 6, 7]],
                    ins=[kvsrc_d[l][c_ * CH:c_ * CH + sz, :]], outs=[kvall_d[l][c_ * 4 * CH:c_ * 4 * CH + 4 * sz, :]]),
                    reads=["kv_d"], writes=["kvall"], inc1=True)
            if debug:
                pass
            if debug and False:
                dma("sp", "dbgcm", dbg_cm[l], cm[:], [("cm", b) for b in range(5)], ["dbgcm"])
            S.barrier()

            KA = carve(0, [128, NKEY], BF16)
            KR = carve(16896, [128, NKEY], BF16)
            VA = carve(33792, [128, NKT, 128], BF16)
            QA = carve(50688, [128, T], BF16)
            QB = carve(55296, [128, T], BF16)
            PT = carve(59904, [128, 4, 512], BF16)
            RT = carve(64000, [128, 2, 512], F32)
            DD = carve(68096, [128, 6, 512], F32)
            ACC = carve(80384, [128, 2, 2, 512], F32)
            ACC = carve(80384, [128, 2, 2, 512], F32)
            def kv_view(a, n_):
                c_ = a // CH
                i0 = a % CH
                sz = min(CH, KVROWS - c_ * CH)
                assert i0 + n_ <= sz
                return kvall_d[l][c_ * 4 * CH:c_ * 4 * CH + 4 * sz, :].rearrange("(r i) col -> i r col", r=4)[i0:i0 + n_]
            kctx = kvctx_d[l]
            SEGS = [(0, 2)] + [(2 + r * 16, 16) for r in range(4)]

            def seg_of(kt):
                return 0 if kt < 2 else 1 + (kt - 2) // 16

            def load_k(dst, rows0, nrows, name, sem):
                dma("sp", sem, dst[0:nrows, 0:CTX], kctx[rows0:rows0 + nrows, :], ["kv_d"], [(name, 0)])
                for r in range(4):
                    dma("sp", sem, dst[0:nrows, CTX + r * OWN:CTX + (r + 1) * OWN], kv_view(rows0, nrows)[:, r, :], ["kvall"], [(name, 1 + r)])

            qblocks = ([(0, 256, 2, 0)] if not last else []) + [(256 + i * 512, 512, NKT, i + 1) for i in range(4)]

            load_k(KR, R_KR, 64, "KR", "ldkr")
            steps = []
            for h in range(4):
                for (q0, nq, nk, qb) in qblocks:
                    for kt in range(nk):
                        steps.append((h, q0, nq, nk, qb, kt))
            qbi_of = {}
            for st_ in steps:
                key = (st_[0], st_[4])
                if key not in qbi_of:
                    qbi_of[key] = len(qbi_of)

            def mla_S(i):
                h, q0, nq, nk, qb, kt = steps[i]
                if kt == 0 and (qb == qblocks[0][3]):
                    load_k(KA, R_KN + h * 128, 128, "KA", "ldka")
                    dma("sp", "ldqa", QA, qn_d[l][h * 128:(h + 1) * 128, :], ["qn_d"], ["QA"])
                    dma("sp", "ldqb", QB[0:64, :], qr_d[l][h * 64:(h + 1) * 64, :], ["qr_d"], ["QB"])
                sg = seg_of(kt)
                sb_ = i % 3
                mm(ps[sb_][:, :nq], KA[:, kt * 128:(kt + 1) * 128], QA[:, q0:q0 + nq], True, False, [("KA", sg), "QA"], [("ps", sb_)])
                mm(ps[sb_][:, :nq], KR[0:64, kt * 128:(kt + 1) * 128], QB[0:64, q0:q0 + nq], False, True, [("KR", sg), "QB"], [("ps", sb_)])

            def mla_PV(i):
                h, q0, nq, nk, qb, kt = steps[i]
                sg = seg_of(kt)
                sb_ = i % 3
                qbi = qbi_of[(h, qb)]
                ob, sm = 3 + qbi % 2, 5 + qbi % 2
                if kt == 0 and (qb == qblocks[0][3]):
                    vm_c = kctx[R_VM + h * 128:R_VM + (h + 1) * 128, :].rearrange("p (t d) -> p t d", d=128)
                    dma("sp", "ldva", VA[:, 0:2, :], vm_c, ["kv_d"], [("VA", 0)])
                    for r in range(4):
                        dma("sp", "ldva", VA[:, 2 + r * 16:2 + (r + 1) * 16, :],
                            kv_view(R_VM + h * 128, 128)[:, r, :].rearrange("p (t d) -> p t d", d=128), ["kvall"], [("VA", 1 + r)])
                p_ = PT[:, i % 4, :nq]
                act(p_, ps[sb_][:, :nq], AF.Exp, [("ps", sb_)], [("PT", i % 4)])
                mm(ps[ob][:, :nq], VA[:, kt, :], p_, kt == 0, kt == nk - 1, [("VA", sg), ("PT", i % 4)], [("ps", ob)])
                mm(ps[sm][:, :nq], onesb[:], p_, kt == 0, kt == nk - 1, ["onesb", ("PT", i % 4)], [("ps", sm)])
                if kt == nk - 1:
                    r_ = RT[:, qbi % 2, :nq]
                    recip(r_, ps[sm][:, :nq], [("ps", sm)], [("RT", qbi % 2)])
                    tt("dve", hc[:, h, q0:q0 + nq], ps[ob][:, :nq], r_, ALU.mult, [("ps", ob), ("RT", qbi % 2)], [("hc", qb)])

            if steps:
                mla_S(0)
                if len(steps) > 1:
                    mla_S(1)
                for i in range(len(steps)):
                    if i + 2 < len(steps):
                        mla_S(i + 2)
                    mla_PV(i)

            dsteps = []
            for h in range(4):
                for (q0, nq, nk, qb) in qblocks:
                    for m in range(2):
                        for kt in range(nk):
                            dsteps.append((h, q0, nq, nk, qb, m, kt))
            dqbi = {}
            for st_ in dsteps:
                key = (st_[0], st_[4])
                if key not in dqbi:
                    dqbi[key] = len(dqbi)
            KM = [KA, KR]
            QM = [QA, QB]
            KMN = ["KA", "KR"]
            QMN = ["QA", "QB"]

            def d_S(i):
                h, q0, nq, nk, qb, m, kt = dsteps[i]
                if kt == 0 and m == 0 and qb == qblocks[0][3]:
                    if h % 2 == 0:
                        load_k(KA, R_DK + (h // 2) * 128, 128, "KA", "ldka")
                    for mm_ in range(2):
                        g = 2 * h + mm_
                        dma("sp", "ldqa" if mm_ == 0 else "ldqb", QM[mm_][:, :], dq_d[l][g * 128:(g + 1) * 128, :], ["dq_d"], [QMN[mm_]])
                sg = seg_of(kt)
                sb_ = i % 3
                mm(ps[sb_][:, :nq], KA[:, kt * 128:(kt + 1) * 128], QM[m][:, q0:q0 + nq], True, True, [("KA", sg), QMN[m]], [("ps", sb_)])

            def d_PV(i):
                h, q0, nq, nk, qb, m, kt = dsteps[i]
                sg = seg_of(kt)
                sb_ = i % 3
                qbi = dqbi[(h, qb)]
                ob = 3 + (qbi % 2) * 2 + m
                if kt == 0 and m == 0 and qb == qblocks[0][3]:
                    if h == 0:
                        S.op("pool", lambda e: e.memset(VA[:, :, 64:128], 1.0), reads=[], writes=[("VA", s_) for s_ in range(5)])
                    vdc = kctx[R_VD:R_VD + 256, :].rearrange("a (two rest) -> (a two) rest", two=2)[h * 128:(h + 1) * 128, :].rearrange("p (t d) -> p t d", d=64)
                    dma("sp", "ldva", VA[:, 0:2, 0:64], vdc, ["kv_d"], [("VA", 0)])
                    for r in range(4):
                        vdr = kv_view(R_VD + h * 64, 64)[:, r, :].rearrange("a (two rest) -> (a two) rest", two=2).rearrange("p (t d) -> p t d", d=64)
                        dma("sp", "ldva", VA[:, 2 + r * 16:2 + (r + 1) * 16, 0:64], vdr, ["kvall"], [("VA", 1 + r)])
                    if l == 0 and n_layers > 1:
                        awb2, modrow2 = ada_bufs(81920)
                        if h >= 1:
                            for g_ in (2 * (h - 1), 2 * (h - 1) + 1):
                                ada_compute(1, g_, awb2[g_ % 2], ("awb2", g_ % 2), modrow2, "modrow2", bank=7)
                        if h <= 2:
                            for g_ in (2 * h, 2 * h + 1):
                                ada_load(1, g_, awb2[g_ % 2], ("awb2", g_ % 2), g_ % 2)
                        if h == 3:
                            ada_finish(1, 81920 + 2 * 32768 + 4096)
                p_ = PT[:, i % 4, :nq]
                act(p_, ps[sb_][:, :nq], AF.Exp, [("ps", sb_)], [("PT", i % 4)])
                mm(ps[ob][:, :nq], VA[:, kt, :], p_, kt == 0, kt == nk - 1, [("VA", sg), ("PT", i % 4)], [("ps", ob)])
                if kt == nk - 1 and m == 1:
                    o0, o1 = 3 + (qbi % 2) * 2, 4 + (qbi % 2) * 2
                    sq_b = 7
                    r0, r1 = RT[0:64, 0, :nq], RT[0:64, 1, :nq]
                    d1, d2, dd, sq_, rs_ = DD[0:64, 0, :nq], DD[0:64, 1, :nq], DD[0:64, 2, :nq], DD[0:64, 3, :nq], DD[0:64, 4, :nq]
                    recip(r0, ps[o0][64:128, :nq], [("ps", o0)], [("RT", 0)])
                    recip(r1, ps[o1][64:128, :nq], [("ps", o1)], [("RT", 1)])
                    tt("dve", d1, ps[o0][0:64, :nq], r0, ALU.mult, [("ps", o0), ("RT", 0)], ["dd1"])
                    tt("dve", d2, ps[o1][0:64, :nq], r1, ALU.mult, [("ps", o1), ("RT", 1)], ["dd2"])
                    stt(dd, d2, C("neglam", l)[0:64, :], d1, ALU.mult, ALU.add, ["dd1", "dd2", "cols"], ["ddd"])
                    tt("pool", sq_, dd, dd, ALU.mult, ["ddd"], ["ddsq"])
                    mm(ps[sq_b][0:64, :nq], onesf[0:64, 0:64], sq_, True, True, ["ddsq", "onesf"], [("ps", sq_b)])
                    act(rs_, ps[sq_b][0:64, :nq], AF.Ln, [("ps", sq_b)], ["ddrs"], scale=1.0 / 64, bias=EPS_RMS[0:64, :])
                    act(rs_, rs_, AF.Exp, ["ddrs"], ["ddrs"], scale=-0.5)
                    hb = (h % 2) * 64
                    stt(hc[hb:hb + 64, 4 + h // 2, q0:q0 + nq], dd, C("gsub", l)[0:64, :], rs_, ALU.mult, ALU.mult, ["ddd", "ddrs", "cols"], [("hc", qb)])

            if dsteps:
                d_S(0)
                if len(dsteps) > 1:
                    d_S(1)
                for i in range(len(dsteps)):
                    if i + 2 < len(dsteps):
                        d_S(i + 2)
                    d_PV(i)
            if debug and l == 0:
                dma("sp", "dbgcc", dbg_cc[l], hc[:], [("hc", b) for b in range(5)], ["dbgcc"])
            S.barrier()

            wo = carve(73728, [128, 8, 1024], BF16)
            for kk in range(8):
                dma("pool", f"wl{kk}", wo[:, kk, :], w_o_d[l, kk * 128:(kk + 1) * 128, :], [], [("wo", kk)])
            for b in blks_res:
                t0, n = BLKS[b]
                dma("sp", "xin", xT[:, :, t0:t0 + n], xsp_d[l][:, :, t0:t0 + n], [("xsp", b)], [("x", b)])
            prev_b = None
            for b in blks_res:
                t0, n = BLKS[b]
                s = 1 if b == 0 else 0
                for oc in range(8):
                    pi = psnext()
                    for k in range(8):
                        rhs = hc[:, k, t0:t0 + n] if k < 6 else cm[:, k - 6, t0:t0 + n]
                        mm(ps[pi][:, :n], wo[:, k, oc * 128:(oc + 1) * 128], rhs, k == 0, k == 7, [("wo", k), ("hc", b), ("cm", b)], [("ps", pi)])
                    stt(xT[:, oc, t0:t0 + n], ps[pi][:, :n], C("g1s", l)[:, s * 8 + oc:s * 8 + oc + 1], xT[:, oc, t0:t0 + n], ALU.mult, ALU.add,
                        [("ps", pi), ("x", b), ("xk", b, oc), "cols"], [("xk", b, oc)])
                if prev_b is not None:
                    emit_ln(prev_b, l, "ln1_g", "ln1_b", ("A2", l), ("B2", l), 1 if prev_b == 0 else 0, True, hc, LNT)
                prev_b = b
            emit_ln(prev_b, l, "ln1_g", "ln1_b", ("A2", l), ("B2", l), 1 if prev_b == 0 else 0, True, hc, LNT)
            if debug and l == 0:
                dma("sp", "dbgx1", dbg_x1[l], xT, [("x", b) for b in range(5)], ["dbgx1"])

            hl = carve(LNT + 16384, [128, 2, 8], BF16)
            hg = carve(LNT + 16384 + 64, [128, 4, 16], BF16)
            hacc = carve(LNT + 16384 + 256, [128, 2, 8], F32)
            cp("dve", hl[:, 0, :], hc[:, :, CTX], [("hc", 1)], ["hl"])
            cp("dve", hl[:, 1, :], hc[:, :, T - 1], [("hc", 4)], ["hl"])
            dma("sp", "hs", hsrc_d[l], hl.rearrange("p a b -> p (a b)"), ["hl"], ["hsrc"])
            if True:
                S.dma("pool", "cc", lambda e, l=l: e.collective_compute("AllGather", ALU.bypass, replica_groups=[[0, 1, 2, 3], [4, 5, 6, 7]],
                                                                      ins=[hsrc_d[l]], outs=[hall_d[l]]), reads=["hsrc"], writes=["hall"], inc1=True)
            dma("sp", "hg", hg, hall_d[l].rearrange("(r p) c -> p r c", p=128), ["hall"], ["hg"])
            for side in range(2):
                w_ = 1 - side
                for r in range(4):
                    src = hg[:, r, w_ * 8:(w_ + 1) * 8]
                    mcol = hmask[:, side, r:r + 1]
                    if r == 0:
                        ts("dve", hacc[:, side, :], src, mcol, None, ALU.mult, None, ["hg", "hmask"], ["hacc"])
                    else:
                        stt(hacc[:, side, :], src, mcol, hacc[:, side, :], ALU.mult, ALU.add, ["hg", "hmask", "hacc"], ["hacc"])
                cp("dve", hc[:, :, T + side], hacc[:, side, :], ["hacc"], ["hchalo"])
            S.barrier()

            FB = 73728
            wgv = [carve(FB + i * 4096, [128, 8, 2, 128], BF16) for i in range(3)]
            wd = [carve(FB + 12288 + i * 2048, [128, 1024], BF16) for i in range(3)]
            ugb = [carve(FB + 18432 + i * 16400, [128, 2, OWN + 2], F32) for i in range(2)]
            cacc = carve(FB + 18432 + 32800, [128, 2, OWN], F32)
            aT = [carve(FB + 18432 + 32800 + 16384 + i * 4096, [128, OWN], BF16) for i in range(2)]
            CB = FB + 18432 + 32800 + 16384 + 8192
            ugc = carve(CB, [128, 2, CTX + 2], F32)
            caccc = carve(CB + 2064, [128, 2, CTX], F32)
            aTc = carve(CB + 2064 + 2048, [128, CTX], BF16)
            assert CB + 2064 + 2048 + 512 <= ARW * 4
            if not last:
                S.op("pool", lambda e, o=ugc: e.memset(o, 0.0), writes=["ugc"])
            wupv = wup_d[l].rearrange("(k p) (gv f c) -> p k gv f c", p=128, gv=2, c=128)

            facc = cm[:].rearrange("p a b -> p (a b)").bitcast(F32)[:, 0:2048].rearrange("p (a b) -> p a b", a=4)
            facc_i = [0]

            def cw(f, gv):
                fc = gv * NFC + f
                return (CI("cw0", l)[:, fc:fc + 1], CI("cw1", l)[:, fc:fc + 1], CI("cw2", l)[:, fc:fc + 1], CI("cb", l)[:, fc:fc + 1])

            def ffn_load(f):
                wb_ = f % 3
                for gv in range(2):
                    dma("pool", f"wgv{wb_}", wgv[wb_][:, :, gv, :], wupv[:, :, gv, f, :], [], [("wgv", wb_)])
                dma("pool", f"wd{wb_}", wd[wb_], wdown_d[l, f * 128:(f + 1) * 128, :], [], [("wd", wb_)])

            def ffn_up(f):
                wb_, ub = f % 3, f % 2
                for gv in range(2):
                    for tb in range(4):
                        t0 = CTX + tb * 512
                        pp = psnext()
                        for k in range(8):
                            mm(ps[pp][:, :], wgv[wb_][:, k, gv, :], hc[:, k, t0:t0 + 512], k == 0, k == 7, [("wgv", wb_), ("hc", tb + 1)], [("ps", pp)])
                        act(ugb[ub][:, gv, 1 + tb * 512:1 + (tb + 1) * 512], ps[pp][:, :], AF.Identity, [("ps", pp)], [("ugb", ub, gv)])
                    pp = psnext()
                    for k in range(8):
                        mm(ps[pp][:, 0:2], wgv[wb_][:, k, gv, :], hc[:, k, T:T + 2], k == 0, k == 7, [("wgv", wb_), "hchalo"], [("ps", pp)])
                    cp("dve", ugb[ub][:, gv, 0:1], ps[pp][:, 0:1], [("ps", pp)], [("ugb", ub, gv)])
                    cp("dve", ugb[ub][:, gv, OWN + 1:OWN + 2], ps[pp][:, 1:2], [("ps", pp)], [("ugb", ub, gv)])

            def ffn_conv(f):
                ub = f % 2
                for gv in range(2):
                    w0, w1, w2, bb = cw(f, gv)
                    act(cacc[:, gv, :], ugb[ub][:, gv, 1:OWN + 1], AF.Identity, [("ugb", ub, gv), "colsin"], [("cacc", gv)], scale=w1, bias=bb)
                    stt(cacc[:, gv, :], ugb[ub][:, gv, 0:OWN], w0, cacc[:, gv, :], ALU.mult, ALU.add, [("ugb", ub, gv), ("cacc", gv), "colsin"], [("cacc", gv)])
                    stt(cacc[:, gv, :], ugb[ub][:, gv, 2:OWN + 2], w2, cacc[:, gv, :], ALU.mult, ALU.add, [("ugb", ub, gv), ("cacc", gv), "colsin"], [("cacc", gv)])
                act(cacc[:, 0, :], cacc[:, 0, :], AF.Silu, [("cacc", 0)], [("cacc", 0)])
                tt("pool", aT[ub], cacc[:, 0, :], cacc[:, 1, :], ALU.mult, [("cacc", 0), ("cacc", 1)], [("aT", ub)])

            def ffn_down(f):
                wb_, ub = f % 3, f % 2
                for oc in range(8):
                    for tb in range(4):
                        t0 = CTX + tb * 512
                        pp = psnext()
                        mm(ps[pp][:, :], wd[wb_][:, oc * 128:(oc + 1) * 128], aT[ub][:, tb * 512:(tb + 1) * 512], True, True, [("wd", wb_), ("aT", ub)], [("ps", pp)])
                        xk = ("xo", tb, oc)
                        if oc % 2 == 0:
                            stt(xT[:, oc, t0:t0 + 512], ps[pp][:, :], C("g2s", l)[:, oc:oc + 1], xT[:, oc, t0:t0 + 512], ALU.mult, ALU.add,
                                [("ps", pp), xk, "cols"], [xk])
                        else:
                            fi = facc_i[0] % 4
                            facc_i[0] += 1
                            act(facc[:, fi, :], ps[pp][:, :], AF.Identity, [("ps", pp), "cols"], [("facc", fi)], scale=C("g2s", l)[:, oc:oc + 1])
                            tt("pool", xT[:, oc, t0:t0 + 512], xT[:, oc, t0:t0 + 512], facc[:, fi, :], ALU.add, [("facc", fi), xk], [xk])

            def ffn_ctx_up(f):
                wb_ = f % 3
                for gv in range(2):
                    pp = psnext()
                    for k in range(8):
                        mm(ps[pp][:, 0:CTX], wgv[wb_][:, k, gv, :], hc[:, k, 0:CTX], k == 0, k == 7, [("wgv", wb_), ("hc", 0)], [("ps", pp)])
                    act(ugc[:, gv, 1:CTX + 1], ps[pp][:, 0:CTX], AF.Identity, [("ps", pp)], ["ugc"])
                for gv in range(2):
                    w0, w1, w2, bb = cw(f, gv)
                    act(caccc[:, gv, :], ugc[:, gv, 1:CTX + 1], AF.Identity, ["ugc", "colsin"], [("caccc", gv)], scale=w1, bias=bb)
                    stt(caccc[:, gv, :], ugc[:, gv, 0:CTX], w0, caccc[:, gv, :], ALU.mult, ALU.add, ["ugc", ("caccc", gv), "colsin"], [("caccc", gv)])
                    stt(caccc[:, gv, :], ugc[:, gv, 2:CTX + 2], w2, caccc[:, gv, :], ALU.mult, ALU.add, ["ugc", ("caccc", gv), "colsin"], [("caccc", gv)])
                act(caccc[:, 0, :], caccc[:, 0, :], AF.Silu, [("caccc", 0)], [("caccc", 0)])
                tt("pool", aTc, caccc[:, 0, :], caccc[:, 1, :], ALU.mult, [("caccc", 0), ("caccc", 1)], ["aTc"])

            def ffn_ctx_down(f):
                wb_ = f % 3
                for oc in range(8):
                    pp = psnext()
                    mm(ps[pp][:, 0:CTX], wd[wb_][:, oc * 128:(oc + 1) * 128], aTc, True, True, [("wd", wb_), "aTc"], [("ps", pp)])
                    stt(xT[:, oc, 0:CTX], ps[pp][:, 0:CTX], C("g2s", l)[:, 8 + oc:8 + oc + 1], xT[:, oc, 0:CTX], ALU.mult, ALU.add,
                        [("ps", pp), ("xo", "c", oc), "cols"], [("xo", "c", oc)])

            ffn_load(0)
            ffn_load(1)
            ffn_up(0)
            ffn_conv(0)
            for f in range(NFC):
                if f + 2 < NFC:
                    ffn_load(f + 2)
                if f + 1 < NFC:
                    ffn_up(f + 1)
                if not last:
                    ffn_ctx_up(f)
                ffn_down(f)
                if not last:
                    ffn_ctx_down(f)
                if f + 1 < NFC:
                    ffn_conv(f + 1)
            S.barrier()
            for b in blks_res:
                s = 1 if b == 0 else 0
                if last:
                    emit_ln(b, l, "ln2_g", "ln2_b", None, None, s, True, None, LNT)
                else:
                    emit_ln(b, l, "ln2_g", "ln2_b", ("A1", l + 1), ("B1", l + 1), s, True, hc, LNT)
            if debug and l == 0:
                dma("sp", "dbgx2", dbg_x2[l], xT, [("x", b) for b in range(5)], ["dbgx2"])
            if l == n_layers - 1:
                for b in [1, 2, 3, 4]:
                    t0, n = BLKS[b]
                    dma("sp", "outs", out_d.rearrange("(k p) t -> p k t", p=128)[:, :, t0 - CTX:t0 - CTX + n], xT[:, :, t0:t0 + n], [("x", b)] + [("xk", b, k) for k in range(8)], ["out"])
            S.barrier()

        S.final_wait("sp")
        S.emit()
    return nc


def _mk_cols():
    cols = {}
    o = 0

    def add(name, w):
        nonlocal o
        cols[name] = (o, w)
        o += w

    add("eps", 4)
    add("mod", DEPTH * 6 * 16)
    for l in range(DEPTH):
        for nm in ("sc1p", "sc2p", "g1s", "g2s", "A1", "B1", "A2", "B2"):
            add((nm, l), 16)
        add(("lsum", l), 2)
        add(("neglam", l), 1)
        add(("gsub", l), 1)
    return cols, o


COLS, NCOLS = _mk_cols()


def _mk_colin():
    c = {}
    o = 0
    for nm, w in (("adab", 48), ("gq", 3), ("gkv", 2), ("subg", 1), ("ln1_g", 8), ("ln1_b", 8), ("ln2_g", 8), ("ln2_b", 8),
                  ("cw0", 44), ("cw1", 44), ("cw2", 44), ("cb", 44)):
        c[nm] = (o, w)
        o += w
    return c, o


COLIN, NCOLIN = _mk_colin()


def _rope_tables(core):
    r = core % 4
    tpos = np.arange(r * OWN, (r + 1) * OWN)
    rows = (tpos // 64).astype(np.float32)
    colsp = (tpos % 64).astype(np.float32)
    tabs = np.zeros((128, 4, T), np.float32)
    tabs[:, 0, :CTX] = 1.0
    tabs[:, 2, :CTX] = 1.0

    def fill(ci, si, width, nparts):
        half = width // 2
        n = half // 2
        inv = (10000.0 ** (-np.arange(n, dtype=np.float32) / n)).astype(np.float32)
        for p in range(nparts):
            j = p % width
            hf = j // half
            i = (j % half) % n
            pos = rows if hf == 0 else colsp
            ang = (pos * inv[i]).astype(np.float32)
            tabs[p, ci, CTX:] = np.cos(ang)
            tabs[p, si, CTX:] = np.sin(ang)

    fill(0, 1, 64, 64)
    fill(2, 3, 32, 128)
    return tabs


def _rot_mats():
    rm = np.zeros((128, 3, 128), np.float32)
    rm[:, 2, :] = np.eye(128, dtype=np.float32)
    for base in (0, 32):
        for i in range(16):
            rm[base + i + 16, 0, base + i] = -1.0
            rm[base + i, 0, base + i + 16] = 1.0
    for base in range(0, 128, 16):
        for i in range(8):
            rm[base + i + 8, 1, base + i] = -1.0
            rm[base + i, 1, base + i + 8] = 1.0
    return rm


def _colmajor(v, nchunk):
    return np.ascontiguousarray(np.asarray(v, np.float32).reshape(nchunk, 128).T)


def prep_inputs(inp):
    f = lambda a: np.ascontiguousarray(np.asarray(a, np.float32))
    L = DEPTH
    colsin = np.zeros((128, L, NCOLIN), np.float32)

    def put(name, l, arr):
        o, w = COLIN[name]
        colsin[:, l, o:o + w] = arr

    for l in range(L):
        put("adab", l, _colmajor(inp["ada_b"][l], 48))
        put("gq", l, _colmajor(inp["mla_gq"][l], 3))
        put("gkv", l, _colmajor(inp["mla_gkv"][l], 2))
        put("subg", l, np.tile(np.asarray(inp["diff_subln_g"][l], np.float32), 2)[:, None])
        put("ln1_g", l, _colmajor(inp["ln1_g"][l], 8))
        put("ln1_b", l, _colmajor(inp["ln1_b"][l], 8))
        put("ln2_g", l, _colmajor(inp["ln2_g"][l], 8))
        put("ln2_b", l, _colmajor(inp["ln2_b"][l], 8))
        for j in range(3):
            put(f"cw{j}", l, _colmajor(inp["ffn_convw"][l][j], 44))
        put("cb", l, _colmajor(inp["ffn_convb"][l], 44))
    wuq = f(inp["mla_wuq"]).reshape(L, 384, 4, 192)
    wuq_p = np.ascontiguousarray(np.concatenate([wuq[..., :128].reshape(L, 384, 512), wuq[..., 128:].reshape(L, 384, 256)], -1))
    wukv = f(inp["mla_wukv"]).reshape(L, 256, 4, 256)
    wukv_p = np.ascontiguousarray(np.concatenate([wukv[..., :128].reshape(L, 256, 512), wukv[..., 128:].reshape(L, 256, 512)], -1))
    ws = f(inp["sgu_ws"])
    wsT = np.ascontiguousarray(ws.transpose(0, 3, 1, 2))
    bs = f(inp["sgu_bs"])
    bsT = np.zeros((L, 128, 2, 128), np.float32)
    for g in range(4):
        bsT[:, (g % 2) * 64:(g % 2) * 64 + 64, g // 2, :] = bs[:, g, None, :]
    sgb = np.zeros((L, 128, 2, 256), np.float32)
    sgb[:, :, 0, :] = f(inp["sgu_ln_g"])[:, None, :]
    sgb[:, :, 1, :] = f(inp["sgu_ln_b"])[:, None, :]
    lamv = np.zeros((L, 128, 4, 32), np.float32)
    for i, nm in enumerate(("diff_lq1", "diff_lk1", "diff_lq2", "diff_lk2")):
        lamv[:, :, i, :] = f(inp[nm])[:, None, :]
    shared = {
        "rmat": _rot_mats(), "ada_w": f(inp["ada_w"]), "colsin": colsin, "w_in": f(inp["w_in"]), "wuq": wuq_p, "wukv": wukv_p,
        "w_o": f(inp["w_o"]), "wup": f(inp["ffn_wup"]), "wdown": f(inp["ffn_wdown"]), "sgu_wsT": wsT, "sgu_bsT": bsT,
        "sgu_gb": sgb, "lamv": lamv,
    }
    x = f(inp["x"])
    ctx = f(inp["ctx"])
    c = f(inp["c"])
    c_ctx = f(inp["c_ctx"])
    in_maps = []
    for core in range(8):
        b, r = core // 4, core % 4
        xloc = np.concatenate([ctx[b], x[b, r * OWN:(r + 1) * OWN]], 0)
        m = dict(shared)
        m["xT"] = np.ascontiguousarray(xloc.T)
        cc = np.zeros((128, 8, 2), np.float32)
        cc[:, :, 0] = _colmajor(c[b], 8)
        cc[:, :, 1] = _colmajor(c_ctx, 8)
        m["cc"] = cc
        m["tabs"] = _rope_tables(core)
        hm = np.zeros((128, 2, 4), np.float32)
        if r > 0:
            hm[:, 0, r - 1] = 1.0
        if r < 3:
            hm[:, 1, r + 1] = 1.0
        m["hmask"] = hm
        gm = np.zeros((128, 4), np.float32)
        for j in range(4):
            gm[j * 32:(j + 1) * 32, j] = 1.0
        m["gmask"] = gm
        in_maps.append(m)
    return in_maps


_NC_CACHE = {}


def kernel(**inputs):
    if "nc" not in _NC_CACHE:
        _NC_CACHE["nc"] = build_program()
    nc = _NC_CACHE["nc"]
    in_maps = prep_inputs(inputs)
    res = run_bass_kernel_spmd(nc, in_maps, core_ids=list(range(8)))
    out = np.zeros((2, SEQ, D), np.float32)
    for core in range(8):
        b, r = core // 4, core % 4
        out[b, r * OWN:(r + 1) * OWN, :] = res.results[core]["outT"].T
    return out
```

```python
import contextlib
import math
import numpy as np
import concourse.bass as bass
import concourse.mybir as mybir
from concourse.bass_utils import run_bass_kernel_spmd

F32 = mybir.dt.float32
BF16 = mybir.dt.bfloat16
AF = mybir.ActivationFunctionType
ALU = mybir.AluOpType
AX = mybir.AxisListType

D = 1024
DEPTH = 2
SEQ = 8192
CTX = 256
OWN = 2048
T = CTX + OWN
TH = T + 2
NKEY = CTX + SEQ
NKT = NKEY // 128
INW = 1984
DFF = 2816
NFC = 22
EPS = 1e-6
ALPHA = (2 * DEPTH) ** 0.25
MLA_SCALE = 192 ** -0.5
DIFF_SCALE = 32 ** -0.5
KVROWS = 1600
CH = 256
R_KN, R_VM, R_DK, R_KR, R_VD = 0, 512, 1024, 1280, 1344
BLKS = [(0, 256), (256, 512), (768, 512), (1280, 512), (1792, 512)]


class _Op:
    __slots__ = ("eng", "fn", "deps", "signal", "sigval", "dma_sem", "ndma", "inc1")

    def __init__(self, eng, fn, deps, dma_sem=None, ndma=1):
        self.eng = eng
        self.fn = fn
        self.deps = deps
        self.signal = dma_sem is not None
        self.sigval = None
        self.dma_sem = dma_sem
        self.ndma = ndma
        self.inc1 = False


class _Nop:
    def then_inc(self, *a, **k):
        return self


class Sched:
    ENGS = ("pe", "act", "dve", "pool", "sp")
    SAME_ENG_SYNC = ("act", "dve", "pool")

    def __init__(self, nc, stack):
        self.nc = nc
        self.stack = stack
        self.ops = {e: [] for e in self.ENGS}
        self.lastw = {}
        self.readers = {}
        self.dma_sems = {}
        self.dma_last = {}
        self.bar = []
        self.esem = {e: stack.enter_context(nc.semaphore(f"S_{e}")) for e in self.ENGS}

    def _deps(self, reads, writes):
        deps = [(b, "raw") for b in self.bar]
        for k in reads:
            w = self.lastw.get(k)
            if w is not None:
                deps.append((w, "raw"))
        for k in writes:
            w = self.lastw.get(k)
            if w is not None:
                deps.append((w, "waw"))
            last = {}
            for r in self.readers.get(k, ()):
                last[r.dma_sem if r.dma_sem is not None else ("e", r.eng)] = r
            for r in last.values():
                deps.append((r, "war"))
        return deps

    def _commit(self, op, reads, writes):
        for k in reads:
            self.readers.setdefault(k, []).append(op)
        for k in writes:
            self.lastw[k] = op
            self.readers[k] = []

    def op(self, eng, fn, reads=(), writes=()):
        o = _Op(eng, fn, self._deps(reads, writes))
        self.ops[eng].append(o)
        self._commit(o, reads, writes)
        return o

    def dma(self, queue, semname, fns, reads=(), writes=(), inc1=False):
        if not isinstance(fns, (list, tuple)):
            fns = [fns]
        if semname not in self.dma_sems:
            self.dma_sems[semname] = [self.stack.enter_context(self.nc.semaphore(f"D_{semname}")), 0]
        deps = self._deps(reads, writes)
        prev = self.dma_last.get(semname)
        if prev is not None:
            deps.append((prev, "waw"))
        o = _Op(queue, fns, deps, dma_sem=semname, ndma=len(fns))
        o.inc1 = inc1
        self.ops[queue].append(o)
        self.dma_last[semname] = o
        self._commit(o, reads, writes)
        return o

    def barrier(self):
        b = []
        for e in self.ENGS:
            if self.ops[e]:
                b.append(self.ops[e][-1])
        b.extend(self.dma_last.values())
        self.bar = b

    def final_wait(self, eng_name):
        o = _Op(eng_name, lambda eng: _Nop(), [(d, "raw") for d in self.dma_last.values()])
        self.ops[eng_name].append(o)

    def _skip(self, o, d, kind):
        if d.dma_sem is not None:
            return False
        if d.eng == o.eng and o.dma_sem is None:
            if kind == "war" or d.eng not in self.SAME_ENG_SYNC:
                return True
        return False

    def emit(self):
        nc = self.nc
        for e in self.ENGS:
            for o in self.ops[e]:
                for (d, kind) in o.deps:
                    if d.dma_sem is None and not self._skip(o, d, kind):
                        d.signal = True
        for e in self.ENGS:
            c = 0
            for o in self.ops[e]:
                if o.dma_sem is not None:
                    s = self.dma_sems[o.dma_sem]
                    s[1] += (1 if o.inc1 else 16) * o.ndma
                    o.sigval = s[1]
                    assert s[1] < 60000, ("dma sem overflow", o.dma_sem)
                elif o.signal:
                    c += 1
                    o.sigval = c
            assert c < 60000, ("engine sem overflow", e, c)

        def run(eng_name, eng):
            waited = {}
            for o in self.ops[eng_name]:
                need = {}
                for (d, kind) in o.deps:
                    if self._skip(o, d, kind):
                        continue
                    if d.dma_sem is not None:
                        key = ("d", d.dma_sem)
                        sem = self.dma_sems[d.dma_sem][0]
                    else:
                        key = ("e", d.eng)
                        sem = self.esem[d.eng]
                    v = d.sigval
                    if waited.get(key, 0) >= v:
                        continue
                    if key not in need or need[key][1] < v:
                        need[key] = (sem, v)
                for key, (sem, v) in need.items():
                    eng.wait_ge(sem, v)
                    waited[key] = v
                if o.dma_sem is not None:
                    sem = self.dma_sems[o.dma_sem][0]
                    for f in o.fn:
                        f(eng).then_inc(sem, 1 if o.inc1 else 16)
                else:
                    ins = o.fn(eng)
                    if o.signal:
                        ins.then_inc(self.esem[eng_name], 1)

        with nc.Block() as block:
            @block.tensor
            def _(e):
                run("pe", e)

            @block.scalar
            def _(e):
                run("act", e)

            @block.vector
            def _(e):
                run("dve", e)

            @block.gpsimd
            def _(e):
                run("pool", e)

            @block.sync
            def _(e):
                run("sp", e)


import os
NOCC = bool(int(os.environ.get("K_NOCC", "0")))
PARAMS = ["ada_w", "adab", "w_in", "wuq", "wukv", "w_o", "wup", "wdown"]


def build_program(n_layers=DEPTH, debug=False):
    nc = bass.Bass("TRN2", target_bir_lowering=False)
    dram = {}

    def din(name, shape, dt=F32):
        dram[name] = nc.dram_tensor(name, list(shape), dt, kind="ExternalInput").ap()
        return dram[name]

    def dint(name, shape, dt=BF16, kind="Internal"):
        dram[name] = nc.dram_tensor(name, list(shape), dt, kind=kind).ap()
        return dram[name]

    xT_d = din("xT", [D, T])
    cc_d = din("cc", [128, 8, 2])
    tabs_d = din("tabs", [128, 4, T])
    rmat_d = din("rmat", [128, 3, 128])
    hmask_d = din("hmask", [128, 2, 4])
    gmask_d = din("gmask", [128, 4])
    ada_w_d = din("ada_w", [DEPTH, D, 6 * D])
    colsin_d = din("colsin", [128, DEPTH, NCOLIN])
    w_in_d = din("w_in", [DEPTH, D, INW])
    wuq_d = din("wuq", [DEPTH, 384, 768])
    wukv_d = din("wukv", [DEPTH, 256, 1024])
    w_o_d = din("w_o", [DEPTH, D, D])
    wup_d = din("wup", [DEPTH, D, 2 * DFF])
    wdown_d = din("wdown", [DEPTH, DFF, D])
    sgu_ws_d = din("sgu_wsT", [DEPTH, 128, 4, 128])
    sgu_bsT_d = din("sgu_bsT", [DEPTH, 128, 2, 128])
    sgu_gb_d = din("sgu_gb", [DEPTH, 128, 2, 256])
    lamv_d = din("lamv", [DEPTH, 128, 4, 32])
    okind = "ExternalOutput"
    out_d = dint("outT", [D, OWN], F32, kind=okind)

    dbg = {}
    ikind = "Internal"
    xsp_d = [dint(f"xsp{l}", [128, 8, T], F32) for l in range(n_layers)]
    kvsrc_d = [dint(f"kvsrc{l}", [KVROWS, OWN], BF16) for l in range(n_layers)]
    kvall_d = [dint(f"kvall{l}", [4 * KVROWS, OWN], BF16) for l in range(n_layers)]
    kvctx_d = [dint(f"kvctx{l}", [KVROWS, CTX], BF16, kind=ikind) for l in range(n_layers)]
    qn_d = [dint(f"qn{l}", [512, T], BF16, kind=ikind) for l in range(n_layers)]
    qr_d = [dint(f"qr{l}", [256, T], BF16, kind=ikind) for l in range(n_layers)]
    dq_d = [dint(f"dq{l}", [8 * 128, T], BF16, kind=ikind) for l in range(n_layers)]
    hsrc_d = [dint(f"hsrc{l}", [128, 16], BF16) for l in range(n_layers)]
    hall_d = [dint(f"hall{l}", [4 * 128, 16], BF16) for l in range(n_layers)]
    if debug:
        dbg_cc = [dint(f"dbg_cc{l}", [128, 8, TH], BF16, kind="ExternalOutput") for l in range(1)]
        dbg_x1 = [dint(f"dbg_x1{l}", [128, 8, T], F32, kind="ExternalOutput") for l in range(1)]
        dbg_x2 = [dint(f"dbg_x2{l}", [128, 8, T], F32, kind="ExternalOutput") for l in range(1)]
        dbg_cols = dint("dbg_cols", [128, NCOLS], F32, kind="ExternalOutput")

    with contextlib.ExitStack() as st:
        S = Sched(nc, st)

        def sb(name, shape, dt):
            return st.enter_context(nc.sbuf_tensor(name, list(shape), dt))

        ARW = 38920
        AR = sb("AR", [128, ARW], F32)
        hc = sb("hc", [128, 8, TH], BF16)
        cm = sb("cm", [128, 2, T], BF16)
        cols = sb("cols", [128, NCOLS], F32)
        colsin = sb("colsin_sb", [128, DEPTH, NCOLIN], F32)
        onesb = sb("onesb", [128, 128], BF16)
        onesf = sb("onesf", [128, 128], F32)
        rmat = sb("rmat_sb", [128, 3, 128], F32)
        hmask = sb("hmask_sb", [128, 2, 4], F32)
        gmask = sb("gmask_sb", [128, 4], F32)
        scs_t = sb("scs_t", [128, 8, 2], F32)
        ps = [st.enter_context(nc.psum_tensor(f"ps{i}", [128, 512], F32)) for i in range(8)]

        def carve(off_bytes, shape, dt):
            n = int(np.prod(shape[1:]))
            nb = n * (4 if dt == F32 else 2)
            assert off_bytes % 4 == 0 and nb % 4 == 0
            assert off_bytes + nb <= ARW * 4, (off_bytes, nb)
            v = AR[:, off_bytes // 4:(off_bytes + nb) // 4]
            if dt != F32:
                v = v.bitcast(dt)
            if len(shape) == 3:
                v = v.rearrange("p (a b) -> p a b", a=shape[1])
            elif len(shape) == 4:
                v = v.rearrange("p (a b c) -> p a b c", a=shape[1], b=shape[2])
            elif len(shape) == 5:
                v = v.rearrange("p (a b c d) -> p a b c d", a=shape[1], b=shape[2], c=shape[3])
            return v

        xT = carve(0, [128, 8, T], F32)

        def mm(out, lhsT, rhs, start, stop, reads, writes):
            return S.op("pe", lambda e: e.matmul(out, lhsT=lhsT, rhs=rhs, start=start, stop=stop), reads=reads, writes=writes)

        def act(out, in_, func, reads, writes, scale=1.0, bias=None):
            if bias is None:
                return S.op("act", lambda e: e.activation(out=out, in_=in_, func=func, scale=scale), reads=reads, writes=writes)
            return S.op("act", lambda e: e.activation(out=out, in_=in_, func=func, bias=bias, scale=scale), reads=reads, writes=writes)

        def tt(eng, out, in0, in1, op, reads, writes):
            return S.op(eng, lambda e: e.tensor_tensor(out=out, in0=in0, in1=in1, op=op), reads=reads, writes=writes)

        def ts(eng, out, in0, s1, s2, op0, op1, reads, writes):
            if s2 is None:
                return S.op(eng, lambda e: e.tensor_scalar(out=out, in0=in0, scalar1=s1, scalar2=None, op0=op0), reads=reads, writes=writes)
            return S.op(eng, lambda e: e.tensor_scalar(out=out, in0=in0, scalar1=s1, scalar2=s2, op0=op0, op1=op1), reads=reads, writes=writes)

        def stt(out, in0, scalar, in1, op0, op1, reads, writes):
            return S.op("dve", lambda e: e.scalar_tensor_tensor(out=out, in0=in0, scalar=scalar, in1=in1, op0=op0, op1=op1), reads=reads, writes=writes)

        def cp(eng, out, in_, reads, writes):
            return S.op(eng, lambda e: e.tensor_copy(out=out, in_=in_), reads=reads, writes=writes)

        def recip(out, in_, reads, writes):
            return S.op("dve", lambda e: e.reciprocal(out=out, in_=in_), reads=reads, writes=writes)

        def dma(q, sem, out, in_, reads, writes):
            return S.dma(q, sem, lambda e: e.dma_start(out=out, in_=in_), reads=reads, writes=writes)

        psrr = [0]

        def psnext():
            i = psrr[0] % 8
            psrr[0] += 1
            return i

        def C(name, l=None):
            o, w = COLS[name if l is None else (name, l)]
            return cols[:, o:o + w]

        def CI(name, l):
            o, w = COLIN[name]
            return colsin[:, l, o:o + w]

        dma("sp", "c0", colsin[:], colsin_d, [], ["colsin"])
        dma("sp", "c1", rmat[:], rmat_d, [], ["rmat"])
        dma("sp", "c2", hmask[:], hmask_d, [], ["hmask"])
        dma("sp", "c2", gmask[:], gmask_d, [], ["gmask"])
        S.op("pool", lambda e: e.memset(onesb[:], 1.0), writes=["onesb"])
        S.op("pool", lambda e: e.memset(onesf[:], 1.0), writes=["onesf"])
        S.op("pool", lambda e: e.memset(cols[:], 0.0), writes=["cols"])
        for b, (t0, n) in enumerate(BLKS):
            dma("sp", "xin", xT[:, :, t0:t0 + n], xT_d.rearrange("(k p) t -> p k t", p=128)[:, :, t0:t0 + n], [], [("x", b)])
        o_eps, _ = COLS["eps"]
        S.op("dve", lambda e: e.memset(cols[:, o_eps:o_eps + 1], EPS / (ALPHA * ALPHA)), reads=["cols"], writes=["cols"])
        S.op("dve", lambda e: e.memset(cols[:, o_eps + 1:o_eps + 2], EPS), reads=["cols"], writes=["cols"])
        S.op("dve", lambda e: e.memset(cols[:, o_eps + 2:o_eps + 3], 4.0 * EPS), reads=["cols"], writes=["cols"])
        EPS_LN, EPS_RMS, EPS_G = cols[:, o_eps:o_eps + 1], cols[:, o_eps + 1:o_eps + 2], cols[:, o_eps + 2:o_eps + 3]

        ccs = carve(73728, [128, 8, 2], F32)
        dma("sp", "c3", ccs, cc_d, [], ["ccs"])
        act(scs_t[:], ccs, AF.Silu, ["ccs"], ["scs"])
        o_mod, _ = COLS["mod"]
        ada_it = [0]

        def MOD(l, j, s):
            o = o_mod + (l * 6 + j) * 16 + s * 8
            return cols[:, o:o + 8]

        def ada_bufs(base):
            return [carve(base + i * 32768, [128, 8, 1024], F32) for i in range(2)], carve(base + 2 * 32768, [128, 1024], F32)

        def ada_load(l, j, buf, key, slot):
            dma("sp", f"aw{slot}", buf, ada_w_d[l].rearrange("(k p) c -> p k c", p=128)[:, :, j * 1024:(j + 1) * 1024], [], [key])

        def ada_compute(l, j, buf, key, modrow, mkey, bank=None):
            ident = rmat[0:2, 2, 0:2]
            for half in range(2):
                pr = psnext() if bank is None else bank
                for kk in range(8):
                    mm(ps[pr][0:2, 0:512], scs_t[:, kk, :], buf[:, kk, half * 512:(half + 1) * 512], kk == 0, kk == 7, [key, "scs"], [("ps", pr)])
                cp("dve", modrow[0:2, half * 512:(half + 1) * 512], ps[pr][0:2, 0:512], [("ps", pr)], [mkey])
            pi = psnext() if bank is None else bank
            for ko in range(8):
                mm(ps[pi][:, ko * 2:ko * 2 + 2], modrow[0:2, ko * 128:(ko + 1) * 128], ident, True, True, [mkey, "rmat"], [("ps", pi)])
            o = o_mod + (l * 6 + j) * 16
            for s in range(2):
                tt("dve", cols[:, o + s * 8:o + s * 8 + 8], ps[pi][:, 0:16].rearrange("p (k s) -> p k s", s=2)[:, :, s],
                   CI("adab", l)[:, j * 8:(j + 1) * 8], ALU.add, [("ps", pi), "colsin"], ["cols"])

        def emit_ada(l):
            awb, modrow = ada_bufs(74240)
            for j in range(6):
                it = ada_it[0]
                ada_load(l, j, awb[it % 2], ("awb", it % 2), it % 2)
                ada_compute(l, j, awb[it % 2], ("awb", it % 2), modrow, "modrow")
                ada_it[0] += 1
            ada_finish(l, 73728 + 256)

        def ada_finish(l, lbase):
            for s in range(2):
                sl = slice(s * 8, s * 8 + 8)
                ts("dve", C("sc1p", l)[:, sl], MOD(l, 1, s), 1.0, None, ALU.add, None, ["cols"], ["cols"])
                ts("dve", C("sc2p", l)[:, sl], MOD(l, 4, s), 1.0, None, ALU.add, None, ["cols"], ["cols"])
                ts("dve", C("g1s", l)[:, sl], MOD(l, 2, s), 1.0 / ALPHA, None, ALU.mult, None, ["cols"], ["cols"])
                ts("dve", C("g2s", l)[:, sl], MOD(l, 5, s), 1.0 / ALPHA, None, ALU.mult, None, ["cols"], ["cols"])
                if l == 0:
                    cp("dve", C("A1", l)[:, sl], C("sc1p", l)[:, sl], ["cols"], ["cols"])
                    cp("dve", C("B1", l)[:, sl], MOD(l, 0, s), ["cols"], ["cols"])
                else:
                    tt("dve", C("A1", l)[:, sl], CI("ln2_g", l - 1), C("sc1p", l)[:, sl], ALU.mult, ["cols", "colsin"], ["cols"])
                    tt("dve", C("B1", l)[:, sl], CI("ln2_b", l - 1), C("sc1p", l)[:, sl], ALU.mult, ["cols", "colsin"], ["cols"])
                    tt("dve", C("B1", l)[:, sl], C("B1", l)[:, sl], MOD(l, 0, s), ALU.add, ["cols"], ["cols"])
                tt("dve", C("A2", l)[:, sl], CI("ln1_g", l), C("sc2p", l)[:, sl], ALU.mult, ["cols", "colsin"], ["cols"])
                tt("dve", C("B2", l)[:, sl], CI("ln1_b", l), C("sc2p", l)[:, sl], ALU.mult, ["cols", "colsin"], ["cols"])
                tt("dve", C("B2", l)[:, sl], C("B2", l)[:, sl], MOD(l, 3, s), ALU.add, ["cols"], ["cols"])
            lv = carve(lbase, [128, 4, 32], F32)
            lt = carve(lbase + 512, [128, 2, 32], F32)
            dma("sp", "c4", lv, lamv_d[l], [], ["lv"])
            tt("dve", lt[:, 0, :], lv[:, 0, :], lv[:, 1, :], ALU.mult, ["lv"], ["lt"])
            tt("dve", lt[:, 1, :], lv[:, 2, :], lv[:, 3, :], ALU.mult, ["lv"], ["lt"])
            S.op("dve", lambda e, l=l, lt=lt: e.reduce_sum(out=C("lsum", l), in_=lt, axis=AX.X), reads=["lt"], writes=["cols"])
            act(C("lsum", l), C("lsum", l), AF.Exp, ["cols"], ["cols"])
            lam_init = 0.8 - 0.6 * math.exp(-0.3 * l)
            tt("dve", C("neglam", l), C("lsum", l)[:, 1:2], C("lsum", l)[:, 0:1], ALU.subtract, ["cols"], ["cols"])
            ts("dve", C("neglam", l), C("neglam", l), -lam_init, None, ALU.add, None, ["cols"], ["cols"])
            ts("dve", C("gsub", l), CI("subg", l), 1.0 - lam_init, None, ALU.mult, None, ["colsin"], ["cols"])

        emit_ada(0)
        for l_ in range(2, n_layers):
            emit_ada(l_)
        if debug:
            dma("sp", "dbgc", dbg_cols, cols[:], ["cols"], ["dbgcols"])

        LNT = 73728 + 16384

        def emit_ln(b, l, gname, bname, Aname, Bname, s, write_x, hdst, lnbase):
            t0, n = BLKS[b]
            sqb = [carve(lnbase + i * 2048, [128, 512], F32) for i in range(2)]
            mean = carve(lnbase + 4096, [128, 512], F32)
            m2 = carve(lnbase + 6144, [128, 512], F32)
            rstd = carve(lnbase + 8192, [128, 512], F32)
            xh = [carve(lnbase + 10240 + i * 2048, [128, 512], F32) for i in range(2)]
            xk = ("x", b)
            p1, p2 = psnext(), psnext()
            for k in range(8):
                u = xT[:, k, t0:t0 + n]
                sq = sqb[k % 2]
                tt("pool", sq[:, :n], u, u, ALU.mult, [xk], [("lnsq", k % 2)])
                mm(ps[p1][:, :n], onesf[:], u, k == 0, k == 7, [xk, "onesf"], [("ps", p1)])
                mm(ps[p2][:, :n], onesf[:], sq[:, :n], k == 0, k == 7, [("lnsq", k % 2), "onesf"], [("ps", p2)])
            act(mean[:, :n], ps[p1][:, :n], AF.Identity, [("ps", p1)], ["lnmean"], scale=1.0 / D)
            tt("pool", m2[:, :n], mean[:, :n], mean[:, :n], ALU.mult, ["lnmean"], ["lnm2"])
            stt(rstd[:, :n], ps[p2][:, :n], 1.0 / D, m2[:, :n], ALU.mult, ALU.subtract, [("ps", p2), "lnm2"], ["lnrstd"])
            act(rstd[:, :n], rstd[:, :n], AF.Sqrt, ["lnrstd"], ["lnrstd"], bias=EPS_LN)
            recip(rstd[:, :n], rstd[:, :n], ["lnrstd"], ["lnrstd"])
            for k in range(8):
                u = xT[:, k, t0:t0 + n]
                x_ = xh[k % 2]
                tt("pool", x_[:, :n], u, mean[:, :n], ALU.subtract, [xk, "lnmean"], [("lnxh", k % 2)])
                tt("dve", x_[:, :n], x_[:, :n], rstd[:, :n], ALU.mult, [("lnxh", k % 2), "lnrstd"], [("lnxh", k % 2)])
                if write_x:
                    act(u, x_[:, :n], AF.Identity, [("lnxh", k % 2), "colsin"], [xk],
                        scale=CI(gname, l)[:, k:k + 1], bias=CI(bname, l)[:, k:k + 1])
                if hdst is not None:
                    act(hdst[:, k, t0:t0 + n], x_[:, :n], AF.Identity, [("lnxh", k % 2), "cols"], [("hc", b)],
                        scale=C(Aname[0], Aname[1])[:, s * 8 + k:s * 8 + k + 1], bias=C(Bname[0], Bname[1])[:, s * 8 + k:s * 8 + k + 1])

        for l in range(n_layers):
            last = (l == DEPTH - 1)
            blks_res = [1, 2, 3, 4] if last else [0, 1, 2, 3, 4]
            if l == 0:
                for b, (t0, n) in enumerate(BLKS):
                    s = 1 if b == 0 else 0
                    for k in range(8):
                        act(hc[:, k, t0:t0 + n], xT[:, k, t0:t0 + n], AF.Identity, [("x", b), "cols"], [("hc", b)],
                            scale=C("A1", l)[:, s * 8 + k:s * 8 + k + 1], bias=C("B1", l)[:, s * 8 + k:s * 8 + k + 1])
            for b in blks_res:
                t0, n = BLKS[b]
                dma("sp", "xsp", xsp_d[l][:, :, t0:t0 + n], xT[:, :, t0:t0 + n], [("x", b)], [("xsp", b)])
            if debug and False:
                dma("sp", "dbgh", dbg_h[l], hc[:], [("hc", b) for b in range(5)], ["dbgh"])
            S.barrier()

            win = carve(0, [128, 8, INW], BF16)
            wuq = carve(31744, [128, 3, 768], BF16)
            wukv = carve(36352, [128, 2, 1024], BF16)
            tabs = carve(40448, [128, 4, 512], F32)
            qlat = carve(48640, [128, 3, 512], F32)
            kvlat = carve(54784, [128, 2, 512], F32)
            sqq = carve(58880, [128, 2, 512], F32)
            rstdq = carve(62976, [128, 2, 512], F32)
            qn = carve(67072, [128, 3, 512], BF16)
            kvn = carve(70144, [128, 2, 512], BF16)
            rxs = carve(72192, [128, 2, 512], F32)
            rt1 = carve(76288, [128, 2, 512], F32)
            rt2 = carve(80384, [128, 2, 512], F32)
            stQn = carve(84480, [128, 4, 512], BF16)
            stQr = carve(88576, [128, 4, 512], BF16)
            stdQ = carve(92672, [128, 2, 512], BF16)
            stKn = carve(94720, [128, 4, 512], BF16)
            stKr = carve(98816, [128, 512], BF16)
            stdK = carve(99840, [128, 2, 512], BF16)
            stVm = carve(101888, [128, 4, 512], BF16)
            stVd = carve(105984, [128, 4, 256], BF16)
            ug = carve(108032, [128, 2, 512], F32)
            gtm = carve(112128, [128, 2, 512], F32)
            vt = carve(116224, [128, 2, 256], F32)
            vt2 = carve(118272, [128, 2, 256], F32)
            vbf = carve(120320, [128, 2, 256], BF16)
            bnst = carve(121344, [128, 2, 8], F32)
            bnag = carve(121408, [128, 2, 2], F32)
            tmpc = carve(121424, [128, 2, 128], F32)
            wsT = carve(122448, [128, 4, 128], BF16)
            bsT = carve(123472, [128, 2, 128], F32)
            sgb = carve(124496, [128, 2, 256], F32)
            stM = carve(126976, [128, 8, 512], BF16)
            rxs5 = carve(135168, [128, 5, 512], F32)
            sqq5 = carve(145408, [128, 5, 512], F32)
            vbf4 = carve(58880, [128, 4, 256], BF16)

            for kk in range(8):
                dma("pool", f"wl{kk}", win[:, kk, :], w_in_d[l, kk * 128:(kk + 1) * 128, :], [], [("win", kk)])
            dma("pool", "wld", wuq, wuq_d[l].rearrange("(k p) c -> p k c", p=128), [], ["wuq"])
            dma("pool", "wld", wukv, wukv_d[l].rearrange("(k p) c -> p k c", p=128), [], ["wukv"])
            dma("pool", "wld", wsT, sgu_ws_d[l], [], ["wsT"])
            dma("sp", "c5", bsT, sgu_bsT_d[l], [], ["bsT"])
            dma("sp", "c6", sgb, sgu_gb_d[l], [], ["sgb"])

            RM = rmat[0:64, 0, 0:64]
            RD = rmat[:, 1, :]

            def rope(pz, P, n, scale, cos, sin, R, dst, dkeys, wkeys, i):
                xs_, t1_, t2_ = rxs[0:P, i % 2, :n], rt1[0:P, i % 2, :n], rt2[0:P, i % 2, :n]
                act(xs_, pz, AF.Identity, dkeys, [("rxs", i % 2)], scale=scale)
                pr = psnext()
                mm(ps[pr][0:P, :n], R, xs_, True, True, [("rxs", i % 2), "rmat"], [("ps", pr)])
                tt("pool", t1_, xs_, cos, ALU.mult, [("rxs", i % 2), "tabs"], [("rt1", i % 2)])
                tt("dve", t2_, ps[pr][0:P, :n], sin, ALU.mult, [("ps", pr), "tabs"], [("rt2", i % 2)])
                tt("pool", dst, t1_, t2_, ALU.add, [("rt1", i % 2), ("rt2", i % 2)], wkeys)

            ri = [0]
            for b, (t0, n) in enumerate(BLKS):
                hk = ("hc", b)
                ntile = n // 128
                dma("sp", "tabs", tabs[:, :, :n], tabs_d[:, :, t0:t0 + n], [], ["tabs"])
                cosM, sinM, cosD, sinD = tabs[0:64, 0, :n], tabs[0:64, 1, :n], tabs[:, 2, :n], tabs[:, 3, :n]

                def proj(col0, M, K_src="h"):
                    pi = psnext()
                    for k in range(8):
                        mm(ps[pi][0:M, :n], win[:, k, col0:col0 + M], hc[:, k, t0:t0 + n], k == 0, k == 7, [("win", k), hk], [("ps", pi)])
                    return pi

                def rms(src, nch, width, gname, dstn, tag, ti):
                    pi = psnext()
                    for c in range(nch):
                        tt("pool", sqq[:, c % 2, :n], src[:, c, :n], src[:, c, :n], ALU.mult, [tag], [("sqq", c % 2)])
                        mm(ps[pi][:, :n], onesf[:], sqq[:, c % 2, :n], c == 0, c == nch - 1, [("sqq", c % 2), "onesf"], [("ps", pi)])
                    r_ = rstdq[:, ti, :n]
                    act(r_, ps[pi][:, :n], AF.Sqrt, [("ps", pi)], [("rstdq", ti)], scale=1.0 / width, bias=EPS_RMS)
                    recip(r_, r_, [("rstdq", ti)], [("rstdq", ti)])
                    for c in range(nch):
                        stt(dstn[:, c, :n], src[:, c, :n], CI(gname, l)[:, c:c + 1], r_, ALU.mult, ALU.mult,
                            [tag, ("rstdq", ti), "colsin"], [tag + "n"])

                for c in range(3):
                    pi = proj(c * 128, 128)
                    act(qlat[:, c, :n], ps[pi][:, :n], AF.Identity, [("ps", pi)], [("qlat", c)])
                    tt("pool", sqq5[:, c, :n], qlat[:, c, :n], qlat[:, c, :n], ALU.mult, [("qlat", c)], [("sqq5", c)])
                for c in range(2):
                    pi = proj(384 + c * 128, 128)
                    act(kvlat[:, c, :n], ps[pi][:, :n], AF.Identity, [("ps", pi)], [("kvlat", c)])
                    tt("pool", sqq5[:, 3 + c, :n], kvlat[:, c, :n], kvlat[:, c, :n], ALU.mult, [("kvlat", c)], [("sqq5", 3 + c)])
                rspec = [(640, 64, 1.0), (704, 128, DIFF_SCALE), (832, 128, DIFF_SCALE), (960, 128, 1.0), (1088, 128, 1.0)]
                for idx, (col0, P_, sc_) in enumerate(rspec):
                    pi = proj(col0, P_)
                    act(rxs5[0:P_, idx, :n], ps[pi][0:P_, :n], AF.Identity, [("ps", pi)], [("rxs5", idx)], scale=sc_)
                for t in range(ntile):
                    pi = psnext()
                    for k in range(8):
                        mm(ps[pi][:, 0:256], hc[:, k, t0 + t * 128:t0 + (t + 1) * 128], win[:, k, 1216:1472], k == 0, k == 7, [("win", k), hk], [("ps", pi)])
                    cp("dve", stVd[:, t, :], ps[pi][:, 0:256], [("ps", pi)], ["stVd"])
                for c in range(2):
                    pi = proj(1472 + c * 128, 128)
                    xs_ = gtm[:, 0, :n]
                    t_ = gtm[:, 1, :n]
                    act(xs_, ps[pi][:, :n], AF.Identity, [("ps", pi)], ["gxs"])
                    tt("pool", t_, xs_, xs_, ALU.mult, ["gxs"], ["gt"])
                    ts("pool", t_, t_, 0.044715, 1.0, ALU.mult, ALU.add, ["gt"], ["gt"])
                    tt("pool", t_, t_, xs_, ALU.mult, ["gt", "gxs"], ["gt"])
                    act(t_, t_, AF.Tanh, ["gt"], ["gt"], scale=0.7978845608028654)
                    stt(ug[:, c, :n], t_, 1.0, xs_, ALU.add, ALU.mult, ["gt", "gxs"], [("ug", c)])
                for t in range(ntile):
                    vi = t % 2
                    pi = psnext()
                    for k in range(8):
                        mm(ps[pi][:, 0:256], hc[:, k, t0 + t * 128:t0 + (t + 1) * 128], win[:, k, 1728:1984], k == 0, k == 7, [("win", k), hk], [("ps", pi)])
                    xs_, t_ = vt[:, vi, :], vt2[:, vi, :]
                    act(xs_, ps[pi][:, 0:256], AF.Identity, [("ps", pi)], [("vt", vi)])
                    tt("pool", t_, xs_, xs_, ALU.mult, [("vt", vi)], [("vt2", vi)])
                    ts("pool", t_, t_, 0.044715, 1.0, ALU.mult, ALU.add, [("vt2", vi)], [("vt2", vi)])
                    tt("pool", t_, t_, xs_, ALU.mult, [("vt2", vi), ("vt", vi)], [("vt2", vi)])
                    act(t_, t_, AF.Tanh, [("vt2", vi)], [("vt2", vi)], scale=0.7978845608028654)
                    stt(xs_, t_, 1.0, xs_, ALU.add, ALU.mult, [("vt2", vi), ("vt", vi)], [("vt", vi)])
                    S.op("dve", lambda e, o=bnst[:, vi, 0:6], i_=xs_: e.bn_stats(out=o, in_=i_), reads=[("vt", vi)], writes=[("bnst", vi)])
                    S.op("dve", lambda e, o=bnag[:, vi, :], i_=bnst[:, vi, 0:6]: e.bn_aggr(out=o, in_=i_), reads=[("bnst", vi)], writes=[("bnag", vi)])
                    act(bnag[:, vi, 1:2], bnag[:, vi, 1:2], AF.Sqrt, [("bnag", vi)], [("bnag", vi)], bias=EPS_G)
                    recip(bnag[:, vi, 1:2], bnag[:, vi, 1:2], [("bnag", vi)], [("bnag", vi)])
                    ts("dve", xs_, xs_, bnag[:, vi, 0:1], bnag[:, vi, 1:2], ALU.subtract, ALU.mult, [("vt", vi), ("bnag", vi)], [("vt", vi)])
                    tt("pool", xs_, xs_, sgb[:, 0, :], ALU.mult, [("vt", vi), "sgb"], [("vt", vi)])
                    tt("pool", vbf4[:, t, :], xs_, sgb[:, 1, :], ALU.add, [("vt", vi), "sgb"], [("vbf4", t)])

                def rms_b(src, nch, sq0, width, gname, dstn, tag, ti):
                    pi = psnext()
                    for c in range(nch):
                        mm(ps[pi][:, :n], onesf[:], sqq5[:, sq0 + c, :n], c == 0, c == nch - 1, [("sqq5", sq0 + c), "onesf"], [("ps", pi)])
                    r_ = rstdq[:, ti, :n]
                    act(r_, ps[pi][:, :n], AF.Sqrt, [("ps", pi)], [("rstdq", ti)], scale=1.0 / width, bias=EPS_RMS)
                    recip(r_, r_, [("rstdq", ti)], [("rstdq", ti)])
                    for c in range(nch):
                        stt(dstn[:, c, :n], src[:, c, :n], CI(gname, l)[:, c:c + 1], r_, ALU.mult, ALU.mult,
                            [(tag, c), ("rstdq", ti), "colsin"], [tag + "n"])

                rms_b(qlat, 3, 0, 384, "gq", qn, "qlat", 0)
                rms_b(kvlat, 2, 3, 256, "gkv", kvn, "kvlat", 1)
                rdst = [(64, cosM, sinM, RM, stKr[0:64, :n], "stKr"), (128, cosD, sinD, RD, stdQ[:, 0, :n], "stdQ"),
                        (128, cosD, sinD, RD, stdQ[:, 1, :n], "stdQ"), (128, cosD, sinD, RD, stdK[:, 0, :n], "stdK"),
                        (128, cosD, sinD, RD, stdK[:, 1, :n], "stdK")]
                for idx, (P_, cos_, sin_, R_, dst_, wk_) in enumerate(rdst):
                    i_ = ri[0]
                    ri[0] += 1
                    xs_, t1_, t2_ = rxs5[0:P_, idx, :n], rt1[0:P_, i_ % 2, :n], rt2[0:P_, i_ % 2, :n]
                    pr = psnext()
                    mm(ps[pr][0:P_, :n], R_, xs_, True, True, [("rxs5", idx), "rmat"], [("ps", pr)])
                    tt("pool", t1_, xs_, cos_, ALU.mult, [("rxs5", idx), "tabs"], [("rt1", i_ % 2)])
                    tt("dve", t2_, ps[pr][0:P_, :n], sin_, ALU.mult, [("ps", pr), "tabs"], [("rt2", i_ % 2)])
                    tt("pool", dst_, t1_, t2_, ALU.add, [("rt1", i_ % 2), ("rt2", i_ % 2)], [wk_])
                    if 1 <= idx <= 2:
                        c = idx - 1
                        for j in range(4):
                            ts("dve", stM[:, c * 4 + j, :n], stdQ[:, c, :n], gmask[:, j:j + 1], None, ALU.mult, None, ["stdQ", "gmask"], ["stM"])
                for h in range(4):
                    pi = psnext()
                    for c in range(3):
                        mm(ps[pi][:, :n], wuq[:, c, h * 128:(h + 1) * 128], qn[:, c, :n], c == 0, c == 2, ["wuq", "qlatn"], [("ps", pi)])
                    act(stQn[:, h, :n], ps[pi][:, :n], AF.Identity, [("ps", pi)], ["stQn"], scale=MLA_SCALE)
                for h in range(4):
                    pi = psnext()
                    for c in range(3):
                        mm(ps[pi][0:64, :n], wuq[:, c, 512 + h * 64:512 + (h + 1) * 64], qn[:, c, :n], c == 0, c == 2, ["wuq", "qlatn"], [("ps", pi)])
                    rope(ps[pi][0:64, :n], 64, n, MLA_SCALE, cosM, sinM, RM, stQr[0:64, h, :n], [("ps", pi)], ["stQr"], ri[0])
                    ri[0] += 1
                for h in range(4):
                    pi = psnext()
                    for c in range(2):
                        mm(ps[pi][:, :n], wukv[:, c, h * 128:(h + 1) * 128], kvn[:, c, :n], c == 0, c == 1, ["wukv", "kvlatn"], [("ps", pi)])
                    cp("dve", stKn[:, h, :n], ps[pi][:, :n], [("ps", pi)], ["stKn"])
                for t in range(ntile):
                    pi = psnext()
                    for c in range(2):
                        mm(ps[pi][:, :], kvn[:, c, t * 128:(t + 1) * 128], wukv[:, c, 512:1024], c == 0, c == 1, ["wukv", "kvlatn"], [("ps", pi)])
                    act(stVm[:, t, :], ps[pi][:, :], AF.Identity, [("ps", pi)], ["stVm"])
                for t in range(ntile):
                    for c in range(2):
                        pm = psnext()
                        for gg in range(2):
                            g = 2 * c + gg
                            mm(ps[pm][gg * 64:(gg + 1) * 64, 0:128], vbf4[:, t, g * 64:(g + 1) * 64], wsT[:, g, :], True, True,
                               [("vbf4", t), "wsT"], [("ps", pm)])
                        tt("dve", tmpc[:, c, :], ps[pm][:, 0:128], bsT[:, c, :], ALU.add, [("ps", pm), "bsT"], [("tmpc", c)])
                        stt(cm[:, c, t0 + t * 128:t0 + (t + 1) * 128], ug[:, c, t * 128:(t + 1) * 128], 0.5, tmpc[:, c, :], ALU.mult, ALU.mult,
                            [("ug", c), ("tmpc", c)], [("cm", b)])
                dma("sp", "stq1", qn_d[l].rearrange("(h p) t -> p h t", p=128)[:, :, t0:t0 + n], stQn[:, :, :n], ["stQn"], ["qn_d"])
                dma("sp", "stq2", qr_d[l].rearrange("(h p) t -> p h t", p=64)[:, :, t0:t0 + n], stQr[0:64, :, :n], ["stQr"], ["qr_d"])
                dma("sp", "stq3", dq_d[l].rearrange("(g p) t -> p g t", p=128)[:, :, t0:t0 + n], stM[:, :, :n], ["stM"], ["dq_d"])
                if b == 0:
                    kd, c0, ncol, nt_all = kvctx_d[l], 0, CTX, 2
                else:
                    kd, c0, ncol, nt_all = kvsrc_d[l], t0 - CTX, OWN, 16
                dma("sp", "stk1", kd[R_KN:R_KN + 512, :].rearrange("(h p) t -> p h t", p=128)[:, :, c0:c0 + n], stKn[:, :, :n], ["stKn"], ["kv_d"])
                dma("sp", "stk2", kd[R_KR:R_KR + 64, c0:c0 + n], stKr[0:64, :n], ["stKr"], ["kv_d"])
                for c in range(2):
                    dma("sp", "stk3", kd[R_DK + c * 128:R_DK + (c + 1) * 128, c0:c0 + n], stdK[:, c, :n], ["stdK"], ["kv_d"])
                tl0 = c0 // 128
                vmv = kd[R_VM:R_VM + 512, :].rearrange("(h p) (t d) -> p t h d", p=128, d=128)
                for h in range(4):
                    dma("sp", "stv1", vmv[:, tl0:tl0 + ntile, h, :], stVm[:, 0:ntile, h * 128:(h + 1) * 128], ["stVm"], ["kv_d"])
                vdv = kd[R_VD:R_VD + 256, :].rearrange("a (two rest) -> (a two) rest", two=2).rearrange("(h p) (t d) -> p t h d", p=128, d=64)
                for h in range(4):
                    dma("sp", "stv2", vdv[:, tl0:tl0 + ntile, h, :], stVd[:, 0:ntile, h * 64:(h + 1) * 64], ["stVd"], ["kv_d"])
            S.barrier()
            for c_ in range((KVROWS + CH - 1) // CH):
                sz = min(CH, KVROWS - c_ * CH)
                S.dma("pool", f"cc{c_ % 2}", lambda e, l=l, c_=c_, sz=sz: e.collective_compute(
                    "AllGather", ALU.bypass, replica_groups=[[0, 1, 2, 3], [4, 5, 6, 7]],
                    ins=[kvsrc_d[l][c_ * CH:c_ * CH + sz, :]], outs=[kvall_d[l][c_ * 4 * CH:c_ * 4 * CH + 4 * sz, :]]),
                    reads=["kv_d"], writes=["kvall"], inc1=True)
            if debug:
                pass
            if debug and False:
                dma("sp", "dbgcm", dbg_cm[l], cm[:], [("cm", b) for b in range(5)], ["dbgcm"])
            S.barrier()

            KA = carve(0, [128, NKEY], BF16)
            KR = carve(16896, [128, NKEY], BF16)
            VA = carve(33792, [128, NKT, 128], BF16)
            QA = carve(50688, [128, T], BF16)
            QB = carve(55296, [128, T], BF16)
            PT = carve(59904, [128, 4, 512], BF16)
            RT = carve(64000, [128, 2, 512], F32)
            DD = carve(68096, [128, 6, 512], F32)
            ACC = carve(80384, [128, 2, 2, 512], F32)
            ACC = carve(80384, [128, 2, 2, 512], F32)
            def kv_view(a, n_):
                c_ = a // CH
                i0 = a % CH
                sz = min(CH, KVROWS - c_ * CH)
                assert i0 + n_ <= sz
                return kvall_d[l][c_ * 4 * CH:c_ * 4 * CH + 4 * sz, :].rearrange("(r i) col -> i r col", r=4)[i0:i0 + n_]
            kctx = kvctx_d[l]
            SEGS = [(0, 2)] + [(2 + r * 16, 16) for r in range(4)]

            def seg_of(kt):
                return 0 if kt < 2 else 1 + (kt - 2) // 16

            def load_k(dst, rows0, nrows, name, sem):
                dma("sp", sem, dst[0:nrows, 0:CTX], kctx[rows0:rows0 + nrows, :], ["kv_d"], [(name, 0)])
                for r in range(4):
                    dma("sp", sem, dst[0:nrows, CTX + r * OWN:CTX + (r + 1) * OWN], kv_view(rows0, nrows)[:, r, :], ["kvall"], [(name, 1 + r)])

            qblocks = ([(0, 256, 2, 0)] if not last else []) + [(256 + i * 512, 512, NKT, i + 1) for i in range(4)]

            S.op("pool", lambda e, o=KR[64:128, :]: e.memset(o, 0.0), writes=[("KR", s_) for s_ in range(5)])
            S.op("pool", lambda e, o=QB[64:128, :]: e.memset(o, 0.0), writes=["QB"])
            load_k(KR, R_KR, 64, "KR", "ldkr")
            steps = []
            for h in range(4):
                for (q0, nq, nk, qb) in qblocks:
                    for kt in range(nk):
                        steps.append((h, q0, nq, nk, qb, kt))
            qbi_of = {}
            for st_ in steps:
                key = (st_[0], st_[4])
                if key not in qbi_of:
                    qbi_of[key] = len(qbi_of)

            def mla_S(i):
                h, q0, nq, nk, qb, kt = steps[i]
                if kt == 0 and (qb == qblocks[0][3]):
                    load_k(KA, R_KN + h * 128, 128, "KA", "ldka")
                    dma("sp", "ldqa", QA, qn_d[l][h * 128:(h + 1) * 128, :], ["qn_d"], ["QA"])
                    dma("sp", "ldqb", QB[0:64, :], qr_d[l][h * 64:(h + 1) * 64, :], ["qr_d"], ["QB"])
                sg = seg_of(kt)
                sb_ = i % 3
                mm(ps[sb_][:, :nq], KA[:, kt * 128:(kt + 1) * 128], QA[:, q0:q0 + nq], True, False, [("KA", sg), "QA"], [("ps", sb_)])
                mm(ps[sb_][:, :nq], KR[:, kt * 128:(kt + 1) * 128], QB[:, q0:q0 + nq], False, True, [("KR", sg), "QB"], [("ps", sb_)])

            def mla_PV(i):
                h, q0, nq, nk, qb, kt = steps[i]
                sg = seg_of(kt)
                sb_ = i % 3
                qbi = qbi_of[(h, qb)]
                ob, sm = 3 + qbi % 2, 5 + qbi % 2
                if kt == 0 and (qb == qblocks[0][3]):
                    vm_c = kctx[R_VM + h * 128:R_VM + (h + 1) * 128, :].rearrange("p (t d) -> p t d", d=128)
                    dma("sp", "ldva", VA[:, 0:2, :], vm_c, ["kv_d"], [("VA", 0)])
                    for r in range(4):
                        dma("sp", "ldva", VA[:, 2 + r * 16:2 + (r + 1) * 16, :],
                            kv_view(R_VM + h * 128, 128)[:, r, :].rearrange("p (t d) -> p t d", d=128), ["kvall"], [("VA", 1 + r)])
                p_ = PT[:, i % 4, :nq]
                act(p_, ps[sb_][:, :nq], AF.Exp, [("ps", sb_)], [("PT", i % 4)])
                mm(ps[ob][:, :nq], VA[:, kt, :], p_, kt == 0, kt == nk - 1, [("VA", sg), ("PT", i % 4)], [("ps", ob)])
                mm(ps[sm][:, :nq], onesb[:], p_, kt == 0, kt == nk - 1, ["onesb", ("PT", i % 4)], [("ps", sm)])
                if kt == nk - 1:
                    r_ = RT[:, qbi % 2, :nq]
                    recip(r_, ps[sm][:, :nq], [("ps", sm)], [("RT", qbi % 2)])
                    tt("dve", hc[:, h, q0:q0 + nq], ps[ob][:, :nq], r_, ALU.mult, [("ps", ob), ("RT", qbi % 2)], [("hc", qb)])

            if steps:
                mla_S(0)
                if len(steps) > 1:
                    mla_S(1)
                for i in range(len(steps)):
                    if i + 2 < len(steps):
                        mla_S(i + 2)
                    mla_PV(i)

            dsteps = []
            for h in range(4):
                for (q0, nq, nk, qb) in qblocks:
                    for m in range(2):
                        for kt in range(nk):
                            dsteps.append((h, q0, nq, nk, qb, m, kt))
            dqbi = {}
            for st_ in dsteps:
                key = (st_[0], st_[4])
                if key not in dqbi:
                    dqbi[key] = len(dqbi)
            KM = [KA, KR]
            QM = [QA, QB]
            KMN = ["KA", "KR"]
            QMN = ["QA", "QB"]

            def d_S(i):
                h, q0, nq, nk, qb, m, kt = dsteps[i]
                if kt == 0 and m == 0 and qb == qblocks[0][3]:
                    if h % 2 == 0:
                        load_k(KA, R_DK + (h // 2) * 128, 128, "KA", "ldka")
                    for mm_ in range(2):
                        g = 2 * h + mm_
                        dma("sp", "ldqa" if mm_ == 0 else "ldqb", QM[mm_][:, :], dq_d[l][g * 128:(g + 1) * 128, :], ["dq_d"], [QMN[mm_]])
                sg = seg_of(kt)
                sb_ = i % 3
                mm(ps[sb_][:, :nq], KA[:, kt * 128:(kt + 1) * 128], QM[m][:, q0:q0 + nq], True, True, [("KA", sg), QMN[m]], [("ps", sb_)])

            def d_PV(i):
                h, q0, nq, nk, qb, m, kt = dsteps[i]
                sg = seg_of(kt)
                sb_ = i % 3
                qbi = dqbi[(h, qb)]
                ob = 3 + (qbi % 2) * 2 + m
                if kt == 0 and m == 0 and qb == qblocks[0][3]:
                    if h == 0:
                        S.op("pool", lambda e: e.memset(VA[:, :, 64:128], 1.0), reads=[], writes=[("VA", s_) for s_ in range(5)])
                    vdc = kctx[R_VD:R_VD + 256, :].rearrange("a (two rest) -> (a two) rest", two=2)[h * 128:(h + 1) * 128, :].rearrange("p (t d) -> p t d", d=64)
                    dma("sp", "ldva", VA[:, 0:2, 0:64], vdc, ["kv_d"], [("VA", 0)])
                    for r in range(4):
                        vdr = kv_view(R_VD + h * 64, 64)[:, r, :].rearrange("a (two rest) -> (a two) rest", two=2).rearrange("p (t d) -> p t d", d=64)
                        dma("sp", "ldva", VA[:, 2 + r * 16:2 + (r + 1) * 16, 0:64], vdr, ["kvall"], [("VA", 1 + r)])
                    if l == 0 and n_layers > 1:
                        awb2, modrow2 = ada_bufs(81920)
                        if h >= 1:
                            for g_ in (2 * (h - 1), 2 * (h - 1) + 1):
                                ada_compute(1, g_, awb2[g_ % 2], ("awb2", g_ % 2), modrow2, "modrow2", bank=7)
                        if h <= 2:
                            for g_ in (2 * h, 2 * h + 1):
                                ada_load(1, g_, awb2[g_ % 2], ("awb2", g_ % 2), g_ % 2)
                        if h == 3:
                            ada_finish(1, 81920 + 2 * 32768 + 4096)
                p_ = PT[:, i % 4, :nq]
                act(p_, ps[sb_][:, :nq], AF.Exp, [("ps", sb_)], [("PT", i % 4)])
                mm(ps[ob][:, :nq], VA[:, kt, :], p_, kt == 0, kt == nk - 1, [("VA", sg), ("PT", i % 4)], [("ps", ob)])
                if kt == nk - 1 and m == 1:
                    o0, o1 = 3 + (qbi % 2) * 2, 4 + (qbi % 2) * 2
                    sq_b = 7
                    r0, r1 = RT[0:64, 0, :nq], RT[0:64, 1, :nq]
                    d1, d2, dd, sq_, rs_ = DD[0:64, 0, :nq], DD[0:64, 1, :nq], DD[0:64, 2, :nq], DD[0:64, 3, :nq], DD[0:64, 4, :nq]
                    recip(r0, ps[o0][64:128, :nq], [("ps", o0)], [("RT", 0)])
                    recip(r1, ps[o1][64:128, :nq], [("ps", o1)], [("RT", 1)])
                    tt("dve", d1, ps[o0][0:64, :nq], r0, ALU.mult, [("ps", o0), ("RT", 0)], ["dd1"])
                    tt("dve", d2, ps[o1][0:64, :nq], r1, ALU.mult, [("ps", o1), ("RT", 1)], ["dd2"])
                    stt(dd, d2, C("neglam", l)[0:64, :], d1, ALU.mult, ALU.add, ["dd1", "dd2", "cols"], ["ddd"])
                    tt("pool", sq_, dd, dd, ALU.mult, ["ddd"], ["ddsq"])
                    mm(ps[sq_b][0:64, :nq], onesf[0:64, 0:64], sq_, True, True, ["ddsq", "onesf"], [("ps", sq_b)])
                    act(rs_, ps[sq_b][0:64, :nq], AF.Ln, [("ps", sq_b)], ["ddrs"], scale=1.0 / 64, bias=EPS_RMS[0:64, :])
                    act(rs_, rs_, AF.Exp, ["ddrs"], ["ddrs"], scale=-0.5)
                    hb = (h % 2) * 64
                    stt(hc[hb:hb + 64, 4 + h // 2, q0:q0 + nq], dd, C("gsub", l)[0:64, :], rs_, ALU.mult, ALU.mult, ["ddd", "ddrs", "cols"], [("hc", qb)])

            if dsteps:
                d_S(0)
                if len(dsteps) > 1:
                    d_S(1)
                for i in range(len(dsteps)):
                    if i + 2 < len(dsteps):
                        d_S(i + 2)
                    d_PV(i)
            if debug and l == 0:
                dma("sp", "dbgcc", dbg_cc[l], hc[:], [("hc", b) for b in range(5)], ["dbgcc"])
            S.barrier()

            wo = carve(73728, [128, 8, 1024], BF16)
            for kk in range(8):
                dma("pool", f"wl{kk}", wo[:, kk, :], w_o_d[l, kk * 128:(kk + 1) * 128, :], [], [("wo", kk)])
            for b in blks_res:
                t0, n = BLKS[b]
                dma("sp", "xin", xT[:, :, t0:t0 + n], xsp_d[l][:, :, t0:t0 + n], [("xsp", b)], [("x", b)])
            prev_b = None
            for b in blks_res:
                t0, n = BLKS[b]
                s = 1 if b == 0 else 0
                for oc in range(8):
                    pi = psnext()
                    for k in range(8):
                        rhs = hc[:, k, t0:t0 + n] if k < 6 else cm[:, k - 6, t0:t0 + n]
                        mm(ps[pi][:, :n], wo[:, k, oc * 128:(oc + 1) * 128], rhs, k == 0, k == 7, [("wo", k), ("hc", b), ("cm", b)], [("ps", pi)])
                    stt(xT[:, oc, t0:t0 + n], ps[pi][:, :n], C("g1s", l)[:, s * 8 + oc:s * 8 + oc + 1], xT[:, oc, t0:t0 + n], ALU.mult, ALU.add,
                        [("ps", pi), ("x", b), "cols"], [("x", b)])
                if prev_b is not None:
                    emit_ln(prev_b, l, "ln1_g", "ln1_b", ("A2", l), ("B2", l), 1 if prev_b == 0 else 0, True, hc, LNT)
                prev_b = b
            emit_ln(prev_b, l, "ln1_g", "ln1_b", ("A2", l), ("B2", l), 1 if prev_b == 0 else 0, True, hc, LNT)
            if debug and l == 0:
                dma("sp", "dbgx1", dbg_x1[l], xT, [("x", b) for b in range(5)], ["dbgx1"])

            hl = carve(LNT + 16384, [128, 2, 8], BF16)
            hg = carve(LNT + 16384 + 64, [128, 4, 16], BF16)
            hacc = carve(LNT + 16384 + 256, [128, 2, 8], F32)
            cp("dve", hl[:, 0, :], hc[:, :, CTX], [("hc", 1)], ["hl"])
            cp("dve", hl[:, 1, :], hc[:, :, T - 1], [("hc", 4)], ["hl"])
            dma("sp", "hs", hsrc_d[l], hl.rearrange("p a b -> p (a b)"), ["hl"], ["hsrc"])
            if True:
                S.dma("pool", "cc", lambda e, l=l: e.collective_compute("AllGather", ALU.bypass, replica_groups=[[0, 1, 2, 3], [4, 5, 6, 7]],
                                                                      ins=[hsrc_d[l]], outs=[hall_d[l]]), reads=["hsrc"], writes=["hall"], inc1=True)
            dma("sp", "hg", hg, hall_d[l].rearrange("(r p) c -> p r c", p=128), ["hall"], ["hg"])
            for side in range(2):
                w_ = 1 - side
                for r in range(4):
                    src = hg[:, r, w_ * 8:(w_ + 1) * 8]
                    mcol = hmask[:, side, r:r + 1]
                    if r == 0:
                        ts("dve", hacc[:, side, :], src, mcol, None, ALU.mult, None, ["hg", "hmask"], ["hacc"])
                    else:
                        stt(hacc[:, side, :], src, mcol, hacc[:, side, :], ALU.mult, ALU.add, ["hg", "hmask", "hacc"], ["hacc"])
                cp("dve", hc[:, :, T + side], hacc[:, side, :], ["hacc"], ["hchalo"])
            S.barrier()

            FB = 73728
            wgv = [carve(FB + i * 4096, [128, 8, 2, 128], BF16) for i in range(3)]
            wd = [carve(FB + 12288 + i * 2048, [128, 1024], BF16) for i in range(3)]
            ugb = [carve(FB + 18432 + i * 16400, [128, 2, OWN + 2], F32) for i in range(2)]
            cacc = carve(FB + 18432 + 32800, [128, 2, OWN], F32)
            aT = [carve(FB + 18432 + 32800 + 16384 + i * 4096, [128, OWN], BF16) for i in range(2)]
            CB = FB + 18432 + 32800 + 16384 + 8192
            ugc = carve(CB, [128, 2, CTX + 2], F32)
            caccc = carve(CB + 2064, [128, 2, CTX], F32)
            aTc = carve(CB + 2064 + 2048, [128, CTX], BF16)
            assert CB + 2064 + 2048 + 512 <= ARW * 4
            if not last:
                S.op("pool", lambda e, o=ugc: e.memset(o, 0.0), writes=["ugc"])
            wupv = wup_d[l].rearrange("(k p) (gv f c) -> p k gv f c", p=128, gv=2, c=128)

            facc = cm[:].rearrange("p a b -> p (a b)").bitcast(F32)[:, 0:2048].rearrange("p (a b) -> p a b", a=4)
            facc_i = [0]

            def cw(f, gv):
                fc = gv * NFC + f
                return (CI("cw0", l)[:, fc:fc + 1], CI("cw1", l)[:, fc:fc + 1], CI("cw2", l)[:, fc:fc + 1], CI("cb", l)[:, fc:fc + 1])

            def ffn_load(f):
                wb_ = f % 3
                for gv in range(2):
                    dma("pool", f"wgv{wb_}", wgv[wb_][:, :, gv, :], wupv[:, :, gv, f, :], [], [("wgv", wb_)])
                dma("pool", f"wd{wb_}", wd[wb_], wdown_d[l, f * 128:(f + 1) * 128, :], [], [("wd", wb_)])

            def ffn_up(f):
                wb_, ub = f % 3, f % 2
                for gv in range(2):
                    for tb in range(4):
                        t0 = CTX + tb * 512
                        pp = psnext()
                        for k in range(8):
                            mm(ps[pp][:, :], wgv[wb_][:, k, gv, :], hc[:, k, t0:t0 + 512], k == 0, k == 7, [("wgv", wb_), ("hc", tb + 1)], [("ps", pp)])
                        act(ugb[ub][:, gv, 1 + tb * 512:1 + (tb + 1) * 512], ps[pp][:, :], AF.Identity, [("ps", pp)], [("ugb", ub, gv)])
                    pp = psnext()
                    for k in range(8):
                        mm(ps[pp][:, 0:2], wgv[wb_][:, k, gv, :], hc[:, k, T:T + 2], k == 0, k == 7, [("wgv", wb_), "hchalo"], [("ps", pp)])
                    cp("dve", ugb[ub][:, gv, 0:1], ps[pp][:, 0:1], [("ps", pp)], [("ugb", ub, gv)])
                    cp("dve", ugb[ub][:, gv, OWN + 1:OWN + 2], ps[pp][:, 1:2], [("ps", pp)], [("ugb", ub, gv)])

            def ffn_conv(f):
                ub = f % 2
                for gv in range(2):
                    w0, w1, w2, bb = cw(f, gv)
                    act(cacc[:, gv, :], ugb[ub][:, gv, 1:OWN + 1], AF.Identity, [("ugb", ub, gv), "colsin"], [("cacc", gv)], scale=w1, bias=bb)
                    stt(cacc[:, gv, :], ugb[ub][:, gv, 0:OWN], w0, cacc[:, gv, :], ALU.mult, ALU.add, [("ugb", ub, gv), ("cacc", gv), "colsin"], [("cacc", gv)])
                    stt(cacc[:, gv, :], ugb[ub][:, gv, 2:OWN + 2], w2, cacc[:, gv, :], ALU.mult, ALU.add, [("ugb", ub, gv), ("cacc", gv), "colsin"], [("cacc", gv)])
                act(cacc[:, 0, :], cacc[:, 0, :], AF.Silu, [("cacc", 0)], [("cacc", 0)])
                tt("pool", aT[ub], cacc[:, 0, :], cacc[:, 1, :], ALU.mult, [("cacc", 0), ("cacc", 1)], [("aT", ub)])

            def ffn_down(f):
                wb_, ub = f % 3, f % 2
                for oc in range(8):
                    for tb in range(4):
                        t0 = CTX + tb * 512
                        pp = psnext()
                        mm(ps[pp][:, :], wd[wb_][:, oc * 128:(oc + 1) * 128], aT[ub][:, tb * 512:(tb + 1) * 512], True, True, [("wd", wb_), ("aT", ub)], [("ps", pp)])
                        xk = ("xo", tb, oc)
                        if oc % 2 == 0:
                            stt(xT[:, oc, t0:t0 + 512], ps[pp][:, :], C("g2s", l)[:, oc:oc + 1], xT[:, oc, t0:t0 + 512], ALU.mult, ALU.add,
                                [("ps", pp), xk, "cols"], [xk])
                        else:
                            fi = facc_i[0] % 4
                            facc_i[0] += 1
                            act(facc[:, fi, :], ps[pp][:, :], AF.Identity, [("ps", pp), "cols"], [("facc", fi)], scale=C("g2s", l)[:, oc:oc + 1])
                            tt("pool", xT[:, oc, t0:t0 + 512], xT[:, oc, t0:t0 + 512], facc[:, fi, :], ALU.add, [("facc", fi), xk], [xk])

            def ffn_ctx_up(f):
                wb_ = f % 3
                for gv in range(2):
                    pp = psnext()
                    for k in range(8):
                        mm(ps[pp][:, 0:CTX], wgv[wb_][:, k, gv, :], hc[:, k, 0:CTX], k == 0, k == 7, [("wgv", wb_), ("hc", 0)], [("ps", pp)])
                    act(ugc[:, gv, 1:CTX + 1], ps[pp][:, 0:CTX], AF.Identity, [("ps", pp)], ["ugc"])
                for gv in range(2):
                    w0, w1, w2, bb = cw(f, gv)
                    act(caccc[:, gv, :], ugc[:, gv, 1:CTX + 1], AF.Identity, ["ugc", "colsin"], [("caccc", gv)], scale=w1, bias=bb)
                    stt(caccc[:, gv, :], ugc[:, gv, 0:CTX], w0, caccc[:, gv, :], ALU.mult, ALU.add, ["ugc", ("caccc", gv), "colsin"], [("caccc", gv)])
                    stt(caccc[:, gv, :], ugc[:, gv, 2:CTX + 2], w2, caccc[:, gv, :], ALU.mult, ALU.add, ["ugc", ("caccc", gv), "colsin"], [("caccc", gv)])
                act(caccc[:, 0, :], caccc[:, 0, :], AF.Silu, [("caccc", 0)], [("caccc", 0)])
                tt("pool", aTc, caccc[:, 0, :], caccc[:, 1, :], ALU.mult, [("caccc", 0), ("caccc", 1)], ["aTc"])

            def ffn_ctx_down(f):
                wb_ = f % 3
                for oc in range(8):
                    pp = psnext()
                    mm(ps[pp][:, 0:CTX], wd[wb_][:, oc * 128:(oc + 1) * 128], aTc, True, True, [("wd", wb_), "aTc"], [("ps", pp)])
                    stt(xT[:, oc, 0:CTX], ps[pp][:, 0:CTX], C("g2s", l)[:, 8 + oc:8 + oc + 1], xT[:, oc, 0:CTX], ALU.mult, ALU.add,
                        [("ps", pp), ("xo", "c", oc), "cols"], [("xo", "c", oc)])

            ffn_load(0)
            ffn_load(1)
            ffn_up(0)
            ffn_conv(0)
            for f in range(NFC):
                if f + 2 < NFC:
                    ffn_load(f + 2)
                if f + 1 < NFC:
                    ffn_up(f + 1)
                if not last:
                    ffn_ctx_up(f)
                ffn_down(f)
                if not last:
                    ffn_ctx_down(f)
                if f + 1 < NFC:
                    ffn_conv(f + 1)
            S.barrier()
            for b in blks_res:
                s = 1 if b == 0 else 0
                if last:
                    emit_ln(b, l, "ln2_g", "ln2_b", None, None, s, True, None, LNT)
                else:
                    emit_ln(b, l, "ln2_g", "ln2_b", ("A1", l + 1), ("B1", l + 1), s, True, hc, LNT)
            if debug and l == 0:
                dma("sp", "dbgx2", dbg_x2[l], xT, [("x", b) for b in range(5)], ["dbgx2"])
            if l == n_layers - 1:
                for b in [1, 2, 3, 4]:
                    t0, n = BLKS[b]
                    dma("sp", "outs", out_d.rearrange("(k p) t -> p k t", p=128)[:, :, t0 - CTX:t0 - CTX + n], xT[:, :, t0:t0 + n], [("x", b)], ["out"])
            S.barrier()

        S.final_wait("sp")
        S.emit()
    return nc


def _mk_cols():
    cols = {}
    o = 0

    def add(name, w):
        nonlocal o
        cols[name] = (o, w)
        o += w

    add("eps", 4)
    add("mod", DEPTH * 6 * 16)
    for l in range(DEPTH):
        for nm in ("sc1p", "sc2p", "g1s", "g2s", "A1", "B1", "A2", "B2"):
            add((nm, l), 16)
        add(("lsum", l), 2)
        add(("neglam", l), 1)
        add(("gsub", l), 1)
    return cols, o


COLS, NCOLS = _mk_cols()


def _mk_colin():
    c = {}
    o = 0
    for nm, w in (("adab", 48), ("gq", 3), ("gkv", 2), ("subg", 1), ("ln1_g", 8), ("ln1_b", 8), ("ln2_g", 8), ("ln2_b", 8),
                  ("cw0", 44), ("cw1", 44), ("cw2", 44), ("cb", 44)):
        c[nm] = (o, w)
        o += w
    return c, o


COLIN, NCOLIN = _mk_colin()


def _rope_tables(core):
    r = core % 4
    tpos = np.arange(r * OWN, (r + 1) * OWN)
    rows = (tpos // 64).astype(np.float32)
    colsp = (tpos % 64).astype(np.float32)
    tabs = np.zeros((128, 4, T), np.float32)
    tabs[:, 0, :CTX] = 1.0
    tabs[:, 2, :CTX] = 1.0

    def fill(ci, si, width, nparts):
        half = width // 2
        n = half // 2
        inv = (10000.0 ** (-np.arange(n, dtype=np.float32) / n)).astype(np.float32)
        for p in range(nparts):
            j = p % width
            hf = j // half
            i = (j % half) % n
            pos = rows if hf == 0 else colsp
            ang = (pos * inv[i]).astype(np.float32)
            tabs[p, ci, CTX:] = np.cos(ang)
            tabs[p, si, CTX:] = np.sin(ang)

    fill(0, 1, 64, 64)
    fill(2, 3, 32, 128)
    return tabs


def _rot_mats():
    rm = np.zeros((128, 3, 128), np.float32)
    rm[:, 2, :] = np.eye(128, dtype=np.float32)
    for base in (0, 32):
        for i in range(16):
            rm[base + i + 16, 0, base + i] = -1.0
            rm[base + i, 0, base + i + 16] = 1.0
    for base in range(0, 128, 16):
        for i in range(8):
            rm[base + i + 8, 1, base + i] = -1.0
            rm[base + i, 1, base + i + 8] = 1.0
    return rm


def _colmajor(v, nchunk):
    return np.ascontiguousarray(np.asarray(v, np.float32).reshape(nchunk, 128).T)


def prep_inputs(inp):
    f = lambda a: np.ascontiguousarray(np.asarray(a, np.float32))
    L = DEPTH
    colsin = np.zeros((128, L, NCOLIN), np.float32)

    def put(name, l, arr):
        o, w = COLIN[name]
        colsin[:, l, o:o + w] = arr

    for l in range(L):
        put("adab", l, _colmajor(inp["ada_b"][l], 48))
        put("gq", l, _colmajor(inp["mla_gq"][l], 3))
        put("gkv", l, _colmajor(inp["mla_gkv"][l], 2))
        put("subg", l, np.tile(np.asarray(inp["diff_subln_g"][l], np.float32), 2)[:, None])
        put("ln1_g", l, _colmajor(inp["ln1_g"][l], 8))
        put("ln1_b", l, _colmajor(inp["ln1_b"][l], 8))
        put("ln2_g", l, _colmajor(inp["ln2_g"][l], 8))
        put("ln2_b", l, _colmajor(inp["ln2_b"][l], 8))
        for j in range(3):
            put(f"cw{j}", l, _colmajor(inp["ffn_convw"][l][j], 44))
        put("cb", l, _colmajor(inp["ffn_convb"][l], 44))
    wuq = f(inp["mla_wuq"]).reshape(L, 384, 4, 192)
    wuq_p = np.ascontiguousarray(np.concatenate([wuq[..., :128].reshape(L, 384, 512), wuq[..., 128:].reshape(L, 384, 256)], -1))
    wukv = f(inp["mla_wukv"]).reshape(L, 256, 4, 256)
    wukv_p = np.ascontiguousarray(np.concatenate([wukv[..., :128].reshape(L, 256, 512), wukv[..., 128:].reshape(L, 256, 512)], -1))
    ws = f(inp["sgu_ws"])
    wsT = np.ascontiguousarray(ws.transpose(0, 3, 1, 2))
    bs = f(inp["sgu_bs"])
    bsT = np.zeros((L, 128, 2, 128), np.float32)
    for g in range(4):
        bsT[:, (g % 2) * 64:(g % 2) * 64 + 64, g // 2, :] = bs[:, g, None, :]
    sgb = np.zeros((L, 128, 2, 256), np.float32)
    sgb[:, :, 0, :] = f(inp["sgu_ln_g"])[:, None, :]
    sgb[:, :, 1, :] = f(inp["sgu_ln_b"])[:, None, :]
    lamv = np.zeros((L, 128, 4, 32), np.float32)
    for i, nm in enumerate(("diff_lq1", "diff_lk1", "diff_lq2", "diff_lk2")):
        lamv[:, :, i, :] = f(inp[nm])[:, None, :]
    shared = {
        "rmat": _rot_mats(), "ada_w": f(inp["ada_w"]), "colsin": colsin, "w_in": f(inp["w_in"]), "wuq": wuq_p, "wukv": wukv_p,
        "w_o": f(inp["w_o"]), "wup": f(inp["ffn_wup"]), "wdown": f(inp["ffn_wdown"]), "sgu_wsT": wsT, "sgu_bsT": bsT,
        "sgu_gb": sgb, "lamv": lamv,
    }
    x = f(inp["x"])
    ctx = f(inp["ctx"])
    c = f(inp["c"])
    c_ctx = f(inp["c_ctx"])
    in_maps = []
    for core in range(8):
        b, r = core // 4, core % 4
        xloc = np.concatenate([ctx[b], x[b, r * OWN:(r + 1) * OWN]], 0)
        m = dict(shared)
        m["xT"] = np.ascontiguousarray(xloc.T)
        cc = np.zeros((128, 8, 2), np.float32)
        cc[:, :, 0] = _colmajor(c[b], 8)
        cc[:, :, 1] = _colmajor(c_ctx, 8)
        m["cc"] = cc
        m["tabs"] = _rope_tables(core)
        hm = np.zeros((128, 2, 4), np.float32)
        if r > 0:
            hm[:, 0, r - 1] = 1.0
        if r < 3:
            hm[:, 1, r + 1] = 1.0
        m["hmask"] = hm
        gm = np.zeros((128, 4), np.float32)
        for j in range(4):
            gm[j * 32:(j + 1) * 32, j] = 1.0
        m["gmask"] = gm
        in_maps.append(m)
    return in_maps


_NC_CACHE = {}


def kernel(**inputs):
    if "nc" not in _NC_CACHE:
        _NC_CACHE["nc"] = build_program()
    nc = _NC_CACHE["nc"]
    in_maps = prep_inputs(inputs)
    res = run_bass_kernel_spmd(nc, in_maps, core_ids=list(range(8)))
    out = np.zeros((2, SEQ, D), np.float32)
    for core in range(8):
        b, r = core // 4, core % 4
        out[b, r * OWN:(r + 1) * OWN, :] = res.results[core]["outT"].T
    return out
```
